# Optimizing a Trainium2 kernel written in Bass

```python
import math
import jax
import jax.numpy as jnp
from jax import lax
import numpy as np

D_MODEL = 1024
BATCH = 4
SEQ = 4096
DEPTH = 4

GRID_W = 64
CTX_LEN = 256
N_MIXERS = 4
N_MOD = 9
D_FF = 2816
EPS = 1e-6
ROPE_BASE = 10000.0
ROPE_DIM = 64
Q_BLOCK = 128
NEG_INF = -1e30

MLA_HEADS = 8
MLA_Q_RANK = 512
MLA_KV_RANK = 256
MLA_NOPE = 128
MLA_ROPE = ROPE_DIM
MLA_V = 128

SWA_Q_HEADS = 16
SWA_KV_HEADS = 4
SWA_GROUP = SWA_Q_HEADS // SWA_KV_HEADS
SWA_HEAD_DIM = ROPE_DIM
SWA_WINDOW = 128

NA_HEADS = 16
NA_HEAD_DIM = D_MODEL // NA_HEADS
NA_KH = 8
NA_KW = 16

DIFF_HEADS = 8
DIFF_HEAD_DIM = ROPE_DIM

kernel_name = 'hybrid_mla_swa_na_diff_macaron_dit'


def rms_norm(x, g):
    xf = x.astype(jnp.float32)
    y = xf * lax.rsqrt(jnp.mean(xf * xf, axis=-1, keepdims=True) + EPS)
    return (y * g.astype(jnp.float32)).astype(x.dtype)


def modulate(h, shift, scale):
    return h * (1.0 + scale) + shift


def swiglu(h, w_in, w_out):
    g, u = jnp.split(h @ w_in, 2, axis=-1)
    return (jax.nn.silu(g) * u) @ w_out


def axial_rope_tables(L, dim):
    t = jnp.arange(L, dtype=jnp.int32)
    row = (t // GRID_W).astype(jnp.float32)
    col = (t % GRID_W).astype(jnp.float32)
    n_freq = dim // 4
    inv = jnp.exp(-math.log(ROPE_BASE) * jnp.arange(n_freq, dtype=jnp.float32) / n_freq)
    ang = jnp.concatenate([row[:, None] * inv, col[:, None] * inv], axis=-1)
    return jnp.cos(ang), jnp.sin(ang)


def apply_rope(x, cos, sin):
    half = x.shape[-1] // 2
    x1 = x[..., :half].astype(jnp.float32)
    x2 = x[..., half:].astype(jnp.float32)
    return jnp.concatenate([x1 * cos - x2 * sin, x1 * sin + x2 * cos], axis=-1).astype(x.dtype)


def softmax32(s):
    return jax.nn.softmax(s.astype(jnp.float32), axis=-1)


def joint_softmax(*scores):
    sizes = [s.shape[-1] for s in scores]
    p = softmax32(jnp.concatenate([s.astype(jnp.float32) for s in scores], axis=-1))
    cuts = [int(v) for v in np.cumsum(sizes)[:-1]]
    return jnp.split(p, cuts, axis=-1)


def to_blocks(t):
    lead = t.shape[:-2]
    L, d = t.shape[-2], t.shape[-1]
    t = t.reshape(lead + (L // Q_BLOCK, Q_BLOCK, d))
    return jnp.moveaxis(t, -3, 0)


def from_blocks(t):
    t = jnp.moveaxis(t, 0, -3)
    lead = t.shape[:-3]
    nb, qb, d = t.shape[-3], t.shape[-2], t.shape[-1]
    return t.reshape(lead + (nb * qb, d))


def merge_heads(o):
    B, H, N, d = o.shape
    return o.transpose(0, 2, 1, 3).reshape(B, N, H * d)


def pv(eq, p, v):
    return jnp.einsum(eq, p.astype(v.dtype), v)


def mla_mixer(hx, hc, w_down, q_norm, kv_norm, w_uq, w_ukv, w_o, cos, sin, ctx_out):
    H = MLA_HEADS
    scale = (MLA_NOPE + MLA_ROPE) ** -0.5

    def project(h):
        B, N, _ = h.shape
        d = h @ w_down
        cq = rms_norm(d[..., :MLA_Q_RANK], q_norm)
        ckv = rms_norm(d[..., MLA_Q_RANK:MLA_Q_RANK + MLA_KV_RANK], kv_norm)
        k_rope = d[..., MLA_Q_RANK + MLA_KV_RANK:]
        q = (cq @ w_uq).reshape(B, N, H, MLA_NOPE + MLA_ROPE).transpose(0, 2, 1, 3)
        kv = (ckv @ w_ukv).reshape(B, N, H, MLA_NOPE + MLA_V).transpose(0, 2, 1, 3)
        return q[..., :MLA_NOPE], q[..., MLA_NOPE:], kv[..., :MLA_NOPE], k_rope, kv[..., MLA_NOPE:]

    qn_x, qr_x, kn_x, kr_x, v_x = project(hx)
    qn_c, qr_c, kn_c, kr_c, v_c = project(hc)
    qr_x_rot = apply_rope(qr_x, cos, sin)
    kr_x_rot = apply_rope(kr_x, cos, sin)

    def scores(qn, qr, kn, kr):
        s = jnp.einsum('bhqd,bhkd->bhqk', qn, kn) + jnp.einsum('bhqr,bkr->bhqk', qr, kr)
        return s.astype(jnp.float32) * scale

    def block(args):
        qn, qr_rot, qr_pl = args
        p_x, p_c = joint_softmax(scores(qn, qr_rot, kn_x, kr_x_rot), scores(qn, qr_pl, kn_c, kr_c))
        return pv('bhqk,bhkd->bhqd', p_x, v_x) + pv('bhqk,bhkd->bhqd', p_c, v_c)

    o_x = from_blocks(lax.map(block, (to_blocks(qn_x), to_blocks(qr_x_rot), to_blocks(qr_x))))
    out_x = merge_heads(o_x) @ w_o
    out_c = None
    if ctx_out:
        p = softmax32(scores(qn_c, qr_c, kn_c, kr_c))
        out_c = merge_heads(pv('bhqk,bhkd->bhqd', p, v_c)) @ w_o
    return out_x, out_c


def swa_mixer(hx, hc, w_qkv, sink, w_o, cos, sin, ctx_out):
    Hk, G, Dh = SWA_KV_HEADS, SWA_GROUP, SWA_HEAD_DIM
    scale = Dh ** -0.5

    def project(h):
        B, N, _ = h.shape
        p = h @ w_qkv
        q = p[..., :Hk * G * Dh].reshape(B, N, Hk, G, Dh).transpose(0, 2, 3, 1, 4)
        k = p[..., Hk * G * Dh:(Hk * G + Hk) * Dh].reshape(B, N, Hk, Dh).transpose(0, 2, 1, 3)
        v = p[..., (Hk * G + Hk) * Dh:].reshape(B, N, Hk, Dh).transpose(0, 2, 1, 3)
        return q, k, v

    B, L, _ = hx.shape
    C = hc.shape[1]
    q_x, k_x, v_x = project(hx)
    q_c, k_c, v_c = project(hc)
    q_x_rot = apply_rope(q_x, cos, sin)
    k_x_rot = apply_rope(k_x, cos, sin)
    pad = ((0, 0), (0, 0), (SWA_WINDOW, SWA_WINDOW), (0, 0))
    k_pad = jnp.pad(k_x_rot, pad)
    v_pad = jnp.pad(v_x, pad)
    span = Q_BLOCK + 2 * SWA_WINDOW
    qi = jnp.arange(Q_BLOCK)
    kj = jnp.arange(span)
    sink_b = sink.reshape(Hk, G)[None, :, :, None, None].astype(jnp.float32)

    def ctx_scores(q):
        return jnp.einsum('bkgqd,bkcd->bkgqc', q, k_c).astype(jnp.float32) * scale

    def block(args):
        n, q_rot, q_pl = args
        start = n * Q_BLOCK
        k_b = lax.dynamic_slice_in_dim(k_pad, start, span, axis=2)
        v_b = lax.dynamic_slice_in_dim(v_pad, start, span, axis=2)
        qpos = start + qi
        kpos = start - SWA_WINDOW + kj
        valid = ((jnp.abs(kpos[None, :] - qpos[:, None]) <= SWA_WINDOW)
                 & (kpos[None, :] >= 0) & (kpos[None, :] < L))
        s_w = jnp.einsum('bkgqd,bkjd->bkgqj', q_rot, k_b).astype(jnp.float32) * scale
        s_w = jnp.where(valid, s_w, NEG_INF)
        s_c = ctx_scores(q_pl)
        s_s = jnp.broadcast_to(sink_b, s_c.shape[:-1] + (1,))
        p_w, p_c, _ = joint_softmax(s_w, s_c, s_s)
        return pv('bkgqj,bkjd->bkgqd', p_w, v_b) + pv('bkgqc,bkcd->bkgqd', p_c, v_c)

    nb = L // Q_BLOCK
    o = from_blocks(lax.map(block, (jnp.arange(nb), to_blocks(q_x_rot), to_blocks(q_x))))
    out_x = o.transpose(0, 3, 1, 2, 4).reshape(B, L, Hk * G * Dh) @ w_o
    out_c = None
    if ctx_out:
        s_c = ctx_scores(q_c)
        s_s = jnp.broadcast_to(sink_b, s_c.shape[:-1] + (1,))
        p_c, _ = joint_softmax(s_c, s_s)
        o_c = pv('bkgqc,bkcd->bkgqd', p_c, v_c)
        out_c = o_c.transpose(0, 3, 1, 2, 4).reshape(B, C, Hk * G * Dh) @ w_o
    return out_x, out_c


def na_mixer(hx, hc, w_qkv, rpb, w_o, ctx_out):
    H, Dh = NA_HEADS, NA_HEAD_DIM
    scale = Dh ** -0.5

    def project(h):
        B, N, _ = h.shape
        p = (h @ w_qkv).reshape(B, N, 3, H, Dh).transpose(2, 0, 3, 1, 4)
        return p[0], p[1], p[2]

    B, L, _ = hx.shape
    rows = L // GRID_W
    kh = min(NA_KH, rows)
    kw = NA_KW
    q_x, k_x, v_x = project(hx)
    q_c, k_c, v_c = project(hc)
    col = jnp.arange(GRID_W)
    col_start = jnp.clip(col - kw // 2, 0, GRID_W - kw)
    key_cols = col_start[:, None] + jnp.arange(kw)[None, :]
    col_off = key_cols - col[:, None] + (NA_KW - 1)

    def block(args):
        r, q_row = args
        row_start = jnp.clip(r - kh // 2, 0, rows - kh)
        key_rows = row_start + jnp.arange(kh)
        idx = (key_rows[None, :, None] * GRID_W + key_cols[:, None, :]).reshape(-1)
        k_g = jnp.take(k_x, idx, axis=2).reshape(B, H, GRID_W, kh * kw, Dh)
        v_g = jnp.take(v_x, idx, axis=2).reshape(B, H, GRID_W, kh * kw, Dh)
        row_off = key_rows - r + (NA_KH - 1)
        bias = rpb[:, row_off[None, :, None], col_off[:, None, :]]
        bias = bias.reshape(H, GRID_W, kh * kw).astype(jnp.float32)
        s_n = jnp.einsum('bhqd,bhqkd->bhqk', q_row, k_g).astype(jnp.float32) * scale + bias[None]
        s_c = jnp.einsum('bhqd,bhcd->bhqc', q_row, k_c).astype(jnp.float32) * scale
        p_n, p_c = joint_softmax(s_n, s_c)
        return pv('bhqk,bhqkd->bhqd', p_n, v_g) + pv('bhqc,bhcd->bhqd', p_c, v_c)

    q_rows = q_x.reshape(B, H, rows, GRID_W, Dh).transpose(2, 0, 1, 3, 4)
    o = lax.map(block, (jnp.arange(rows), q_rows))
    out_x = o.transpose(1, 0, 3, 2, 4).reshape(B, L, H * Dh) @ w_o
    out_c = None
    if ctx_out:
        p = softmax32(jnp.einsum('bhqd,bhcd->bhqc', q_c, k_c).astype(jnp.float32) * scale)
        out_c = merge_heads(pv('bhqc,bhcd->bhqd', p, v_c)) @ w_o
    return out_x, out_c


def diff_mixer(hx, hc, w_qkv, lam_params, norm_g, w_o, cos, sin, lam_init, ctx_out):
    H, Dh = DIFF_HEADS, DIFF_HEAD_DIM
    scale = Dh ** -0.5

    def project(h):
        B, N, _ = h.shape
        p = h @ w_qkv
        q = p[..., :H * 2 * Dh].reshape(B, N, H, 2, Dh).transpose(0, 2, 3, 1, 4)
        k = p[..., H * 2 * Dh:H * 4 * Dh].reshape(B, N, H, 2, Dh).transpose(0, 2, 3, 1, 4)
        v = p[..., H * 4 * Dh:].reshape(B, N, H, 2 * Dh).transpose(0, 2, 1, 3)
        return q, k, v

    q_x, k_x, v_x = project(hx)
    q_c, k_c, v_c = project(hc)
    q_x_rot = apply_rope(q_x, cos, sin)
    k_x_rot = apply_rope(k_x, cos, sin)
    lp = lam_params.astype(jnp.float32)
    lam = jnp.exp(jnp.sum(lp[0] * lp[1])) - jnp.exp(jnp.sum(lp[2] * lp[3])) + lam_init

    def scores(q, k):
        return jnp.einsum('bhiqd,bhikd->bhiqk', q, k).astype(jnp.float32) * scale

    def block(args):
        q_rot, q_pl = args
        p_x, p_c = joint_softmax(scores(q_rot, k_x_rot), scores(q_pl, k_c))
        a_x = p_x[:, :, 0] - lam * p_x[:, :, 1]
        a_c = p_c[:, :, 0] - lam * p_c[:, :, 1]
        return pv('bhqk,bhkd->bhqd', a_x, v_x) + pv('bhqk,bhkd->bhqd', a_c, v_c)

    def finish(o):
        return merge_heads(rms_norm(o, norm_g) * (1.0 - lam_init)) @ w_o

    o_x = from_blocks(lax.map(block, (to_blocks(q_x_rot), to_blocks(q_x))))
    out_x = finish(o_x)
    out_c = None
    if ctx_out:
        p = softmax32(scores(q_c, k_c))
        out_c = finish(pv('bhqk,bhkd->bhqd', p[:, :, 0] - lam * p[:, :, 1], v_c))
    return out_x, out_c


def setup_inputs(seed: int = 0) -> dict:
    key = jax.random.key(seed)
    keys = jax.random.split(key, 32)
    counter = [0]

    def normal(shape, std):
        k = keys[counter[0]]
        counter[0] += 1
        return jax.random.normal(k, shape, jnp.float32) * std

    def gain(shape):
        return 1.0 + normal(shape, 0.02)

    nA, nB, nC, nD = [len(range(m, DEPTH, N_MIXERS)) for m in range(N_MIXERS)]
    D = D_MODEL
    return {
        'x': normal((BATCH, SEQ, D), 1.0),
        'c': normal((BATCH, D), 1.0),
        'ctx': normal((BATCH, CTX_LEN, D), 1.0),
        'c_ctx': normal((D,), 1.0),
        'mod_w': normal((DEPTH, D, N_MOD * D), 0.5 * D ** -0.5),
        'mod_b': normal((DEPTH, N_MOD * D), 0.02),
        'norm_g': gain((DEPTH, 3, D)),
        'final_norm_g': gain((D,)),
        'ffn_w_in': normal((DEPTH, 2, D, 2 * D_FF), D ** -0.5),
        'ffn_w_out': normal((DEPTH, 2, D_FF, D), D_FF ** -0.5),
        'mla_w_down': normal((nA, D, MLA_Q_RANK + MLA_KV_RANK + MLA_ROPE), D ** -0.5),
        'mla_q_norm': gain((nA, MLA_Q_RANK)),
        'mla_kv_norm': gain((nA, MLA_KV_RANK)),
        'mla_w_uq': normal((nA, MLA_Q_RANK, MLA_HEADS * (MLA_NOPE + MLA_ROPE)), MLA_Q_RANK ** -0.5),
        'mla_w_ukv': normal((nA, MLA_KV_RANK, MLA_HEADS * (MLA_NOPE + MLA_V)), MLA_KV_RANK ** -0.5),
        'mla_w_o': normal((nA, MLA_HEADS * MLA_V, D), (MLA_HEADS * MLA_V) ** -0.5),
        'swa_w_qkv': normal((nB, D, (SWA_Q_HEADS + 2 * SWA_KV_HEADS) * SWA_HEAD_DIM), D ** -0.5),
        'swa_sink': normal((nB, SWA_Q_HEADS), 0.5),
        'swa_w_o': normal((nB, SWA_Q_HEADS * SWA_HEAD_DIM, D), (SWA_Q_HEADS * SWA_HEAD_DIM) ** -0.5),
        'na_w_qkv': normal((nC, D, 3 * NA_HEADS * NA_HEAD_DIM), D ** -0.5),
        'na_rpb': normal((nC, NA_HEADS, 2 * NA_KH - 1, 2 * NA_KW - 1), 0.2),
        'na_w_o': normal((nC, NA_HEADS * NA_HEAD_DIM, D), (NA_HEADS * NA_HEAD_DIM) ** -0.5),
        'diff_w_qkv': normal((nD, D, 6 * DIFF_HEADS * DIFF_HEAD_DIM), D ** -0.5),
        'diff_lambda': normal((nD, 4, DIFF_HEAD_DIM), 0.1),
        'diff_norm_g': gain((nD, 2 * DIFF_HEAD_DIM)),
        'diff_w_o': normal((nD, 2 * DIFF_HEADS * DIFF_HEAD_DIM, D), (2 * DIFF_HEADS * DIFF_HEAD_DIM) ** -0.5),
    }


def reference(x, c, ctx, c_ctx, mod_w, mod_b, norm_g, final_norm_g, ffn_w_in, ffn_w_out,
              mla_w_down, mla_q_norm, mla_kv_norm, mla_w_uq, mla_w_ukv, mla_w_o,
              swa_w_qkv, swa_sink, swa_w_o, na_w_qkv, na_rpb, na_w_o,
              diff_w_qkv, diff_lambda, diff_norm_g, diff_w_o):
    L = x.shape[1]
    cos, sin = axial_rope_tables(L, ROPE_DIM)
    xc = ctx
    cond_x = jax.nn.silu(c)
    cond_c = jax.nn.silu(c_ctx)
    for layer in range(DEPTH):
        kind = layer % N_MIXERS
        j = layer // N_MIXERS
        last = layer == DEPTH - 1
        mx = jnp.split((cond_x @ mod_w[layer] + mod_b[layer])[:, None, :], N_MOD, axis=-1)
        mc = jnp.split(cond_c @ mod_w[layer] + mod_b[layer], N_MOD, axis=-1)

        x = x + 0.5 * mx[2] * swiglu(modulate(rms_norm(x, norm_g[layer, 0]), mx[0], mx[1]),
                                     ffn_w_in[layer, 0], ffn_w_out[layer, 0])
        xc = xc + 0.5 * mc[2] * swiglu(modulate(rms_norm(xc, norm_g[layer, 0]), mc[0], mc[1]),
                                       ffn_w_in[layer, 0], ffn_w_out[layer, 0])

        hx = modulate(rms_norm(x, norm_g[layer, 1]), mx[3], mx[4])
        hc = modulate(rms_norm(xc, norm_g[layer, 1]), mc[3], mc[4])
        if kind == 0:
            ox, oc = mla_mixer(hx, hc, mla_w_down[j], mla_q_norm[j], mla_kv_norm[j], mla_w_uq[j],
                               mla_w_ukv[j], mla_w_o[j], cos, sin, not last)
        elif kind == 1:
            ox, oc = swa_mixer(hx, hc, swa_w_qkv[j], swa_sink[j], swa_w_o[j], cos, sin, not last)
        elif kind == 2:
            ox, oc = na_mixer(hx, hc, na_w_qkv[j], na_rpb[j], na_w_o[j], not last)
        else:
            lam_init = 0.8 - 0.6 * math.exp(-0.3 * layer)
            ox, oc = diff_mixer(hx, hc, diff_w_qkv[j], diff_lambda[j], diff_norm_g[j], diff_w_o[j],
                                cos, sin, lam_init, not last)
        x = x + mx[5] * ox

        x = x + 0.5 * mx[8] * swiglu(modulate(rms_norm(x, norm_g[layer, 2]), mx[6], mx[7]),
                                     ffn_w_in[layer, 1], ffn_w_out[layer, 1])
        if not last:
            xc = xc + mc[5] * oc
            xc = xc + 0.5 * mc[8] * swiglu(modulate(rms_norm(xc, norm_g[layer, 2]), mc[6], mc[7]),
                                           ffn_w_in[layer, 1], ffn_w_out[layer, 1])
    return rms_norm(x, final_norm_g)
```

```python
import numpy as np
from contextlib import ExitStack
import concourse.bass as bass
import concourse.mybir as mybir
from concourse.bass_utils import run_bass_kernel_spmd

F32 = mybir.dt.float32
BF16 = mybir.dt.bfloat16
AF = mybir.ActivationFunctionType
ALU = mybir.AluOpType

D = 1024
KD = 8
TL = 2048
TCX = 256
T = TL + TCX
DFF = 2816
CH = [(0, 512), (512, 512), (1024, 512), (1536, 512), (2048, 256)]
EPS = 1e-6
NEGFILL = -30000.0
ENGS = ['sp', 'act', 'pe', 'dve', 'pool']
DEBUG = False


class Op:
    __slots__ = ('eng', 'fn', 'r', 'w', 'dma', 'inc', 'waits', 'signal', 'val', 'fdeps')

    def __init__(self, eng, fn, r, w, dma, inc):
        self.eng, self.fn, self.r, self.w, self.dma, self.inc = eng, fn, r, w, dma, inc
        self.waits = []
        self.signal = False
        self.val = 0
        self.fdeps = ()


class Prog:
    def __init__(self):
        self.ops = []

    def add(self, eng, fn, r=(), w=(), dma=None, inc=16):
        self.ops.append(Op(eng, fn, tuple(r), tuple(w), dma, inc))

    def analyse(self):
        last_w = {}
        rd_eng = {}
        rd_dma = {}
        waited = {e: {} for e in ENGS}
        dma_cnt = {}
        ops = self.ops
        for j, op in enumerate(ops):
            deps = set()
            for b in op.r:
                i = last_w.get(b)
                if i is not None:
                    deps.add(i)
            for b in op.w:
                i = last_w.get(b)
                if i is not None:
                    deps.add(i)
                for i in rd_eng.get(b, {}).values():
                    deps.add(i)
                for i in rd_dma.get(b, ()):
                    deps.add(i)
            E = op.eng
            wt = waited[E]
            more = set()
            for i in deps:
                if ops[i].fn is None:
                    more.update(ops[i].fdeps)
            deps = set(i for i in deps if ops[i].fn is not None) | more
            if op.fn is None:
                op.fdeps = tuple(deps)
            for i in sorted(deps):
                p = ops[i]
                if p.dma is None:
                    if p.eng == E and E in ('pe', 'sp', 'pool'):
                        continue
                    if wt.get(p.eng, -1) >= i:
                        continue
                    wt[p.eng] = i
                    p.signal = True
                    op.waits.append(('c', i))
                else:
                    if wt.get(p.dma, 0) >= p.val:
                        continue
                    wt[p.dma] = p.val
                    op.waits.append(('d', p.dma, p.val))
            if op.dma is not None:
                dma_cnt[op.dma] = dma_cnt.get(op.dma, 0) + op.inc
                op.val = dma_cnt[op.dma]
            for b in op.r:
                if op.dma is not None:
                    rd_dma.setdefault(b, []).append(j)
                else:
                    rd_eng.setdefault(b, {})[E] = j
            for b in op.w:
                last_w[b] = j
                rd_eng[b] = {}
                rd_dma[b] = []
        cnt = {e: 0 for e in ENGS}
        for op in ops:
            if op.dma is None and op.signal:
                cnt[op.eng] += 1
                op.val = cnt[op.eng]
        self.dma_keys = sorted(dma_cnt.keys())
        return cnt

    def emit(self, nc, es):
        self.analyse()
        esem = {e: es.enter_context(nc.semaphore('s_' + e)) for e in ENGS}
        dsem = {k: es.enter_context(nc.semaphore('d_' + k)) for k in self.dma_keys}
        ops = self.ops
        per = {e: [op for op in ops if op.eng == e] for e in ENGS}

        def replay(ename, eng):
            for op in per[ename]:
                for w in op.waits:
                    if w[0] == 'c':
                        p = ops[w[1]]
                        eng.wait_ge(esem[p.eng], p.val)
                    else:
                        eng.wait_ge(dsem[w[1]], w[2])
                if op.fn is None:
                    continue
                ins = op.fn(eng)
                if op.dma is not None:
                    ins.then_inc(dsem[op.dma], op.inc)
                elif op.signal:
                    ins.then_inc(esem[op.eng], 1)

        with nc.Block() as block:
            @block.sync
            def _(e):
                replay('sp', e)

            @block.scalar
            def _(e):
                replay('act', e)

            @block.tensor
            def _(e):
                replay('pe', e)

            @block.vector
            def _(e):
                replay('dve', e)

            @block.gpsimd
            def _(e):
                replay('pool', e)


class Rot:
    def __init__(self, name, tiles):
        self.name, self.tiles, self.i = name, tiles, 0

    def get(self):
        i = self.i
        self.i = (i + 1) % len(self.tiles)
        return self.tiles[i], (self.name, i)


def build(layers=(0, 1, 2, 3), pair_groups=((0, 1), (2, 3), (4, 5), (6, 7)), do_mixer=True):
    nc = bass.Bass("TRN2", target_bir_lowering=False)
    P = Prog()
    es = ExitStack()

    def din(name, shape, dt=F32):
        return nc.dram_tensor(name, list(shape), dt, kind="ExternalInput")

    def dscr(name, shape, dt=BF16):
        return nc.dram_tensor(name, list(shape), dt)

    xT_in = din("xT_in", [128, KD, T])
    cc_in = din("cc_in", [128, KD, 2])
    modw = din("modw", [4, 128, KD, 9216])
    modb = din("modb", [128, 4, 72])
    ng_in = din("ng_in", [128, 4, 3, KD])
    fng_in = din("fng_in", [128, KD])
    win_d = din("win", [4, 2, 128, KD, 2 * DFF])
    wout_d = din("wout", [4, 2, 128, 22, D])
    cos_d = din("cos4", [128, TL])
    sin_d = din("sin4", [128, TL])
    mla_down = din("mla_down", [128, KD, 832])
    mla_downp = din("mla_downp", [128, KD, 64])
    mla_qn = din("mla_qn", [128, 4])
    mla_kvn = din("mla_kvn", [128, 2])
    mla_uq = din("mla_uq", [128, 4, 1536])
    mla_uqp = din("mla_uqp", [128, 4, 512])
    mla_ukv = din("mla_ukv", [128, 2, 2048])
    mla_wo = din("mla_wo", [128, 8, D])
    swa_qkv = din("swa_qkv", [128, KD, 1536])
    swa_qkp = din("swa_qkp", [128, KD, 1280])
    swa_sink = din("swa_sink", [128, 16])
    swa_wo = din("swa_wo", [128, 16, D])
    swa_mask = din("swa_mask", [128, 6, 512], BF16)
    halo_v = din("halo_v", [128, 2])
    na_qkv = din("na_qkv", [128, KD, 3072])
    na_rpbE = din("na_rpbE", [16, 128, 22 * 64])
    na_rv = din("na_rv", [128, 4 * 8 * 8])
    na_wo = din("na_wo", [128, 16, D])
    df_qkv = din("df_qkv", [128, KD, 3072])
    df_qkp = din("df_qkp", [128, KD, 2048])
    df_lam = din("df_lam", [128, 256])
    df_ng = din("df_ng", [128, 1])
    df_wo = din("df_wo", [128, 8, D])

    outT = nc.dram_tensor("outT", [128, KD, TL], F32, kind="ExternalOutput")

    qx_scr = dscr("qx_scr", [2048, TL])
    qp_scr = dscr("qp_scr", [2048, T])
    kc_scr = dscr("kc_scr", [1152, TCX])
    vc_scr = dscr("vc_scr", [TCX, 1024])
    KR = {0: 1088, 1: 256, 2: 1024, 3: 1024}
    VC = {0: 1024, 1: 256, 2: 1024, 3: 1024}
    bar_in = dscr("bar_in", [128, 128])
    bar_out = dscr("bar_out", [256, 128])
    kparts = {}
    vparts = {}
    for l in layers:
        kparts[l] = []
        r0 = 0
        while r0 < KR[l]:
            nr = min(512, KR[l] - r0)
            kparts[l].append((r0, nr, dscr("kmine%d_%d" % (l, r0), [nr, TL]), dscr("kall%d_%d" % (l, r0), [2 * nr, TL])))
            r0 += nr
        vparts[l] = [(dscr("vmine%d_%d" % (l, j), [1024, VC[l]]), dscr("vall%d_%d" % (l, j), [2048, VC[l]]))
                     for j in range(2)]

    def kmine_rows(l, row0, nrows):
        for (r0, nr, mine, all_) in kparts[l]:
            if r0 <= row0 and row0 + nrows <= r0 + nr:
                return mine[row0 - r0:row0 - r0 + nrows, :]
        raise AssertionError((l, row0, nrows))

    def kall_rows(l, r, row0, nrows):
        for (r0, nr, mine, all_) in kparts[l]:
            if r0 <= row0 and row0 + nrows <= r0 + nr:
                return all_[r * nr + row0 - r0:r * nr + row0 - r0 + nrows, :]
        raise AssertionError((l, row0, nrows))

    def vmine_rows(l, t0, nt):
        j = t0 // 1024
        assert t0 + nt <= (j + 1) * 1024
        return vparts[l][j][0][t0 - j * 1024:t0 - j * 1024 + nt, :]

    def vall_rows(l, r, t0, nt):
        j = t0 // 1024
        assert t0 + nt <= (j + 1) * 1024
        return vparts[l][j][1][r * 1024 + t0 - j * 1024:r * 1024 + t0 - j * 1024 + nt, :]

    ARENA = 212000
    arena = es.enter_context(nc.sbuf_tensor("arena", [128, ARENA // 4], F32))
    a0 = nc.lookup_mloc(arena).addr
    cur = [0]

    def sb(name, shape, dt=F32, at=None):
        nb = int(np.prod(shape[1:])) * (4 if dt == F32 else 2)
        nb = (nb + 31) // 32 * 32
        if at is None:
            off = cur[0]
            cur[0] += nb
            assert cur[0] <= ARENA, (name, cur[0])
        else:
            off = at
        return nc.alloc_sbuf_tensor_at(name, list(shape), dt, offset=a0 + off)

    def region(nbytes):
        off = cur[0]
        cur[0] += nbytes
        assert cur[0] <= ARENA, ('region', cur[0])
        return off

    xT = sb("xT", [128, KD, T])
    R1 = region(36864)
    R2 = region(27648)
    hT = sb("hT", [128, KD, T], BF16, at=R1)
    win_p = Rot("win", [sb("win%d" % i, [128, KD, 512], BF16) for i in range(3)])
    wout_p = Rot("wout", [sb("wout%d" % i, [128, 2, D], BF16) for i in range(2)])
    modw_p = Rot("modw", [sb("modw%d" % i, [128, KD, 128], F32, at=R2 + i * 4096) for i in range(2)])
    tmp_p = Rot("tmp", [sb("tmp%d" % i, [128, 512]) for i in range(4)])
    rs_p = Rot("rs", [sb("rs%d" % i, [128, 512]) for i in range(2)])
    cs_p = Rot("cs", [sb("cs%d" % i, [128, 2, 512]) for i in range(1)])
    act_p = Rot("act", [sb("act%d" % i, [128, 2, 512], BF16) for i in range(2)])
    pt_p = Rot("pt", [sb("pt%d" % i, [128, 512], BF16) for i in range(3)])
    stg_p = Rot("stg", [sb("stg%d" % i, [128, 512], BF16) for i in range(3)])
    condT = sb("condT", [128, KD, 2])
    modT = sb("modT", [128, 72, 2])
    modb_s = sb("modb_s", [128, 4, 72])
    ng_s = sb("ng_s", [128, 4, 3, KD])
    fng_s = sb("fng_s", [128, KD])
    gs_s = sb("gs_s", [128, 3, KD, 2])
    gt_s = sb("gt_s", [128, 3, KD, 2])
    ones_f = sb("ones_f", [128, 128])
    ones_b = sb("ones_b", [128, 128], BF16)
    small = sb("small", [128, 64])
    eps_t = sb("eps_t", [128, 8])
    q0 = sb("q0", [128, T], BF16, at=R1)
    q1 = sb("q1", [128, T], BF16, at=R1 + 4608)
    kx = sb("kx", [128, 2 * TL], BF16, at=R1 + 9216)
    krx = sb("krx", [128, 2 * TL], BF16, at=R1 + 17408)
    kcx = sb("kcx", [128, TCX], BF16, at=R1 + 25600)
    krc = sb("krc", [128, TCX], BF16, at=R1 + 26112)
    vt = sb("vt", [128, 34, 128], BF16, at=R1 + 26624)
    assert 26624 + 8704 <= 36864
    q2 = sb("q2", [128, TL], BF16)
    oh = sb("oh", [128, T], BF16)
    ATT_KEYS = ['q0', 'q1', 'kx', 'krx', 'kcx', 'krc', 'vt']
    HT_KEYS = [('hT', ci) for ci in range(5)]
    R2_KEYS = [('modw', 0), ('modw', 1), 'o0f', 'swa_m', 'na_E', 'na_rv'] + [('cq', ci) for ci in range(5)] + \
              [('ckv', ci) for ci in range(5)]

    def fence(engs, keys):
        for e_ in engs:
            P.add(e_, None, [], keys)

    ps_p = Rot("ps", [es.enter_context(nc.psum_tensor("ps%d" % i, [128, 512], F32)) for i in range(4)])
    acc_p = Rot("acc", [es.enter_context(nc.psum_tensor("acc%d" % i, [128, 512], F32)) for i in range(3)])
    pmod = es.enter_context(nc.psum_tensor("pmod", [128, 72, 2], F32))

    def ACT(out, in_, func, r, w, **kw):
        P.add('act', lambda e: e.activation(out=out, in_=in_, func=func, **kw), r, w)

    def MM(out, lhsT, rhs, start, stop, r, w):
        P.add('pe', lambda e: e.matmul(out, lhsT, rhs, start=start, stop=stop), r, w)

    def TT(out, in0, in1, op, r, w):
        P.add('dve', lambda e: e.tensor_tensor(out=out, in0=in0, in1=in1, op=op), r, w)

    def TS(out, in0, s1, s2, op0, op1, r, w):
        if op1 is None:
            P.add('dve', lambda e: e.tensor_scalar(out=out, in0=in0, scalar1=s1, scalar2=0.0, op0=op0, op1=ALU.add), r, w)
        else:
            P.add('dve', lambda e: e.tensor_scalar(out=out, in0=in0, scalar1=s1, scalar2=s2, op0=op0, op1=op1), r, w)

    def STT(out, in0, scalar, in1, op0, op1, r, w):
        P.add('dve', lambda e: e.scalar_tensor_tensor(out=out, in0=in0, scalar=scalar, in1=in1, op0=op0, op1=op1), r, w)

    def CP(out, in_, r, w):
        P.add('dve', lambda e: e.tensor_copy(out=out, in_=in_), r, w)

    def RCP(out, in_, r, w):
        P.add('dve', lambda e: e.reciprocal(out=out, in_=in_), r, w)

    def DMA(q, out, in_, key, r, w):
        P.add(q, lambda e: e.dma_start(out=out, in_=in_), r, w, dma=key)

    def MEMSET(ap, v, w):
        P.add('dve', lambda e: e.memset(ap, v), (), w)

    def xkeys(ci):
        return [('xT', ci, k) for k in range(KD)]

    MEMSET(ones_f[:, :], 1.0, ['ones_f'])
    MEMSET(ones_b[:, :], 1.0, ['ones_b'])
    MEMSET(eps_t[:, :], EPS, ['eps_t'])
    for ci, (c0, n) in enumerate(CH):
        DMA('sp', xT[:, :, c0:c0 + n], xT_in[:, :, c0:c0 + n], 'xin%d' % ci, [], xkeys(ci))
    DMA('sp', condT[:, :, :], cc_in[:, :, :], 'misc_c', [], ['condT'])
    DMA('sp', modb_s[:, :, :], modb[:, :, :], 'misc', [], ['modb'])
    DMA('sp', ng_s[:, :, :, :], ng_in[:, :, :, :], 'misc', [], ['ng'])
    DMA('sp', fng_s[:, :], fng_in[:, :], 'misc', [], ['fng'])
    ACT(condT[:, :, :], condT[:, :, :], AF.Silu, ['condT'], ['condT'])

    def mod_phase(l):
        for j in range(72):
            mw, mk = modw_p.get()
            DMA('sp', mw[:, :, :], modw[l, :, :, j * 128:(j + 1) * 128], 'modw%d' % mk[1], [], [mk])
            for k in range(KD):
                MM(pmod[:, j, :], mw[:, k, :], condT[:, k, :], k == 0, k == KD - 1, [mk, 'condT'], ['pmod'])
        for s in range(2):
            TT(modT[:, :, s], pmod[:, :, s], modb_s[:, l, :], ALU.add, ['pmod', 'modb'], ['modT'])
        for i in range(3):
            for s in range(2):
                STT(gs_s[:, i, :, s], modT[:, (3 * i + 1) * 8:(3 * i + 2) * 8, s], 1.0, ng_s[:, l, i, :],
                    ALU.add, ALU.mult, ['modT', 'ng'], ['gs'])
                TS(gt_s[:, i, :, s], modT[:, (3 * i + 2) * 8:(3 * i + 3) * 8, s], 0.5 if i != 1 else 1.0, None,
                   ALU.mult, None, ['modT'], ['gt'])

    def norm_phase(sub):
        for ci, (c0, n) in enumerate(CH):
            s = 1 if ci == 4 else 0
            ss, ssk = ps_p.get()
            for k in range(KD):
                sq, sqk = tmp_p.get()
                ACT(sq[:, :n], xT[:, k, c0:c0 + n], AF.Square, [('xT', ci, k)], [sqk])
                MM(ss[:, :n], ones_f[:, :], sq[:, :n], k == 0, k == KD - 1, [sqk, 'ones_f'], [ssk])
            rstd, rk = rs_p.get()
            ACT(rstd[:, :n], ss[:, :n], AF.Sqrt, [ssk], [rk], scale=1.0 / D, bias=eps_t[:, 0:1])
            RCP(rstd[:, :n], rstd[:, :n], [rk], [rk])
            for k in range(KD):
                t, tk = tmp_p.get()
                TT(t[:, :n], xT[:, k, c0:c0 + n], rstd[:, :n], ALU.mult, [('xT', ci, k), rk], [tk])
                ACT(hT[:, k, c0:c0 + n], t[:, :n], AF.Identity, [tk, 'gs', 'modT'], [('hT', ci)],
                    scale=gs_s[:, sub, k, s:s + 1], bias=modT[:, 3 * sub * 8 + k, s:s + 1])

    def ffn_phase(l, f, sub):
        for g in range(11):
            win, wk = win_p.get()
            wo_, wok = wout_p.get()
            DMA('pool', win[:, :, 0:256], win_d[l, f, :, :, g * 256:(g + 1) * 256], 'win%d' % wk[1], [], [wk])
            DMA('pool', win[:, :, 256:512], win_d[l, f, :, :, DFF + g * 256:DFF + (g + 1) * 256], 'win%d' % wk[1], [], [wk])
            DMA('pool', wo_[:, :, :], wout_d[l, f, :, 2 * g:2 * g + 2, :], 'wout%d' % wok[1], [], [wok])
            for ci, (c0, n) in enumerate(CH):
                s = 1 if ci == 4 else 0
                a, ak = act_p.get()
                for fb in range(2):
                    pg, pgk = ps_p.get()
                    pu, puk = ps_p.get()
                    for k in range(KD):
                        MM(pg[:, :n], win[:, k, fb * 128:(fb + 1) * 128], hT[:, k, c0:c0 + n], k == 0, k == KD - 1,
                           [wk, ('hT', ci)], [pgk])
                    for k in range(KD):
                        MM(pu[:, :n], win[:, k, 256 + fb * 128:256 + (fb + 1) * 128], hT[:, k, c0:c0 + n], k == 0,
                           k == KD - 1, [wk, ('hT', ci)], [puk])
                    sg, sgk = tmp_p.get()
                    ACT(sg[:, :n], pg[:, :n], AF.Silu, [pgk], [sgk])
                    TT(a[:, fb, :n], sg[:, :n], pu[:, :n], ALU.mult, [sgk, puk], [ak])
                for dk in range(KD):
                    po, pok = ps_p.get()
                    for fb in range(2):
                        MM(po[:, :n], wo_[:, fb, dk * 128:(dk + 1) * 128], a[:, fb, :n], fb == 0, fb == 1, [wok, ak], [pok])
                    STT(xT[:, dk, c0:c0 + n], po[:, :n], gt_s[:, sub, dk, s:s + 1], xT[:, dk, c0:c0 + n],
                        ALU.mult, ALU.add, [pok, 'gt', ('xT', ci, dk)], [('xT', ci, dk)])

    def load_w(src_ap, K, ncols):
        w, wk = win_p.get()
        DMA('pool', w[:, 0:K, 0:ncols], src_ap, 'win%d' % wk[1], [], [wk])
        return w, wk

    def load_cs(ci):
        c0, n = CH[ci]
        cs, ck = cs_p.get()
        DMA('sp', cs[:, 0, :n], cos_d[:, c0:c0 + n], 'cs', [], [ck])
        DMA('sp', cs[:, 1, :n], sin_d[:, c0:c0 + n], 'cs', [], [ck])
        return cs, ck

    def proj_fm(src, srckey, K, wA, wAk, colA, M, ci, wB=None, wBk=None, colB=0):
        c0, n = CH[ci]
        pa, pak = ps_p.get()
        for k in range(K):
            MM(pa[:M, :n], wA[:, k, colA:colA + M], src[:, k, c0:c0 + n], k == 0, k == K - 1, [wAk, srckey(ci)], [pak])
        if wB is None:
            return pa, pak, None, None
        pb, pbk = ps_p.get()
        for k in range(K):
            MM(pb[:M, :n], wB[:, k, colB:colB + M], src[:, k, c0:c0 + n], k == 0, k == K - 1, [wBk, srckey(ci)], [pbk])
        return pa, pak, pb, pbk

    def rope_to(dst_ap, M, n, pa, pak, pb, pbk, cs, ck, dkey):
        t1, t1k = tmp_p.get()
        t2, t2k = tmp_p.get()
        TT(t1[:M, :n], pa[:M, :n], cs[:M, 0, :n], ALU.mult, [pak, ck], [t1k])
        TT(t2[:M, :n], pb[:M, :n], cs[:M, 1, :n], ALU.mult, [pbk, ck], [t2k])
        TT(dst_ap, t1[:M, :n], t2[:M, :n], ALU.add, [t1k, t2k], [dkey])

    def store_rows(dram_ap, sb_ap, sbkey, dkey):
        DMA('sp', dram_ap, sb_ap, dkey if isinstance(dkey, str) else dkey[0], [sbkey], [dkey])

    def hkey(ci):
        return ('hT', ci)

    def exchange(l):
        fence(['sp'], ATT_KEYS + HT_KEYS)
        grp = [list(g) for g in pair_groups]
        P.add('pool', lambda e: e.collective_compute(
            "AllGather", ALU.bypass, replica_groups=grp, ins=[bar_in.ap().opt()], outs=[bar_out.ap().opt()]),
            ['kmine', 'vmine'], ['kmine', 'vmine'], dma='ccp%d' % l, inc=1)
        for pi_, (r0, nr, mine, all_) in enumerate(kparts[l]):
            P.add('pool', lambda e, mine=mine, all_=all_: e.collective_compute(
                "AllGather", ALU.bypass, replica_groups=grp, ins=[mine.ap().opt()], outs=[all_.ap().opt()]),
                ['kmine'], [('kallp', pi_)], dma='cck%d_%d' % (l, pi_), inc=1)
        for pi_, (mine, all_) in enumerate(vparts[l]):
            P.add('pool', lambda e, mine=mine, all_=all_: e.collective_compute(
                "AllGather", ALU.bypass, replica_groups=grp, ins=[mine.ap().opt()], outs=[all_.ap().opt()]),
                ['vmine'], [('vallp', pi_)], dma='ccv%d_%d' % (l, pi_), inc=1)
        P.add('pool', lambda e: e.collective_compute(
            "AllGather", ALU.bypass, replica_groups=grp, ins=[bar_in.ap().opt()], outs=[bar_out.ap().opt()]),
            [('kallp', i) for i in range(len(kparts[l]))] + [('vallp', i) for i in range(2)], ['kall', 'vall'],
            dma='ccb%d' % l, inc=1)

    def v_proj_tm(l, w, wk, col0, ncols, vcol0):
        for tb in range(T // 128):
            ci = min(tb // 4, 4)
            pv, pvk = ps_p.get()
            for k in range(KD):
                MM(pv[:, :ncols], hT[:, k, tb * 128:(tb + 1) * 128], w[:, k, col0:col0 + ncols], k == 0, k == KD - 1,
                   [wk, ('hT', ci)], [pvk])
            st, sk = stg_p.get()
            CP(st[:, :ncols], pv[:, :ncols], [pvk], [sk])
            if tb < 16:
                store_rows(vmine_rows(l, tb * 128, 128)[:, vcol0:vcol0 + ncols], st[:, :ncols], sk, 'vmine')
            else:
                store_rows(vc_scr[(tb - 16) * 128:(tb - 15) * 128, vcol0:vcol0 + ncols], st[:, :ncols], sk, 'vc_scr')

    def qk_tile(l, ci, M, pa, pak, pb, pbk, cs, ck, dst_rot, dst_plain, rot_key, plain_key):
        c0, n = CH[ci]
        if dst_plain is not None:
            st, sk = stg_p.get()
            CP(st[:M, :n], pa[:M, :n], [pak], [sk])
            store_rows(dst_plain, st[:M, :n], sk, plain_key)
        if dst_rot is not None:
            st, sk = stg_p.get()
            rope_to(st[:M, :n], M, n, pa, pak, pb, pbk, cs, ck, sk)
            store_rows(dst_rot, st[:M, :n], sk, rot_key)

    def attn_unit(scale, passes_x, passes_c, xblocks, v_of, dv, qcols, o_dst, o_key, final, mask=None,
                  extra_r=()):
        for ci, (c0, n) in enumerate(CH):
            if ci == 4:
                blocks = []
            else:
                blocks = xblocks(ci)
            po, pok = acc_p.get()
            pd, pdk = acc_p.get()
            seq = [('c', 0)] + [('x', b) for b in blocks] + [('c', 1)]
            for bi, (kind, b) in enumerate(seq):
                if kind == 'c':
                    kc0, vb, (qa, qb), mspec = b * 128, 32 + b, (0, n), None
                    passes = passes_c
                else:
                    kc0, vb, (qa, qb), mspec = b
                    passes = passes_x
                st_, stk = ps_p.get()
                for pi, (kt, qt, rows, kkey, qkey) in enumerate(passes):
                    MM(st_[:, qa:qb], kt[:rows, kc0:kc0 + 128], qt[:rows, c0 + qa:c0 + qb], pi == 0,
                       pi == len(passes) - 1, [kkey, qkey], [stk])
                pt, ptk = pt_p.get()
                ACT(pt[:, qa:qb], st_[:, qa:qb], AF.Exp, [stk], [ptk], scale=scale)
                if mspec is not None:
                    mask(pt, ptk, qa, qb, mspec)
                first = bi == 0
                last = bi == len(seq) - 1
                MM(po[:dv, qa:qb], v_of(vb), pt[:, qa:qb], first, last, ['vt', ptk], [pok])
                MM(pd[:dv, qa:qb], ones_b[:, :dv], pt[:, qa:qb], first, last, ['ones_b', ptk], [pdk])
            final(ci, c0, n, po, pok, pd, pdk)

    def plain_final(dst_tile, dst_key, dv, den_add=None):
        def fin(ci, c0, n, po, pok, pd, pdk):
            r, rk = tmp_p.get()
            if den_add is not None:
                TS(r[:dv, :n], pd[:dv, :n], den_add, None, ALU.add, None, [pdk, 'small'], [rk])
                RCP(r[:dv, :n], r[:dv, :n], [rk], [rk])
            else:
                RCP(r[:dv, :n], pd[:dv, :n], [pdk], [rk])
            TT(dst_tile[:dv, c0:c0 + n], po[:dv, :n], r[:dv, :n], ALU.mult, [pok, rk], [(dst_key, ci)])
        return fin

    def wo_apply(l, wo_d, src_tile, src_key, row_blk, rows):
        w, wk = wout_p.get()
        DMA('pool', w[:rows, 0, :], wo_d[0:rows, row_blk, :] if rows == 128 else wo_d[0:rows, row_blk, :],
            'wout%d' % wk[1], [], [wk])
        for ci, (c0, n) in enumerate(CH):
            s = 1 if ci == 4 else 0
            for dk in range(KD):
                po, pok = ps_p.get()
                MM(po[:, :n], w[:rows, 0, dk * 128:(dk + 1) * 128], src_tile[:rows, c0:c0 + n], True, True,
                   [wk, (src_key, ci)], [pok])
                STT(xT[:, dk, c0:c0 + n], po[:, :n], gt_s[:, 1, dk, s:s + 1], xT[:, dk, c0:c0 + n],
                    ALU.mult, ALU.add, [pok, 'gt', ('xT', ci, dk)], [('xT', ci, dk)])

    def load_v(l, col0, dv):
        for r in range(2):
            for j in range(2):
                DMA('sp', vt[:, 16 * r + 8 * j:16 * r + 8 * j + 8, :dv],
                    vall_rows(l, r, j * 1024, 1024)[:, col0:col0 + dv].rearrange("(b p) c -> p b c", p=128),
                    'vt', ['vall'], ['vt'])
        DMA('sp', vt[:, 32:34, :dv], vc_scr[:, col0:col0 + dv].rearrange("(b p) c -> p b c", p=128), 'vt',
            ['vc_scr'], ['vt'])

    def load_kx(l, tile, key, row0, rows):
        for r in range(2):
            DMA('sp', tile[:rows, r * TL:(r + 1) * TL], kall_rows(l, r, row0, rows), key, ['kall'], [key])

    def full_blocks(ci):
        n = CH[ci][1]
        return [(kb * 128, kb, (0, n), None) for kb in range(32)]

    def diff_mixer(l, lam_init):
        scale = 64 ** -0.5
        lam = small[:, 0:1]
        DMA('sp', tmp_lam[:, :], df_lam[:, :], 'misc', [], ['lamraw'])
        DMA('sp', small[:, 8:9], df_ng[:, :], 'misc', [], ['small'])
        TT(tmp_lam[:, 0:64], tmp_lam[:, 0:64], tmp_lam[:, 64:128], ALU.mult, ['lamraw'], ['lamraw'])
        TT(tmp_lam[:, 128:192], tmp_lam[:, 128:192], tmp_lam[:, 192:256], ALU.mult, ['lamraw'], ['lamraw'])
        P.add('dve', lambda e: e.reduce_sum(out=small[:, 1:2], in_=tmp_lam[:, 0:64], axis=mybir.AxisListType.X),
              ['lamraw'], ['small'])
        P.add('dve', lambda e: e.reduce_sum(out=small[:, 2:3], in_=tmp_lam[:, 128:192], axis=mybir.AxisListType.X),
              ['lamraw'], ['small'])
        ACT(small[:, 1:3], small[:, 1:3], AF.Exp, ['small'], ['small'])
        TT(small[:, 0:1], small[:, 1:2], small[:, 2:3], ALU.subtract, ['small'], ['small'])
        TS(small[:, 0:1], small[:, 0:1], lam_init, None, ALU.add, None, ['small'], ['small'])
        TS(small[:, 3:4], small[:, 0:1], -1.0, None, ALU.mult, None, ['small'], ['small'])
        TS(small[:, 9:10], small[:, 8:9], 1.0 - lam_init, None, ALU.mult, None, ['small'], ['small'])

        for part, dst_rot, dst_plain_l, dst_plain_c in (('q', (lambda a, b: qx_scr[a:a + b, :]), qp_scr, qp_scr), ('k', (lambda a, b: kmine_rows(l, a, b)), None, kc_scr)):
            base = 0 if part == 'q' else 1024
            for half in range(2):
                wA, wAk = load_w(df_qkv[:, :, base + half * 512:base + (half + 1) * 512], KD, 512)
                wB, wBk = load_w(df_qkp[:, :, base + half * 512:base + (half + 1) * 512], KD, 512)
                for ci, (c0, n) in enumerate(CH):
                    cs, ck = load_cs(ci) if ci < 4 else (None, None)
                    for t in range(4):
                        row0 = (half * 4 + t) * 128
                        if ci < 4:
                            pa, pak, pb, pbk = proj_fm(hT, hkey, KD, wA, wAk, t * 128, 128, ci, wB, wBk, t * 128)
                            qk_tile(l, ci, 128, pa, pak, pb, pbk, cs, ck,
                                    dst_rot(row0, 128)[:, c0:c0 + n],
                                    dst_plain_l[row0:row0 + 128, c0:c0 + n] if dst_plain_l is not None else None,
                                    'qx_scr' if part == 'q' else 'kmine', 'qp_scr')
                        else:
                            pa, pak, _, _ = proj_fm(hT, hkey, KD, wA, wAk, t * 128, 128, ci)
                            if part == 'q':
                                qk_tile(l, ci, 128, pa, pak, None, None, None, None, None,
                                        qp_scr[row0:row0 + 128, c0:c0 + n], None, 'qp_scr')
                            else:
                                qk_tile(l, ci, 128, pa, pak, None, None, None, None, None,
                                        kc_scr[row0:row0 + 128, 0:n], None, 'kc_scr')
        for half in range(2):
            w, wk = load_w(df_qkv[:, :, 2048 + half * 512:2048 + (half + 1) * 512], KD, 512)
            v_proj_tm(l, w, wk, 0, 512, half * 512)
        exchange(l)

        for h in range(8):
            load_v(l, h * 128, 128)
            for i in range(2):
                u = h * 2 + i
                qt_x, qt_p = (q0, q1)
                DMA('sp', q2[:64, :], qx_scr[u * 64:(u + 1) * 64, :], 'q2', ['qx_scr'], ['q2'])
                DMA('sp', q1[:64, :], qp_scr[u * 64:(u + 1) * 64, :], 'q1', ['qp_scr'], ['q1'])
                load_kx(l, kx, 'kx', u * 64, 64)
                DMA('sp', kcx[:64, :], kc_scr[u * 64:(u + 1) * 64, :], 'kcx', ['kc_scr'], ['kcx'])

                def fin(ci, c0, n, po, pok, pd, pdk, i=i):
                    r, rk = tmp_p.get()
                    RCP(r[:, :n], pd[:, :n], [pdk], [rk])
                    if i == 0:
                        TT(o0f[:, c0:c0 + n], po[:, :n], r[:, :n], ALU.mult, [pok, rk], [('o0f', ci)])
                    else:
                        t, tk = tmp_p.get()
                        TT(t[:, :n], po[:, :n], r[:, :n], ALU.mult, [pok, rk], [tk])
                        STT(o0f[:, c0:c0 + n], t[:, :n], small[:, 3:4], o0f[:, c0:c0 + n], ALU.mult, ALU.add,
                            [tk, 'small', ('o0f', ci)], [('o0f', ci)])
                attn_unit(scale, [(kx, q2, 64, 'kx', 'q2')], [(kcx, q1, 64, 'kcx', 'q1')], full_blocks,
                          lambda vb: vt[:, vb, :], 128, None, None, None, fin)
            for ci, (c0, n) in enumerate(CH):
                tk = ('o0f', ci)
                sq, sqk = tmp_p.get()
                ACT(sq[:, :n], o0f[:, c0:c0 + n], AF.Square, [tk], [sqk])
                ss, ssk = ps_p.get()
                MM(ss[:, :n], ones_f[:, :], sq[:, :n], True, True, [sqk, 'ones_f'], [ssk])
                rs, rsk = rs_p.get()
                ACT(rs[:, :n], ss[:, :n], AF.Sqrt, [ssk], [rsk], scale=1.0 / 128, bias=eps_t[:, 0:1])
                RCP(rs[:, :n], rs[:, :n], [rsk], [rsk])
                STT(oh[:, c0:c0 + n], o0f[:, c0:c0 + n], small[:, 9:10], rs[:, :n], ALU.mult, ALU.mult,
                    [tk, rsk, 'small'], [('oh', ci)])
            wo_apply(l, df_wo, oh, 'oh', h, 128)

    tmp_lam = sb("tmp_lam", [128, 256])
    o0f = sb("o0f", [128, T], F32, at=R2 + 8192)

    cqn = None

    def mla_mixer(l):
        scale = 192 ** -0.5
        DMA('sp', small[:, 16:20], mla_qn[:, :], 'misc', [], ['small'])
        DMA('sp', small[:, 20:22], mla_kvn[:, :], 'misc', [], ['small'])
        wq_, wqk = load_w(mla_down[:, :, 0:512], KD, 512)
        wk_, wkk = load_w(mla_down[:, :, 512:832], KD, 320)
        for ci, (c0, n) in enumerate(CH):
            for (wt_, wtk, nblk, nrm_col, dst, dkey, inv) in ((wq_, wqk, 4, 16, cq_t, 'cq', 1.0 / 512),
                                                           (wk_, wkk, 2, 20, ckv_t, 'ckv', 1.0 / 256)):
                pss = []
                ss, ssk = acc_p.get()
                for b in range(nblk):
                    pa, pak, _, _ = proj_fm(hT, hkey, KD, wt_, wtk, b * 128, 128, ci)
                    pss.append((pa, pak))
                    sq, sqk = tmp_p.get()
                    ACT(sq[:, :n], pa[:, :n], AF.Square, [pak], [sqk])
                    MM(ss[:, :n], ones_f[:, :], sq[:, :n], b == 0, b == nblk - 1, [sqk, 'ones_f'], [ssk])
                rs, rsk = rs_p.get()
                ACT(rs[:, :n], ss[:, :n], AF.Sqrt, [ssk], [rsk], scale=inv, bias=eps_t[:, 0:1])
                RCP(rs[:, :n], rs[:, :n], [rsk], [rsk])
                for b in range(nblk):
                    pa, pak = pss[b]
                    STT(dst[:, b, c0:c0 + n], pa[:, :n], small[:, nrm_col + b:nrm_col + b + 1], rs[:, :n],
                        ALU.mult, ALU.mult, [pak, rsk, 'small'], [(dkey, ci)])
        wp_, wpk = load_w(mla_downp[:, :, :], KD, 64)
        for ci, (c0, n) in enumerate(CH):
            if ci < 4:
                cs, ck = load_cs(ci)
                pa, pak, pb, pbk = proj_fm(hT, hkey, KD, wk_, wkk, 256, 64, ci, wp_, wpk, 0)
                qk_tile(l, ci, 64, pa, pak, pb, pbk, cs, ck, kmine_rows(l, 1024, 64)[:, c0:c0 + n], None, 'kmine', None)
            else:
                pa, pak, _, _ = proj_fm(hT, hkey, KD, wk_, wkk, 256, 64, ci)
                qk_tile(l, ci, 64, pa, pak, None, None, None, None, None, kc_scr[1024:1088, 0:n], None, 'kc_scr')
        cqkey = lambda ci: ('cq', ci)
        ckvkey = lambda ci: ('ckv', ci)
        for h in range(8):
            wA, wAk = load_w(mla_uq[:, :, h * 192:(h + 1) * 192], 4, 192)
            wB, wBk = load_w(mla_uqp[:, :, h * 64:(h + 1) * 64], 4, 64)
            wkv, wkvk = load_w(mla_ukv[:, :, h * 256:(h + 1) * 256], 2, 256)
            for ci, (c0, n) in enumerate(CH):
                pa, pak, _, _ = proj_fm(cq_t, cqkey, 4, wA, wAk, 0, 128, ci)
                qk_tile(l, ci, 128, pa, pak, None, None, None, None, None, qp_scr[h * 256:h * 256 + 128, c0:c0 + n],
                        None, 'qp_scr')
                if ci < 4:
                    cs, ck = load_cs(ci)
                    pa, pak, pb, pbk = proj_fm(cq_t, cqkey, 4, wA, wAk, 128, 64, ci, wB, wBk, 0)
                    qk_tile(l, ci, 64, pa, pak, pb, pbk, cs, ck, qx_scr[h * 64:h * 64 + 64, c0:c0 + n],
                            qp_scr[h * 256 + 128:h * 256 + 192, c0:c0 + n], 'qx_scr', 'qp_scr')
                else:
                    pa, pak, _, _ = proj_fm(cq_t, cqkey, 4, wA, wAk, 128, 64, ci)
                    qk_tile(l, ci, 64, pa, pak, None, None, None, None, None,
                            qp_scr[h * 256 + 128:h * 256 + 192, c0:c0 + n], None, 'qp_scr')
                pa, pak, _, _ = proj_fm(ckv_t, ckvkey, 2, wkv, wkvk, 0, 128, ci)
                if ci < 4:
                    qk_tile(l, ci, 128, pa, pak, None, None, None, None, None, kmine_rows(l, h * 128, 128)[:, c0:c0 + n],
                            None, 'kmine')
                else:
                    qk_tile(l, ci, 128, pa, pak, None, None, None, None, None, kc_scr[h * 128:(h + 1) * 128, 0:n],
                            None, 'kc_scr')
            for tb in range(T // 128):
                ci = min(tb // 4, 4)
                pv, pvk = ps_p.get()
                for k in range(2):
                    MM(pv[:, :128], ckv_t[:, k, tb * 128:(tb + 1) * 128], wkv[:, k, 128:256], k == 0, k == 1,
                       [wkvk, ('ckv', ci)], [pvk])
                st, sk = stg_p.get()
                CP(st[:, :128], pv[:, :128], [pvk], [sk])
                if tb < 16:
                    store_rows(vmine_rows(l, tb * 128, 128)[:, h * 128:(h + 1) * 128], st[:, :128], sk, 'vmine')
                else:
                    store_rows(vc_scr[(tb - 16) * 128:(tb - 15) * 128, h * 128:(h + 1) * 128], st[:, :128], sk, 'vc_scr')
        exchange(l)
        load_kx(l, krx, 'krx', 1024, 64)
        DMA('sp', krc[:64, :], kc_scr[1024:1088, :], 'krc', ['kc_scr'], ['krc'])
        for h in range(8):
            load_v(l, h * 128, 128)
            DMA('sp', q0[:, :], qp_scr[h * 256:h * 256 + 128, :], 'q0', ['qp_scr'], ['q0'])
            DMA('sp', q1[:64, :], qp_scr[h * 256 + 128:h * 256 + 192, :], 'q1', ['qp_scr'], ['q1'])
            DMA('sp', q2[:64, :], qx_scr[h * 64:(h + 1) * 64, :], 'q2', ['qx_scr'], ['q2'])
            load_kx(l, kx, 'kx', h * 128, 128)
            DMA('sp', kcx[:, :], kc_scr[h * 128:(h + 1) * 128, :], 'kcx', ['kc_scr'], ['kcx'])
            attn_unit(scale, [(kx, q0, 128, 'kx', 'q0'), (krx, q2, 64, 'krx', 'q2')],
                      [(kcx, q0, 128, 'kcx', 'q0'), (krc, q1, 64, 'krc', 'q1')], full_blocks,
                      lambda vb: vt[:, vb, :], 128, None, None, None, plain_final(oh, 'oh', 128))
            wo_apply(l, mla_wo, oh, 'oh', h, 128)

    cq_t = sb("cq_t", [128, 4, T], BF16, at=R2)
    ckv_t = sb("ckv_t", [128, 2, T], BF16, at=R2 + 18432)

    swa_m = sb("swa_m", [128, 6, 512], BF16, at=R2 + 17408)
    halo_s = sb("halo_s", [128, 2])

    def swa_mixer(l):
        scale = 64 ** -0.5
        DMA('sp', swa_m[:, :, :], swa_mask[:, :, :], 'misc', [], ['swa_m'])
        DMA('sp', halo_s[:, :], halo_v[:, :], 'misc', [], ['halo_s'])
        DMA('sp', small[:, 24:40], swa_sink[:, :], 'misc', [], ['small'])
        ACT(small[:, 24:40], small[:, 24:40], AF.Exp, ['small'], ['small'])
        for part, ntile, dst_rot, plain_l, plain_c in (('q', 8, (lambda a, b: qx_scr[a:a + b, :]), qp_scr, qp_scr), ('k', 2, (lambda a, b: kmine_rows(l, a, b)), None, kc_scr)):
            base = 0 if part == 'q' else 1024
            for grp in range((ntile + 3) // 4):
                nt = min(4, ntile - grp * 4)
                wA, wAk = load_w(swa_qkv[:, :, base + grp * 512:base + grp * 512 + nt * 128], KD, nt * 128)
                wB, wBk = load_w(swa_qkp[:, :, base + grp * 512:base + grp * 512 + nt * 128], KD, nt * 128)
                for ci, (c0, n) in enumerate(CH):
                    cs, ck = load_cs(ci) if ci < 4 else (None, None)
                    for t in range(nt):
                        row0 = (grp * 4 + t) * 128
                        if ci < 4:
                            pa, pak, pb, pbk = proj_fm(hT, hkey, KD, wA, wAk, t * 128, 128, ci, wB, wBk, t * 128)
                            qk_tile(l, ci, 128, pa, pak, pb, pbk, cs, ck, dst_rot(row0, 128)[:, c0:c0 + n],
                                    plain_l[row0:row0 + 128, c0:c0 + n] if plain_l is not None else None,
                                    'qx_scr' if part == 'q' else 'kmine', 'qp_scr')
                        else:
                            pa, pak, _, _ = proj_fm(hT, hkey, KD, wA, wAk, t * 128, 128, ci)
                            dstc = qp_scr[row0:row0 + 128, c0:c0 + n] if part == 'q' else kc_scr[row0:row0 + 128, 0:n]
                            qk_tile(l, ci, 128, pa, pak, None, None, None, None, None, dstc, None,
                                    'qp_scr' if part == 'q' else 'kc_scr')
        w, wk = load_w(swa_qkv[:, :, 1280:1536], KD, 256)
        v_proj_tm(l, w, wk, 0, 256, 0)
        exchange(l)

        def mask(pt, ptk, qa, qb, mspec):
            jj, halo = mspec
            if halo is None:
                TT(pt[:, qa:qb], pt[:, qa:qb], swa_m[:, jj, qa:qb], ALU.mult, [ptk, 'swa_m'], [ptk])
            else:
                STT(pt[:, qa:qb], pt[:, qa:qb], halo_s[:, halo:halo + 1], swa_m[:, jj, qa:qb], ALU.mult, ALU.mult,
                    [ptk, 'swa_m', 'halo_s'], [ptk])

        for kvh in range(4):
            DMA('sp', kx[:64, 0:128], kall_rows(l, 0, kvh * 64, 64)[:, TL - 128:TL], 'kx', ['kall'], ['kx'])
            DMA('sp', kx[:64, 128:128 + TL], kmine_rows(l, kvh * 64, 64), 'kx', ['kmine'], ['kx'])
            DMA('sp', kx[:64, 128 + TL:256 + TL], kall_rows(l, 1, kvh * 64, 64)[:, 0:128], 'kx',
                ['kall'], ['kx'])
            DMA('sp', kcx[:64, :], kc_scr[kvh * 64:(kvh + 1) * 64, :], 'kcx', ['kc_scr'], ['kcx'])
            DMA('sp', vt[:, 0:1, :64], vall_rows(l, 0, TL - 128, 128)[:, kvh * 64:(kvh + 1) * 64].rearrange("(b p) c -> p b c", p=128),
                'vt', ['vall'], ['vt'])
            for j in range(2):
                DMA('sp', vt[:, 1 + 8 * j:9 + 8 * j, :64],
                    vmine_rows(l, j * 1024, 1024)[:, kvh * 64:(kvh + 1) * 64].rearrange("(b p) c -> p b c", p=128),
                    'vt', ['vmine'], ['vt'])
            DMA('sp', vt[:, 17:18, :64], vall_rows(l, 1, 0, 128)[:, kvh * 64:(kvh + 1) * 64].rearrange("(b p) c -> p b c", p=128),
                'vt', ['vall'], ['vt'])
            DMA('sp', vt[:, 32:34, :64], vc_scr[:, kvh * 64:(kvh + 1) * 64].rearrange("(b p) c -> p b c", p=128), 'vt',
                ['vc_scr'], ['vt'])
            for g in range(4):
                hq = kvh * 4 + g
                DMA('sp', q2[:64, :], qx_scr[hq * 64:(hq + 1) * 64, :], 'q2', ['qx_scr'], ['q2'])
                DMA('sp', q1[:64, :], qp_scr[hq * 64:(hq + 1) * 64, :], 'q1', ['qp_scr'], ['q1'])

                def xb(ci):
                    out = []
                    order = [None]
                    for jj in range(6):
                        qa = max(0, jj - 2) * 128
                        qb = min(4, jj + 1) * 128
                        blk = 4 * ci + jj
                        halo = None
                        if blk == 0:
                            halo = 0
                        elif blk == 17:
                            halo = 1
                        out.append((blk * 128, blk, (qa, qb), (jj, halo)))
                    return out
                attn_unit(scale, [(kx, q2, 64, 'kx', 'q2')], [(kcx, q1, 64, 'kcx', 'q1')], xb,
                          lambda vb: vt[:, vb, :64], 64, None, None, None,
                          plain_final(oh, 'oh', 64, den_add=small[:64, 24 + hq:25 + hq]), mask=mask)
                wo_apply(l, swa_wo, oh, 'oh', hq, 64)

    na_E = sb("na_E", [128, 22 * 64], BF16, at=R2 + 23552)
    na_rv_s = sb("na_rv_s", [128, 256], BF16, at=R2 + 26368)
    na_rv_f = tmp_lam

    def na_mixer(l):
        scale = 64 ** -0.5
        DMA('sp', na_rv_f[:, :], na_rv[:, :], 'misc', [], ['na_rv_f'])
        CP(na_rv_s[:, :], na_rv_f[:, :], ['na_rv_f'], ['na_rv'])
        for part, dstl, dstc in (('q', (lambda a, b: qp_scr[a:a + b, :]), qp_scr), ('k', (lambda a, b: kmine_rows(l, a, b)), kc_scr)):
            base = 0 if part == 'q' else 1024
            for grp in range(2):
                wA, wAk = load_w(na_qkv[:, :, base + grp * 512:base + (grp + 1) * 512], KD, 512)
                for ci, (c0, n) in enumerate(CH):
                    for t in range(4):
                        row0 = (grp * 4 + t) * 128
                        pa, pak, _, _ = proj_fm(hT, hkey, KD, wA, wAk, t * 128, 128, ci)
                        if part == 'q' or ci < 4:
                            qk_tile(l, ci, 128, pa, pak, None, None, None, None, None,
                                    dstl(row0, 128)[:, c0:c0 + n], None, 'qp_scr' if part == 'q' else 'kmine')
                        else:
                            qk_tile(l, ci, 128, pa, pak, None, None, None, None, None, kc_scr[row0:row0 + 128, 0:n],
                                    None, 'kc_scr')
        for half in range(2):
            w, wk = load_w(na_qkv[:, :, 2048 + half * 512:2048 + (half + 1) * 512], KD, 512)
            v_proj_tm(l, w, wk, 0, 512, half * 512)
        exchange(l)

        for h in range(16):
            for piece in range(3):
                a0 = piece * 512
                a1 = min(1408, a0 + 512)
                t, tk = tmp_p.get()
                DMA('sp', t[:, :a1 - a0], na_rpbE[h, :, a0:a1], 'tmpd%d' % tk[1], [], [tk])
                ACT(na_E[:, a0:a1], t[:, :a1 - a0], AF.Exp, [tk], ['na_E'])
            DMA('sp', kx[:64, 0:256], kall_rows(l, 0, h * 64, 64)[:, TL - 256:TL], 'kx', ['kall'], ['kx'])
            DMA('sp', kx[:64, 256:256 + TL], kmine_rows(l, h * 64, 64), 'kx', ['kmine'], ['kx'])
            DMA('sp', kx[:64, 256 + TL:512 + TL], kall_rows(l, 1, h * 64, 64)[:, 0:256], 'kx',
                ['kall'], ['kx'])
            DMA('sp', kcx[:64, :], kc_scr[h * 64:(h + 1) * 64, :], 'kcx', ['kc_scr'], ['kcx'])
            DMA('sp', vt[:, 0:2, :64], vall_rows(l, 0, TL - 256, 256)[:, h * 64:(h + 1) * 64].rearrange("(b p) c -> p b c", p=128),
                'vt', ['vall'], ['vt'])
            for j in range(2):
                DMA('sp', vt[:, 2 + 8 * j:10 + 8 * j, :64],
                    vmine_rows(l, j * 1024, 1024)[:, h * 64:(h + 1) * 64].rearrange("(b p) c -> p b c", p=128),
                    'vt', ['vmine'], ['vt'])
            DMA('sp', vt[:, 18:20, :64], vall_rows(l, 1, 0, 256)[:, h * 64:(h + 1) * 64].rearrange("(b p) c -> p b c", p=128),
                'vt', ['vall'], ['vt'])
            DMA('sp', vt[:, 32:34, :64], vc_scr[:, h * 64:(h + 1) * 64].rearrange("(b p) c -> p b c", p=128), 'vt',
                ['vc_scr'], ['vt'])
            DMA('sp', q1[:64, :], qp_scr[h * 64:(h + 1) * 64, :], 'q1', ['qp_scr'], ['q1'])

            def xb(ci):
                return [((4 * ci + jb) * 128, 4 * ci + jb, (0, 512), (ci, jb)) for jb in range(8)]

            def mask(pt, ptk, qa, qb, mspec):
                ci, jb = mspec
                j0 = 14 - 2 * jb
                TT(pt[:, 0:512], pt[:, 0:512], na_E[:, j0 * 64:(j0 + 8) * 64], ALU.mult, [ptk, 'na_E'], [ptk])
                i0 = (ci * 8 + jb) * 8
                TT(pt[:, 0:512].rearrange("p (a b) -> p a b", b=64), pt[:, 0:512].rearrange("p (a b) -> p a b", b=64),
                   na_rv_s[:, i0:i0 + 8].unsqueeze(2).broadcast_to([128, 8, 64]), ALU.mult, [ptk, 'na_rv'], [ptk])
            attn_unit(scale, [(kx, q1, 64, 'kx', 'q1')], [(kcx, q1, 64, 'kcx', 'q1')], xb,
                      lambda vb: vt[:, vb, :64], 64, None, None, None, plain_final(oh, 'oh', 64), mask=mask)
            wo_apply(l, na_wo, oh, 'oh', h, 64)

    for l in layers:
        mod_phase(l)
        norm_phase(0)
        ffn_phase(l, 0, 0)
        if do_mixer:
            norm_phase(1)
            if l == 0:
                fence(['dve'], R2_KEYS)
                mla_mixer(l)
                fence(['sp', 'act', 'dve'], R2_KEYS)
            elif l == 1:
                swa_mixer(l)
            elif l == 2:
                na_mixer(l)
            else:
                diff_mixer(l, 0.8 - 0.6 * float(np.exp(-0.3 * l)))
            fence(['act'], ATT_KEYS + HT_KEYS)
        norm_phase(2)
        ffn_phase(l, 1, 2)

    for ci, (c0, n) in enumerate(CH[:4]):
        ss, ssk = ps_p.get()
        for k in range(KD):
            sq, sqk = tmp_p.get()
            ACT(sq[:, :n], xT[:, k, c0:c0 + n], AF.Square, [('xT', ci, k)], [sqk])
            MM(ss[:, :n], ones_f[:, :], sq[:, :n], k == 0, k == KD - 1, [sqk, 'ones_f'], [ssk])
        rstd, rk = rs_p.get()
        ACT(rstd[:, :n], ss[:, :n], AF.Sqrt, [ssk], [rk], scale=1.0 / D, bias=eps_t[:, 0:1])
        RCP(rstd[:, :n], rstd[:, :n], [rk], [rk])
        for k in range(KD):
            t, tk = tmp_p.get()
            STT(t[:, :n], xT[:, k, c0:c0 + n], fng_s[:, k:k + 1], rstd[:, :n], ALU.mult, ALU.mult,
                [('xT', ci, k), rk, 'fng'], [tk])
            DMA('sp', outT[:, k, c0:c0 + n], t[:, :n], 'out', [tk], ['out'])
    if DEBUG and 3 in layers:
        dk1 = nc.dram_tensor("dbg_kall", [1024, TL], BF16, kind="ExternalOutput")
        dk2 = nc.dram_tensor("dbg_kmine", [512, TL], BF16, kind="ExternalOutput")
        DMA('sp', dk1[:, :], kparts[3][0][3][:, :], 'out', ['kall'], ['out'])
        DMA('sp', dk2[:, :], kparts[3][0][2][:, :], 'out', ['kmine'], ['out'])
    P.add('sp', None, ['out'], [])

    P.emit(nc, es)
    es.close()
    return nc


def _kmaj(w):
    K, N = w.shape
    return np.ascontiguousarray(w.reshape(K // 128, 128, N).transpose(1, 0, 2))


def _swap_halves(w, col0, ncols, hd=64):
    idx = np.arange(col0, col0 + ncols).reshape(-1, 2, hd // 2)[:, ::-1, :].reshape(-1)
    return w[:, idx]


def _rope_tables(pos0, n):
    t = np.arange(pos0, pos0 + n)
    row = (t // 64).astype(np.float32)
    col = (t % 64).astype(np.float32)
    nf = 16
    inv = np.exp(-np.log(np.float32(10000.0)) * np.arange(nf, dtype=np.float32) / nf).astype(np.float32)
    ang = np.concatenate([row[:, None] * inv, col[:, None] * inv], axis=-1).astype(np.float32)
    c = np.cos(ang).T.astype(np.float32)
    s = np.sin(ang).T.astype(np.float32)
    cos4 = np.concatenate([c, c, c, c], 0)
    sin4 = np.concatenate([-s, s, -s, s], 0)
    return np.ascontiguousarray(cos4), np.ascontiguousarray(sin4)


def _swa_masks():
    import ml_dtypes
    m = np.zeros((128, 6, 512), np.float32)
    kj = np.arange(128)[:, None]
    q = np.arange(512)[None, :]
    for jj in range(6):
        kpos = (jj - 1) * 128 + kj
        m[:, jj, :] = (np.abs(kpos - q) <= 128)
    return m.astype(ml_dtypes.bfloat16)


def _na_tables(rpb, half):
    H = rpb.shape[0]
    E = np.full((H, 128, 22, 64), NEGFILL, np.float32)
    kc = np.arange(64)[:, None]
    qc = np.arange(64)[None, :]
    cs = np.clip(qc - 8, 0, 48)
    colvalid = (kc >= cs) & (kc < cs + 16)
    coff = np.clip(kc - qc + 15, 0, 30)
    for hf in range(2):
        for jp in range(22):
            e = jp - 3 - hf
            if e < 0 or e > 14:
                continue
            g = rpb[:, 14 - e][:, coff]
            g = np.where(colvalid[None], g, np.float32(NEGFILL))
            E[:, hf * 64:(hf + 1) * 64, jp, :] = g
    rv = np.zeros((128, 4, 8, 8), np.float32)
    for ci in range(4):
        for jb in range(8):
            for hf in range(2):
                krl = 8 * ci - 4 + 2 * jb + hf
                kr = 32 * half + krl
                for qi in range(8):
                    qr = 32 * half + 8 * ci + qi
                    rs = min(max(qr - 4, 0), 56)
                    ok = (rs <= kr < rs + 8) and (0 <= kr < 64)
                    if krl < 0 and half == 0:
                        ok = False
                    if krl >= 32 and half == 1:
                        ok = False
                    rv[hf * 64:(hf + 1) * 64, ci, jb, qi] = 1.0 if ok else 0.0
    return E.reshape(H, 128, 22 * 64), rv.reshape(128, 256)


def prepare_inputs(inp, ncores=8):
    f = lambda a: np.ascontiguousarray(np.asarray(a, dtype=np.float32))
    x, c, ctx, c_ctx = f(inp['x']), f(inp['c']), f(inp['ctx']), f(inp['c_ctx'])
    shared = {}
    shared['modw'] = np.ascontiguousarray(f(inp['mod_w']).reshape(4, 8, 128, 9216).transpose(0, 2, 1, 3))
    shared['modb'] = np.ascontiguousarray(f(inp['mod_b']).reshape(4, 72, 128).transpose(2, 0, 1))
    shared['ng_in'] = np.ascontiguousarray(f(inp['norm_g']).reshape(4, 3, 8, 128).transpose(3, 0, 1, 2))
    shared['fng_in'] = np.ascontiguousarray(f(inp['final_norm_g']).reshape(8, 128).T)
    shared['win'] = np.ascontiguousarray(f(inp['ffn_w_in']).reshape(4, 2, 8, 128, 2 * DFF).transpose(0, 1, 3, 2, 4))
    shared['wout'] = np.ascontiguousarray(f(inp['ffn_w_out']).reshape(4, 2, 22, 128, D).transpose(0, 1, 3, 2, 4))
    wd = f(inp['mla_w_down'])[0]
    shared['mla_down'] = _kmaj(wd)
    shared['mla_downp'] = _kmaj(_swap_halves(wd, 768, 64))
    shared['mla_qn'] = np.ascontiguousarray(f(inp['mla_q_norm'])[0].reshape(4, 128).T)
    shared['mla_kvn'] = np.ascontiguousarray(f(inp['mla_kv_norm'])[0].reshape(2, 128).T)
    uq = f(inp['mla_w_uq'])[0]
    shared['mla_uq'] = _kmaj(uq)
    ropecols = np.concatenate([np.arange(h * 192 + 128, h * 192 + 192) for h in range(8)])
    uqr = uq[:, ropecols]
    shared['mla_uqp'] = _kmaj(_swap_halves(uqr, 0, 512))
    shared['mla_ukv'] = _kmaj(f(inp['mla_w_ukv'])[0])
    shared['mla_wo'] = _kmaj(f(inp['mla_w_o'])[0])
    sq = f(inp['swa_w_qkv'])[0]
    shared['swa_qkv'] = _kmaj(sq)
    shared['swa_qkp'] = _kmaj(_swap_halves(sq, 0, 1280))
    shared['swa_sink'] = np.ascontiguousarray(np.broadcast_to(f(inp['swa_sink'])[0][None, :], (128, 16)))
    swo = f(inp['swa_w_o'])[0]
    t = np.zeros((128, 16, D), np.float32)
    t[:64] = swo.reshape(16, 64, D).transpose(1, 0, 2)
    shared['swa_wo'] = t
    shared['swa_mask'] = _swa_masks()
    shared['na_qkv'] = _kmaj(f(inp['na_w_qkv'])[0])
    nwo = f(inp['na_w_o'])[0]
    t2 = np.zeros((128, 16, D), np.float32)
    t2[:64] = nwo.reshape(16, 64, D).transpose(1, 0, 2)
    shared['na_wo'] = t2
    dq = f(inp['diff_w_qkv'])[0]
    shared['df_qkv'] = _kmaj(dq)
    shared['df_qkp'] = _kmaj(_swap_halves(dq, 0, 2048))
    shared['df_lam'] = np.ascontiguousarray(np.broadcast_to(f(inp['diff_lambda'])[0].reshape(1, 256), (128, 256)))
    shared['df_ng'] = np.ascontiguousarray(f(inp['diff_norm_g'])[0].reshape(128, 1))
    shared['df_wo'] = _kmaj(f(inp['diff_w_o'])[0])
    rpb = f(inp['na_rpb'])[0]
    maps = []
    for core in range(ncores):
        b, h = core // 2, core % 2
        m = dict(shared)
        tok = np.concatenate([x[b, h * TL:(h + 1) * TL], ctx[b]], 0)
        m['xT_in'] = np.ascontiguousarray(tok.T.reshape(8, 128, T).transpose(1, 0, 2))
        cc = np.stack([c[b], c_ctx], -1)
        m['cc_in'] = np.ascontiguousarray(cc.reshape(8, 128, 2).transpose(1, 0, 2))
        m['cos4'], m['sin4'] = _rope_tables(h * TL, TL)
        hv = np.zeros((128, 2), np.float32)
        hv[:, 0] = 1.0 if h == 1 else 0.0
        hv[:, 1] = 1.0 if h == 0 else 0.0
        m['halo_v'] = hv
        E, rv = _na_tables(rpb, h)
        m['na_rpbE'] = E
        m['na_rv'] = rv
        maps.append(m)
    return maps


_NC_CACHE = {}


def kernel(**inputs):
    maps = prepare_inputs(inputs)
    if 'nc' not in _NC_CACHE:
        _NC_CACHE['nc'] = build()
    nc = _NC_CACHE['nc']
    names = set(_INPUT_NAMES)
    in_maps = [{k: v for k, v in m.items() if k in names} for m in maps]
    res = run_bass_kernel_spmd(nc, in_maps, core_ids=list(range(8)))
    out = np.zeros((4, 4096, D), np.float32)
    for core in range(8):
        b, h = core // 2, core % 2
        oT = np.asarray(res.results[core]["outT"]).reshape(128, 8, TL)
        out[b, h * TL:(h + 1) * TL] = oT.transpose(2, 1, 0).reshape(TL, D)
    return out


_INPUT_NAMES = ["xT_in", "cc_in", "modw", "modb", "ng_in", "fng_in", "win", "wout", "cos4", "sin4", "mla_down",
                "mla_downp", "mla_qn", "mla_kvn", "mla_uq", "mla_uqp", "mla_ukv", "mla_wo", "swa_qkv", "swa_qkp",
                "swa_sink", "swa_wo", "swa_mask", "halo_v", "na_qkv", "na_rpbE", "na_rv", "na_wo", "df_qkv", "df_qkp",
                "df_lam", "df_ng", "df_wo"]
```

```python
import numpy as np
from contextlib import ExitStack
import concourse.bass as bass
import concourse.mybir as mybir
from concourse.bass_utils import run_bass_kernel_spmd

F32 = mybir.dt.float32
BF16 = mybir.dt.bfloat16
AF = mybir.ActivationFunctionType
ALU = mybir.AluOpType

D = 1024
KD = 8
TL = 2048
TCX = 256
T = TL + TCX
DFF = 2816
CH = [(0, 512), (512, 512), (1024, 512), (1536, 512), (2048, 256)]
EPS = 1e-6
NEGFILL = -30000.0
ENGS = ['sp', 'act', 'pe', 'dve', 'pool']
DEBUG = False


class Op:
    __slots__ = ('eng', 'fn', 'r', 'w', 'dma', 'inc', 'waits', 'signal', 'val', 'fdeps')

    def __init__(self, eng, fn, r, w, dma, inc):
        self.eng, self.fn, self.r, self.w, self.dma, self.inc = eng, fn, r, w, dma, inc
        self.waits = []
        self.signal = False
        self.val = 0
        self.fdeps = ()


class Prog:
    def __init__(self):
        self.ops = []

    def add(self, eng, fn, r=(), w=(), dma=None, inc=16):
        self.ops.append(Op(eng, fn, tuple(r), tuple(w), dma, inc))

    def analyse(self):
        last_w = {}
        rd_eng = {}
        rd_dma = {}
        waited = {e: {} for e in ENGS}
        dma_cnt = {}
        ops = self.ops
        for j, op in enumerate(ops):
            deps = set()
            for b in op.r:
                i = last_w.get(b)
                if i is not None:
                    deps.add(i)
            for b in op.w:
                i = last_w.get(b)
                if i is not None:
                    deps.add(i)
                for i in rd_eng.get(b, {}).values():
                    deps.add(i)
                for i in rd_dma.get(b, ()):
                    deps.add(i)
            E = op.eng
            wt = waited[E]
            more = set()
            for i in deps:
                if ops[i].fn is None:
                    more.update(ops[i].fdeps)
            deps = set(i for i in deps if ops[i].fn is not None) | more
            if op.fn is None:
                op.fdeps = tuple(deps)
            for i in sorted(deps):
                p = ops[i]
                if p.dma is None:
                    if p.eng == E and E in ('pe', 'sp', 'pool'):
                        continue
                    if wt.get(p.eng, -1) >= i:
                        continue
                    wt[p.eng] = i
                    p.signal = True
                    op.waits.append(('c', i))
                else:
                    if wt.get(p.dma, 0) >= p.val:
                        continue
                    wt[p.dma] = p.val
                    op.waits.append(('d', p.dma, p.val))
            if op.dma is not None:
                dma_cnt[op.dma] = dma_cnt.get(op.dma, 0) + op.inc
                op.val = dma_cnt[op.dma]
            for b in op.r:
                if op.dma is not None:
                    rd_dma.setdefault(b, []).append(j)
                else:
                    rd_eng.setdefault(b, {})[E] = j
            for b in op.w:
                last_w[b] = j
                rd_eng[b] = {}
                rd_dma[b] = []
        cnt = {e: 0 for e in ENGS}
        for op in ops:
            if op.dma is None and op.signal:
                cnt[op.eng] += 1
                op.val = cnt[op.eng]
        self.dma_keys = sorted(dma_cnt.keys())
        return cnt

    def emit(self, nc, es):
        self.analyse()
        esem = {e: es.enter_context(nc.semaphore('s_' + e)) for e in ENGS}
        dsem = {k: es.enter_context(nc.semaphore('d_' + k)) for k in self.dma_keys}
        ops = self.ops
        per = {e: [op for op in ops if op.eng == e] for e in ENGS}

        def replay(ename, eng):
            for op in per[ename]:
                for w in op.waits:
                    if w[0] == 'c':
                        p = ops[w[1]]
                        eng.wait_ge(esem[p.eng], p.val)
                    else:
                        eng.wait_ge(dsem[w[1]], w[2])
                if op.fn is None:
                    continue
                ins = op.fn(eng)
                if op.dma is not None:
                    ins.then_inc(dsem[op.dma], op.inc)
                elif op.signal:
                    ins.then_inc(esem[op.eng], 1)

        with nc.Block() as block:
            @block.sync
            def _(e):
                replay('sp', e)

            @block.scalar
            def _(e):
                replay('act', e)

            @block.tensor
            def _(e):
                replay('pe', e)

            @block.vector
            def _(e):
                replay('dve', e)

            @block.gpsimd
            def _(e):
                replay('pool', e)


class Rot:
    def __init__(self, name, tiles):
        self.name, self.tiles, self.i = name, tiles, 0

    def get(self):
        i = self.i
        self.i = (i + 1) % len(self.tiles)
        return self.tiles[i], (self.name, i)


def build(layers=(0, 1, 2, 3), pair_groups=((0, 1), (2, 3), (4, 5), (6, 7)), do_mixer=True):
    nc = bass.Bass("TRN2", target_bir_lowering=False)
    P = Prog()
    es = ExitStack()

    def din(name, shape, dt=F32):
        return nc.dram_tensor(name, list(shape), dt, kind="ExternalInput")

    def dscr(name, shape, dt=BF16):
        return nc.dram_tensor(name, list(shape), dt)

    xT_in = din("xT_in", [128, KD, T])
    cc_in = din("cc_in", [128, KD, 2])
    modw = din("modw", [4, 128, KD, 9216])
    modb = din("modb", [128, 4, 72])
    ng_in = din("ng_in", [128, 4, 3, KD])
    fng_in = din("fng_in", [128, KD])
    win_d = din("win", [4, 2, 128, KD, 2 * DFF])
    wout_d = din("wout", [4, 2, 128, 22, D])
    cos_d = din("cos4", [128, TL])
    sin_d = din("sin4", [128, TL])
    mla_down = din("mla_down", [128, KD, 832])
    mla_downp = din("mla_downp", [128, KD, 64])
    mla_qn = din("mla_qn", [128, 4])
    mla_kvn = din("mla_kvn", [128, 2])
    mla_uq = din("mla_uq", [128, 4, 1536])
    mla_uqp = din("mla_uqp", [128, 4, 512])
    mla_ukv = din("mla_ukv", [128, 2, 2048])
    mla_wo = din("mla_wo", [128, 8, D])
    swa_qkv = din("swa_qkv", [128, KD, 1536])
    swa_qkp = din("swa_qkp", [128, KD, 1280])
    swa_sink = din("swa_sink", [128, 16])
    swa_wo = din("swa_wo", [128, 16, D])
    swa_mask = din("swa_mask", [128, 6, 512], BF16)
    halo_v = din("halo_v", [128, 2])
    na_qkv = din("na_qkv", [128, KD, 3072])
    na_rpbE = din("na_rpbE", [16, 128, 22 * 64])
    na_rv = din("na_rv", [128, 4 * 8 * 8])
    na_wo = din("na_wo", [128, 16, D])
    df_qkv = din("df_qkv", [128, KD, 3072])
    df_qkp = din("df_qkp", [128, KD, 2048])
    df_lam = din("df_lam", [128, 256])
    df_ng = din("df_ng", [128, 1])
    df_wo = din("df_wo", [128, 8, D])

    outT = nc.dram_tensor("outT", [128, KD, TL], F32, kind="ExternalOutput")

    qx_scr = dscr("qx_scr", [2048, TL])
    qp_scr = dscr("qp_scr", [2048, T])
    kc_scr = dscr("kc_scr", [1152, TCX])
    vc_scr = dscr("vc_scr", [TCX, 1024])
    KR = {0: 1088, 1: 256, 2: 1024, 3: 1024}
    VC = {0: 1024, 1: 256, 2: 1024, 3: 1024}
    bar_in = dscr("bar_in", [128, 128])
    bar_out = dscr("bar_out", [256, 128])
    kparts = {}
    vparts = {}
    for l in layers:
        kparts[l] = []
        r0 = 0
        while r0 < KR[l]:
            nr = min(512, KR[l] - r0)
            kparts[l].append((r0, nr, dscr("kmine%d_%d" % (l, r0), [nr, TL]), dscr("kall%d_%d" % (l, r0), [2 * nr, TL])))
            r0 += nr
        vparts[l] = [(dscr("vmine%d_%d" % (l, j), [1024, VC[l]]), dscr("vall%d_%d" % (l, j), [2048, VC[l]]))
                     for j in range(2)]

    def kmine_rows(l, row0, nrows):
        for (r0, nr, mine, all_) in kparts[l]:
            if r0 <= row0 and row0 + nrows <= r0 + nr:
                return mine[row0 - r0:row0 - r0 + nrows, :]
        raise AssertionError((l, row0, nrows))

    def kall_rows(l, r, row0, nrows):
        for (r0, nr, mine, all_) in kparts[l]:
            if r0 <= row0 and row0 + nrows <= r0 + nr:
                return all_[r * nr + row0 - r0:r * nr + row0 - r0 + nrows, :]
        raise AssertionError((l, row0, nrows))

    def vmine_rows(l, t0, nt):
        j = t0 // 1024
        assert t0 + nt <= (j + 1) * 1024
        return vparts[l][j][0][t0 - j * 1024:t0 - j * 1024 + nt, :]

    def vall_rows(l, r, t0, nt):
        j = t0 // 1024
        assert t0 + nt <= (j + 1) * 1024
        return vparts[l][j][1][r * 1024 + t0 - j * 1024:r * 1024 + t0 - j * 1024 + nt, :]

    ARENA = 212000
    arena = es.enter_context(nc.sbuf_tensor("arena", [128, ARENA // 4], F32))
    a0 = nc.lookup_mloc(arena).addr
    cur = [0]

    def sb(name, shape, dt=F32, at=None):
        nb = int(np.prod(shape[1:])) * (4 if dt == F32 else 2)
        nb = (nb + 31) // 32 * 32
        if at is None:
            off = cur[0]
            cur[0] += nb
            assert cur[0] <= ARENA, (name, cur[0])
        else:
            off = at
        return nc.alloc_sbuf_tensor_at(name, list(shape), dt, offset=a0 + off)

    def region(nbytes):
        off = cur[0]
        cur[0] += nbytes
        assert cur[0] <= ARENA, ('region', cur[0])
        return off

    xT = sb("xT", [128, KD, T])
    R1 = region(36864)
    R2 = region(27648)
    hT = sb("hT", [128, KD, T], BF16, at=R1)
    win_p = Rot("win", [sb("win%d" % i, [128, KD, 512], BF16) for i in range(3)])
    wout_p = Rot("wout", [sb("wout%d" % i, [128, 2, D], BF16) for i in range(2)])
    modw_p = Rot("modw", [sb("modw%d" % i, [128, KD, 128], F32, at=R2 + i * 4096) for i in range(2)])
    tmp_p = Rot("tmp", [sb("tmp%d" % i, [128, 512]) for i in range(4)])
    rs_p = Rot("rs", [sb("rs%d" % i, [128, 512]) for i in range(2)])
    cs_p = Rot("cs", [sb("cs%d" % i, [128, 2, 512]) for i in range(1)])
    act_p = Rot("act", [sb("act%d" % i, [128, 2, 512], BF16) for i in range(2)])
    pt_p = Rot("pt", [sb("pt%d" % i, [128, 512], BF16) for i in range(3)])
    stg_p = Rot("stg", [sb("stg%d" % i, [128, 512], BF16) for i in range(3)])
    condT = sb("condT", [128, KD, 2])
    modT = sb("modT", [128, 72, 2])
    modb_s = sb("modb_s", [128, 4, 72])
    ng_s = sb("ng_s", [128, 4, 3, KD])
    fng_s = sb("fng_s", [128, KD])
    gs_s = sb("gs_s", [128, 3, KD, 2])
    gt_s = sb("gt_s", [128, 3, KD, 2])
    ones_f = sb("ones_f", [128, 128])
    ones_b = sb("ones_b", [128, 128], BF16)
    small = sb("small", [128, 64])
    eps_t = sb("eps_t", [128, 8])
    q0 = sb("q0", [128, T], BF16, at=R1)
    q1 = sb("q1", [128, T], BF16, at=R1 + 4608)
    kx = sb("kx", [128, 2 * TL], BF16, at=R1 + 9216)
    krx = sb("krx", [128, 2 * TL], BF16, at=R1 + 17408)
    kcx = sb("kcx", [128, TCX], BF16, at=R1 + 25600)
    krc = sb("krc", [128, TCX], BF16, at=R1 + 26112)
    vt = sb("vt", [128, 34, 128], BF16, at=R1 + 26624)
    assert 26624 + 8704 <= 36864
    q2 = sb("q2", [128, TL], BF16)
    oh = sb("oh", [128, T], BF16)
    ATT_KEYS = ['q0', 'q1', 'kx', 'krx', 'kcx', 'krc', 'vt']
    HT_KEYS = [('hT', ci) for ci in range(5)]
    R2_KEYS = [('modw', 0), ('modw', 1), 'o0f', 'swa_m', 'na_E', 'na_rv'] + [('cq', ci) for ci in range(5)] + \
              [('ckv', ci) for ci in range(5)]

    def fence(engs, keys):
        for e_ in engs:
            P.add(e_, None, [], keys)

    ps_p = Rot("ps", [es.enter_context(nc.psum_tensor("ps%d" % i, [128, 512], F32)) for i in range(4)])
    acc_p = Rot("acc", [es.enter_context(nc.psum_tensor("acc%d" % i, [128, 512], F32)) for i in range(3)])
    pmod = es.enter_context(nc.psum_tensor("pmod", [128, 72, 2], F32))

    def ACT(out, in_, func, r, w, **kw):
        P.add('act', lambda e: e.activation(out=out, in_=in_, func=func, **kw), r, w)

    def MM(out, lhsT, rhs, start, stop, r, w):
        P.add('pe', lambda e: e.matmul(out, lhsT, rhs, start=start, stop=stop), r, w)

    def TT(out, in0, in1, op, r, w):
        P.add('dve', lambda e: e.tensor_tensor(out=out, in0=in0, in1=in1, op=op), r, w)

    def TS(out, in0, s1, s2, op0, op1, r, w):
        if op1 is None:
            P.add('dve', lambda e: e.tensor_scalar(out=out, in0=in0, scalar1=s1, scalar2=0.0, op0=op0, op1=ALU.add), r, w)
        else:
            P.add('dve', lambda e: e.tensor_scalar(out=out, in0=in0, scalar1=s1, scalar2=s2, op0=op0, op1=op1), r, w)

    def STT(out, in0, scalar, in1, op0, op1, r, w):
        P.add('dve', lambda e: e.scalar_tensor_tensor(out=out, in0=in0, scalar=scalar, in1=in1, op0=op0, op1=op1), r, w)

    def CP(out, in_, r, w):
        P.add('dve', lambda e: e.tensor_copy(out=out, in_=in_), r, w)

    def RCP(out, in_, r, w):
        P.add('dve', lambda e: e.reciprocal(out=out, in_=in_), r, w)

    def DMA(q, out, in_, key, r, w):
        P.add(q, lambda e: e.dma_start(out=out, in_=in_), r, w, dma=key)

    def MEMSET(ap, v, w):
        P.add('dve', lambda e: e.memset(ap, v), (), w)

    def xkeys(ci):
        return [('xT', ci, k) for k in range(KD)]

    MEMSET(ones_f[:, :], 1.0, ['ones_f'])
    MEMSET(ones_b[:, :], 1.0, ['ones_b'])
    MEMSET(eps_t[:, :], EPS, ['eps_t'])
    for ci, (c0, n) in enumerate(CH):
        DMA('sp', xT[:, :, c0:c0 + n], xT_in[:, :, c0:c0 + n], 'xin%d' % ci, [], xkeys(ci))
    DMA('sp', condT[:, :, :], cc_in[:, :, :], 'misc_c', [], ['condT'])
    DMA('sp', modb_s[:, :, :], modb[:, :, :], 'misc', [], ['modb'])
    DMA('sp', ng_s[:, :, :, :], ng_in[:, :, :, :], 'misc', [], ['ng'])
    DMA('sp', fng_s[:, :], fng_in[:, :], 'misc', [], ['fng'])
    ACT(condT[:, :, :], condT[:, :, :], AF.Silu, ['condT'], ['condT'])

    def mod_phase(l):
        for j in range(72):
            mw, mk = modw_p.get()
            DMA('sp', mw[:, :, :], modw[l, :, :, j * 128:(j + 1) * 128], 'modw%d' % mk[1], [], [mk])
            for k in range(KD):
                MM(pmod[:, j, :], mw[:, k, :], condT[:, k, :], k == 0, k == KD - 1, [mk, 'condT'], ['pmod'])
        for s in range(2):
            TT(modT[:, :, s], pmod[:, :, s], modb_s[:, l, :], ALU.add, ['pmod', 'modb'], ['modT'])
        for i in range(3):
            for s in range(2):
                STT(gs_s[:, i, :, s], modT[:, (3 * i + 1) * 8:(3 * i + 2) * 8, s], 1.0, ng_s[:, l, i, :],
                    ALU.add, ALU.mult, ['modT', 'ng'], ['gs'])
                TS(gt_s[:, i, :, s], modT[:, (3 * i + 2) * 8:(3 * i + 3) * 8, s], 0.5 if i != 1 else 1.0, None,
                   ALU.mult, None, ['modT'], ['gt'])

    def norm_phase(sub):
        for ci, (c0, n) in enumerate(CH):
            s = 1 if ci == 4 else 0
            ss, ssk = ps_p.get()
            for k in range(KD):
                sq, sqk = tmp_p.get()
                ACT(sq[:, :n], xT[:, k, c0:c0 + n], AF.Square, [('xT', ci, k)], [sqk])
                MM(ss[:, :n], ones_f[:, :], sq[:, :n], k == 0, k == KD - 1, [sqk, 'ones_f'], [ssk])
            rstd, rk = rs_p.get()
            ACT(rstd[:, :n], ss[:, :n], AF.Sqrt, [ssk], [rk], scale=1.0 / D, bias=eps_t[:, 0:1])
            RCP(rstd[:, :n], rstd[:, :n], [rk], [rk])
            for k in range(KD):
                t, tk = tmp_p.get()
                TT(t[:, :n], xT[:, k, c0:c0 + n], rstd[:, :n], ALU.mult, [('xT', ci, k), rk], [tk])
                ACT(hT[:, k, c0:c0 + n], t[:, :n], AF.Identity, [tk, 'gs', 'modT'], [('hT', ci)],
                    scale=gs_s[:, sub, k, s:s + 1], bias=modT[:, 3 * sub * 8 + k, s:s + 1])

    def ffn_phase(l, f, sub):
        for g in range(11):
            win, wk = win_p.get()
            wo_, wok = wout_p.get()
            DMA('pool', win[:, :, 0:256], win_d[l, f, :, :, g * 256:(g + 1) * 256], 'win%d' % wk[1], [], [wk])
            DMA('pool', win[:, :, 256:512], win_d[l, f, :, :, DFF + g * 256:DFF + (g + 1) * 256], 'win%d' % wk[1], [], [wk])
            DMA('pool', wo_[:, :, :], wout_d[l, f, :, 2 * g:2 * g + 2, :], 'wout%d' % wok[1], [], [wok])
            def stage_a(ci):
                c0, n = CH[ci]
                a, ak = act_p.get()
                for fb in range(2):
                    pg, pgk = ps_p.get()
                    pu, puk = ps_p.get()
                    for k in range(KD):
                        MM(pg[:, :n], win[:, k, fb * 128:(fb + 1) * 128], hT[:, k, c0:c0 + n], k == 0, k == KD - 1,
                           [wk, ('hT', ci)], [pgk])
                    for k in range(KD):
                        MM(pu[:, :n], win[:, k, 256 + fb * 128:256 + (fb + 1) * 128], hT[:, k, c0:c0 + n], k == 0,
                           k == KD - 1, [wk, ('hT', ci)], [puk])
                    sg, sgk = tmp_p.get()
                    ACT(sg[:, :n], pg[:, :n], AF.Silu, [pgk], [sgk])
                    TT(a[:, fb, :n], sg[:, :n], pu[:, :n], ALU.mult, [sgk, puk], [ak])
                return a, ak

            def stage_b(ci, a, ak):
                c0, n = CH[ci]
                s = 1 if ci == 4 else 0
                for dk in range(KD):
                    po, pok = acc_p.get()
                    for fb in range(2):
                        MM(po[:, :n], wo_[:, fb, dk * 128:(dk + 1) * 128], a[:, fb, :n], fb == 0, fb == 1, [wok, ak], [pok])
                    STT(xT[:, dk, c0:c0 + n], po[:, :n], gt_s[:, sub, dk, s:s + 1], xT[:, dk, c0:c0 + n],
                        ALU.mult, ALU.add, [pok, 'gt', ('xT', ci, dk)], [('xT', ci, dk)])

            prev = stage_a(0)
            for ci in range(1, len(CH)):
                cur_ = stage_a(ci)
                stage_b(ci - 1, *prev)
                prev = cur_
            stage_b(len(CH) - 1, *prev)

    def load_w(src_ap, K, ncols):
        w, wk = win_p.get()
        DMA('pool', w[:, 0:K, 0:ncols], src_ap, 'win%d' % wk[1], [], [wk])
        return w, wk

    def load_cs(ci):
        c0, n = CH[ci]
        cs, ck = cs_p.get()
        DMA('sp', cs[:, 0, :n], cos_d[:, c0:c0 + n], 'cs', [], [ck])
        DMA('sp', cs[:, 1, :n], sin_d[:, c0:c0 + n], 'cs', [], [ck])
        return cs, ck

    def proj_fm(src, srckey, K, wA, wAk, colA, M, ci, wB=None, wBk=None, colB=0):
        c0, n = CH[ci]
        pa, pak = ps_p.get()
        for k in range(K):
            MM(pa[:M, :n], wA[:, k, colA:colA + M], src[:, k, c0:c0 + n], k == 0, k == K - 1, [wAk, srckey(ci)], [pak])
        if wB is None:
            return pa, pak, None, None
        pb, pbk = ps_p.get()
        for k in range(K):
            MM(pb[:M, :n], wB[:, k, colB:colB + M], src[:, k, c0:c0 + n], k == 0, k == K - 1, [wBk, srckey(ci)], [pbk])
        return pa, pak, pb, pbk

    def rope_to(dst_ap, M, n, pa, pak, pb, pbk, cs, ck, dkey):
        t1, t1k = tmp_p.get()
        t2, t2k = tmp_p.get()
        TT(t1[:M, :n], pa[:M, :n], cs[:M, 0, :n], ALU.mult, [pak, ck], [t1k])
        TT(t2[:M, :n], pb[:M, :n], cs[:M, 1, :n], ALU.mult, [pbk, ck], [t2k])
        TT(dst_ap, t1[:M, :n], t2[:M, :n], ALU.add, [t1k, t2k], [dkey])

    def store_rows(dram_ap, sb_ap, sbkey, dkey):
        DMA('sp', dram_ap, sb_ap, dkey if isinstance(dkey, str) else dkey[0], [sbkey], [dkey])

    def hkey(ci):
        return ('hT', ci)

    def exchange(l):
        fence(['sp'], ATT_KEYS + HT_KEYS)
        grp = [list(g) for g in pair_groups]
        P.add('pool', lambda e: e.collective_compute(
            "AllGather", ALU.bypass, replica_groups=grp, ins=[bar_in.ap().opt()], outs=[bar_out.ap().opt()]),
            ['kmine', 'vmine'], ['kmine', 'vmine'], dma='ccp%d' % l, inc=1)
        for pi_, (r0, nr, mine, all_) in enumerate(kparts[l]):
            P.add('pool', lambda e, mine=mine, all_=all_: e.collective_compute(
                "AllGather", ALU.bypass, replica_groups=grp, ins=[mine.ap().opt()], outs=[all_.ap().opt()]),
                ['kmine'], [('kallp', pi_)], dma='cck%d_%d' % (l, pi_), inc=1)
        for pi_, (mine, all_) in enumerate(vparts[l]):
            P.add('pool', lambda e, mine=mine, all_=all_: e.collective_compute(
                "AllGather", ALU.bypass, replica_groups=grp, ins=[mine.ap().opt()], outs=[all_.ap().opt()]),
                ['vmine'], [('vallp', pi_)], dma='ccv%d_%d' % (l, pi_), inc=1)
        P.add('pool', lambda e: e.collective_compute(
            "AllGather", ALU.bypass, replica_groups=grp, ins=[bar_in.ap().opt()], outs=[bar_out.ap().opt()]),
            [('kallp', i) for i in range(len(kparts[l]))] + [('vallp', i) for i in range(2)], ['kall', 'vall'],
            dma='ccb%d' % l, inc=1)

    def v_proj_tm(l, w, wk, col0, ncols, vcol0):
        for tb in range(T // 128):
            ci = min(tb // 4, 4)
            pv, pvk = ps_p.get()
            for k in range(KD):
                MM(pv[:, :ncols], hT[:, k, tb * 128:(tb + 1) * 128], w[:, k, col0:col0 + ncols], k == 0, k == KD - 1,
                   [wk, ('hT', ci)], [pvk])
            st, sk = stg_p.get()
            CP(st[:, :ncols], pv[:, :ncols], [pvk], [sk])
            if tb < 16:
                store_rows(vmine_rows(l, tb * 128, 128)[:, vcol0:vcol0 + ncols], st[:, :ncols], sk, 'vmine')
            else:
                store_rows(vc_scr[(tb - 16) * 128:(tb - 15) * 128, vcol0:vcol0 + ncols], st[:, :ncols], sk, 'vc_scr')

    def qk_tile(l, ci, M, pa, pak, pb, pbk, cs, ck, dst_rot, dst_plain, rot_key, plain_key):
        c0, n = CH[ci]
        if dst_plain is not None:
            st, sk = stg_p.get()
            CP(st[:M, :n], pa[:M, :n], [pak], [sk])
            store_rows(dst_plain, st[:M, :n], sk, plain_key)
        if dst_rot is not None:
            st, sk = stg_p.get()
            rope_to(st[:M, :n], M, n, pa, pak, pb, pbk, cs, ck, sk)
            store_rows(dst_rot, st[:M, :n], sk, rot_key)

    def attn_unit(scale, passes_x, passes_c, xblocks, v_of, dv, qcols, o_dst, o_key, final, mask=None,
                  extra_r=()):
        for ci, (c0, n) in enumerate(CH):
            if ci == 4:
                blocks = []
            else:
                blocks = xblocks(ci)
            po, pok = acc_p.get()
            pd, pdk = acc_p.get()
            seq = [('c', 0)] + [('x', b) for b in blocks] + [('c', 1)]

            def stage_a(bi):
                kind, b = seq[bi]
                if kind == 'c':
                    kc0, vb, (qa, qb), mspec = b * 128, 32 + b, (0, n), None
                    passes = passes_c
                else:
                    kc0, vb, (qa, qb), mspec = b
                    passes = passes_x
                st_, stk = ps_p.get()
                for pi, (kt, qt, rows, kkey, qkey) in enumerate(passes):
                    MM(st_[:, qa:qb], kt[:rows, kc0:kc0 + 128], qt[:rows, c0 + qa:c0 + qb], pi == 0,
                       pi == len(passes) - 1, [kkey, qkey], [stk])
                pt, ptk = pt_p.get()
                ACT(pt[:, qa:qb], st_[:, qa:qb], AF.Exp, [stk], [ptk], scale=scale)
                if mspec is not None:
                    mask(pt, ptk, qa, qb, mspec)
                return (pt, ptk, vb, qa, qb)

            def stage_b(bi, st):
                pt, ptk, vb, qa, qb = st
                first = bi == 0
                last = bi == len(seq) - 1
                MM(po[:dv, qa:qb], v_of(vb), pt[:, qa:qb], first, last, ['vt', ptk], [pok])
                MM(pd[:dv, qa:qb], ones_b[:, :dv], pt[:, qa:qb], first, last, ['ones_b', ptk], [pdk])

            prev = stage_a(0)
            for bi in range(1, len(seq)):
                cur_ = stage_a(bi)
                stage_b(bi - 1, prev)
                prev = cur_
            stage_b(len(seq) - 1, prev)
            final(ci, c0, n, po, pok, pd, pdk)

    def plain_final(dst_tile, dst_key, dv, den_add=None):
        def fin(ci, c0, n, po, pok, pd, pdk):
            r, rk = tmp_p.get()
            if den_add is not None:
                TS(r[:dv, :n], pd[:dv, :n], den_add, None, ALU.add, None, [pdk, 'small'], [rk])
                RCP(r[:dv, :n], r[:dv, :n], [rk], [rk])
            else:
                RCP(r[:dv, :n], pd[:dv, :n], [pdk], [rk])
            TT(dst_tile[:dv, c0:c0 + n], po[:dv, :n], r[:dv, :n], ALU.mult, [pok, rk], [(dst_key, ci)])
        return fin

    def wo_apply(l, wo_d, src_tile, src_key, row_blk, rows):
        w, wk = wout_p.get()
        DMA('pool', w[:rows, 0, :], wo_d[0:rows, row_blk, :] if rows == 128 else wo_d[0:rows, row_blk, :],
            'wout%d' % wk[1], [], [wk])
        for ci, (c0, n) in enumerate(CH):
            s = 1 if ci == 4 else 0
            for dk in range(KD):
                po, pok = ps_p.get()
                MM(po[:, :n], w[:rows, 0, dk * 128:(dk + 1) * 128], src_tile[:rows, c0:c0 + n], True, True,
                   [wk, (src_key, ci)], [pok])
                STT(xT[:, dk, c0:c0 + n], po[:, :n], gt_s[:, 1, dk, s:s + 1], xT[:, dk, c0:c0 + n],
                    ALU.mult, ALU.add, [pok, 'gt', ('xT', ci, dk)], [('xT', ci, dk)])

    def load_v(l, col0, dv):
        for r in range(2):
            for j in range(2):
                DMA('sp', vt[:, 16 * r + 8 * j:16 * r + 8 * j + 8, :dv],
                    vall_rows(l, r, j * 1024, 1024)[:, col0:col0 + dv].rearrange("(b p) c -> p b c", p=128),
                    'vt', ['vall'], ['vt'])
        DMA('sp', vt[:, 32:34, :dv], vc_scr[:, col0:col0 + dv].rearrange("(b p) c -> p b c", p=128), 'vt',
            ['vc_scr'], ['vt'])

    def load_kx(l, tile, key, row0, rows):
        for r in range(2):
            DMA('sp', tile[:rows, r * TL:(r + 1) * TL], kall_rows(l, r, row0, rows), key, ['kall'], [key])

    def full_blocks(ci):
        n = CH[ci][1]
        return [(kb * 128, kb, (0, n), None) for kb in range(32)]

    def diff_mixer(l, lam_init):
        scale = 64 ** -0.5
        lam = small[:, 0:1]
        DMA('sp', tmp_lam[:, :], df_lam[:, :], 'misc', [], ['lamraw'])
        DMA('sp', small[:, 8:9], df_ng[:, :], 'misc', [], ['small'])
        TT(tmp_lam[:, 0:64], tmp_lam[:, 0:64], tmp_lam[:, 64:128], ALU.mult, ['lamraw'], ['lamraw'])
        TT(tmp_lam[:, 128:192], tmp_lam[:, 128:192], tmp_lam[:, 192:256], ALU.mult, ['lamraw'], ['lamraw'])
        P.add('dve', lambda e: e.reduce_sum(out=small[:, 1:2], in_=tmp_lam[:, 0:64], axis=mybir.AxisListType.X),
              ['lamraw'], ['small'])
        P.add('dve', lambda e: e.reduce_sum(out=small[:, 2:3], in_=tmp_lam[:, 128:192], axis=mybir.AxisListType.X),
              ['lamraw'], ['small'])
        ACT(small[:, 1:3], small[:, 1:3], AF.Exp, ['small'], ['small'])
        TT(small[:, 0:1], small[:, 1:2], small[:, 2:3], ALU.subtract, ['small'], ['small'])
        TS(small[:, 0:1], small[:, 0:1], lam_init, None, ALU.add, None, ['small'], ['small'])
        TS(small[:, 3:4], small[:, 0:1], -1.0, None, ALU.mult, None, ['small'], ['small'])
        TS(small[:, 9:10], small[:, 8:9], 1.0 - lam_init, None, ALU.mult, None, ['small'], ['small'])

        for part, dst_rot, dst_plain_l, dst_plain_c in (('q', (lambda a, b: qx_scr[a:a + b, :]), qp_scr, qp_scr), ('k', (lambda a, b: kmine_rows(l, a, b)), None, kc_scr)):
            base = 0 if part == 'q' else 1024
            for half in range(2):
                wA, wAk = load_w(df_qkv[:, :, base + half * 512:base + (half + 1) * 512], KD, 512)
                wB, wBk = load_w(df_qkp[:, :, base + half * 512:base + (half + 1) * 512], KD, 512)
                for ci, (c0, n) in enumerate(CH):
                    cs, ck = load_cs(ci) if ci < 4 else (None, None)
                    for t in range(4):
                        row0 = (half * 4 + t) * 128
                        if ci < 4:
                            pa, pak, pb, pbk = proj_fm(hT, hkey, KD, wA, wAk, t * 128, 128, ci, wB, wBk, t * 128)
                            qk_tile(l, ci, 128, pa, pak, pb, pbk, cs, ck,
                                    dst_rot(row0, 128)[:, c0:c0 + n],
                                    dst_plain_l[row0:row0 + 128, c0:c0 + n] if dst_plain_l is not None else None,
                                    'qx_scr' if part == 'q' else 'kmine', 'qp_scr')
                        else:
                            pa, pak, _, _ = proj_fm(hT, hkey, KD, wA, wAk, t * 128, 128, ci)
                            if part == 'q':
                                qk_tile(l, ci, 128, pa, pak, None, None, None, None, None,
                                        qp_scr[row0:row0 + 128, c0:c0 + n], None, 'qp_scr')
                            else:
                                qk_tile(l, ci, 128, pa, pak, None, None, None, None, None,
                                        kc_scr[row0:row0 + 128, 0:n], None, 'kc_scr')
        for half in range(2):
            w, wk = load_w(df_qkv[:, :, 2048 + half * 512:2048 + (half + 1) * 512], KD, 512)
            v_proj_tm(l, w, wk, 0, 512, half * 512)
        exchange(l)

        for h in range(8):
            load_v(l, h * 128, 128)
            for i in range(2):
                u = h * 2 + i
                qt_x, qt_p = (q0, q1)
                DMA('sp', q2[:64, :], qx_scr[u * 64:(u + 1) * 64, :], 'q2', ['qx_scr'], ['q2'])
                DMA('sp', q1[:64, :], qp_scr[u * 64:(u + 1) * 64, :], 'q1', ['qp_scr'], ['q1'])
                load_kx(l, kx, 'kx', u * 64, 64)
                DMA('sp', kcx[:64, :], kc_scr[u * 64:(u + 1) * 64, :], 'kcx', ['kc_scr'], ['kcx'])

                def fin(ci, c0, n, po, pok, pd, pdk, i=i):
                    r, rk = tmp_p.get()
                    RCP(r[:, :n], pd[:, :n], [pdk], [rk])
                    if i == 0:
                        TT(o0f[:, c0:c0 + n], po[:, :n], r[:, :n], ALU.mult, [pok, rk], [('o0f', ci)])
                    else:
                        t, tk = tmp_p.get()
                        TT(t[:, :n], po[:, :n], r[:, :n], ALU.mult, [pok, rk], [tk])
                        STT(o0f[:, c0:c0 + n], t[:, :n], small[:, 3:4], o0f[:, c0:c0 + n], ALU.mult, ALU.add,
                            [tk, 'small', ('o0f', ci)], [('o0f', ci)])
                attn_unit(scale, [(kx, q2, 64, 'kx', 'q2')], [(kcx, q1, 64, 'kcx', 'q1')], full_blocks,
                          lambda vb: vt[:, vb, :], 128, None, None, None, fin)
            for ci, (c0, n) in enumerate(CH):
                tk = ('o0f', ci)
                sq, sqk = tmp_p.get()
                ACT(sq[:, :n], o0f[:, c0:c0 + n], AF.Square, [tk], [sqk])
                ss, ssk = ps_p.get()
                MM(ss[:, :n], ones_f[:, :], sq[:, :n], True, True, [sqk, 'ones_f'], [ssk])
                rs, rsk = rs_p.get()
                ACT(rs[:, :n], ss[:, :n], AF.Sqrt, [ssk], [rsk], scale=1.0 / 128, bias=eps_t[:, 0:1])
                RCP(rs[:, :n], rs[:, :n], [rsk], [rsk])
                STT(oh[:, c0:c0 + n], o0f[:, c0:c0 + n], small[:, 9:10], rs[:, :n], ALU.mult, ALU.mult,
                    [tk, rsk, 'small'], [('oh', ci)])
            wo_apply(l, df_wo, oh, 'oh', h, 128)

    tmp_lam = sb("tmp_lam", [128, 256])
    o0f = sb("o0f", [128, T], F32, at=R2 + 8192)

    cqn = None

    def mla_mixer(l):
        scale = 192 ** -0.5
        DMA('sp', small[:, 16:20], mla_qn[:, :], 'misc', [], ['small'])
        DMA('sp', small[:, 20:22], mla_kvn[:, :], 'misc', [], ['small'])
        wq_, wqk = load_w(mla_down[:, :, 0:512], KD, 512)
        wk_, wkk = load_w(mla_down[:, :, 512:832], KD, 320)
        for ci, (c0, n) in enumerate(CH):
            for (wt_, wtk, nblk, nrm_col, dst, dkey, inv) in ((wq_, wqk, 4, 16, cq_t, 'cq', 1.0 / 512),
                                                           (wk_, wkk, 2, 20, ckv_t, 'ckv', 1.0 / 256)):
                pss = []
                ss, ssk = acc_p.get()
                for b in range(nblk):
                    pa, pak, _, _ = proj_fm(hT, hkey, KD, wt_, wtk, b * 128, 128, ci)
                    pss.append((pa, pak))
                    sq, sqk = tmp_p.get()
                    ACT(sq[:, :n], pa[:, :n], AF.Square, [pak], [sqk])
                    MM(ss[:, :n], ones_f[:, :], sq[:, :n], b == 0, b == nblk - 1, [sqk, 'ones_f'], [ssk])
                rs, rsk = rs_p.get()
                ACT(rs[:, :n], ss[:, :n], AF.Sqrt, [ssk], [rsk], scale=inv, bias=eps_t[:, 0:1])
                RCP(rs[:, :n], rs[:, :n], [rsk], [rsk])
                for b in range(nblk):
                    pa, pak = pss[b]
                    STT(dst[:, b, c0:c0 + n], pa[:, :n], small[:, nrm_col + b:nrm_col + b + 1], rs[:, :n],
                        ALU.mult, ALU.mult, [pak, rsk, 'small'], [(dkey, ci)])
        wp_, wpk = load_w(mla_downp[:, :, :], KD, 64)
        for ci, (c0, n) in enumerate(CH):
            if ci < 4:
                cs, ck = load_cs(ci)
                pa, pak, pb, pbk = proj_fm(hT, hkey, KD, wk_, wkk, 256, 64, ci, wp_, wpk, 0)
                qk_tile(l, ci, 64, pa, pak, pb, pbk, cs, ck, kmine_rows(l, 1024, 64)[:, c0:c0 + n], None, 'kmine', None)
            else:
                pa, pak, _, _ = proj_fm(hT, hkey, KD, wk_, wkk, 256, 64, ci)
                qk_tile(l, ci, 64, pa, pak, None, None, None, None, None, kc_scr[1024:1088, 0:n], None, 'kc_scr')
        cqkey = lambda ci: ('cq', ci)
        ckvkey = lambda ci: ('ckv', ci)
        for h in range(8):
            wA, wAk = load_w(mla_uq[:, :, h * 192:(h + 1) * 192], 4, 192)
            wB, wBk = load_w(mla_uqp[:, :, h * 64:(h + 1) * 64], 4, 64)
            wkv, wkvk = load_w(mla_ukv[:, :, h * 256:(h + 1) * 256], 2, 256)
            for ci, (c0, n) in enumerate(CH):
                pa, pak, _, _ = proj_fm(cq_t, cqkey, 4, wA, wAk, 0, 128, ci)
                qk_tile(l, ci, 128, pa, pak, None, None, None, None, None, qp_scr[h * 256:h * 256 + 128, c0:c0 + n],
                        None, 'qp_scr')
                if ci < 4:
                    cs, ck = load_cs(ci)
                    pa, pak, pb, pbk = proj_fm(cq_t, cqkey, 4, wA, wAk, 128, 64, ci, wB, wBk, 0)
                    qk_tile(l, ci, 64, pa, pak, pb, pbk, cs, ck, qx_scr[h * 64:h * 64 + 64, c0:c0 + n],
                            qp_scr[h * 256 + 128:h * 256 + 192, c0:c0 + n], 'qx_scr', 'qp_scr')
                else:
                    pa, pak, _, _ = proj_fm(cq_t, cqkey, 4, wA, wAk, 128, 64, ci)
                    qk_tile(l, ci, 64, pa, pak, None, None, None, None, None,
                            qp_scr[h * 256 + 128:h * 256 + 192, c0:c0 + n], None, 'qp_scr')
                pa, pak, _, _ = proj_fm(ckv_t, ckvkey, 2, wkv, wkvk, 0, 128, ci)
                if ci < 4:
                    qk_tile(l, ci, 128, pa, pak, None, None, None, None, None, kmine_rows(l, h * 128, 128)[:, c0:c0 + n],
                            None, 'kmine')
                else:
                    qk_tile(l, ci, 128, pa, pak, None, None, None, None, None, kc_scr[h * 128:(h + 1) * 128, 0:n],
                            None, 'kc_scr')
            for tb in range(T // 128):
                ci = min(tb // 4, 4)
                pv, pvk = ps_p.get()
                for k in range(2):
                    MM(pv[:, :128], ckv_t[:, k, tb * 128:(tb + 1) * 128], wkv[:, k, 128:256], k == 0, k == 1,
                       [wkvk, ('ckv', ci)], [pvk])
                st, sk = stg_p.get()
                CP(st[:, :128], pv[:, :128], [pvk], [sk])
                if tb < 16:
                    store_rows(vmine_rows(l, tb * 128, 128)[:, h * 128:(h + 1) * 128], st[:, :128], sk, 'vmine')
                else:
                    store_rows(vc_scr[(tb - 16) * 128:(tb - 15) * 128, h * 128:(h + 1) * 128], st[:, :128], sk, 'vc_scr')
        exchange(l)
        load_kx(l, krx, 'krx', 1024, 64)
        DMA('sp', krc[:64, :], kc_scr[1024:1088, :], 'krc', ['kc_scr'], ['krc'])
        for h in range(8):
            load_v(l, h * 128, 128)
            DMA('sp', q0[:, :], qp_scr[h * 256:h * 256 + 128, :], 'q0', ['qp_scr'], ['q0'])
            DMA('sp', q1[:64, :], qp_scr[h * 256 + 128:h * 256 + 192, :], 'q1', ['qp_scr'], ['q1'])
            DMA('sp', q2[:64, :], qx_scr[h * 64:(h + 1) * 64, :], 'q2', ['qx_scr'], ['q2'])
            load_kx(l, kx, 'kx', h * 128, 128)
            DMA('sp', kcx[:, :], kc_scr[h * 128:(h + 1) * 128, :], 'kcx', ['kc_scr'], ['kcx'])
            attn_unit(scale, [(kx, q0, 128, 'kx', 'q0'), (krx, q2, 64, 'krx', 'q2')],
                      [(kcx, q0, 128, 'kcx', 'q0'), (krc, q1, 64, 'krc', 'q1')], full_blocks,
                      lambda vb: vt[:, vb, :], 128, None, None, None, plain_final(oh, 'oh', 128))
            wo_apply(l, mla_wo, oh, 'oh', h, 128)

    cq_t = sb("cq_t", [128, 4, T], BF16, at=R2)
    ckv_t = sb("ckv_t", [128, 2, T], BF16, at=R2 + 18432)

    swa_m = sb("swa_m", [128, 6, 512], BF16, at=R2 + 17408)
    halo_s = sb("halo_s", [128, 2])

    def swa_mixer(l):
        scale = 64 ** -0.5
        DMA('sp', swa_m[:, :, :], swa_mask[:, :, :], 'misc', [], ['swa_m'])
        DMA('sp', halo_s[:, :], halo_v[:, :], 'misc', [], ['halo_s'])
        DMA('sp', small[:, 24:40], swa_sink[:, :], 'misc', [], ['small'])
        ACT(small[:, 24:40], small[:, 24:40], AF.Exp, ['small'], ['small'])
        for part, ntile, dst_rot, plain_l, plain_c in (('q', 8, (lambda a, b: qx_scr[a:a + b, :]), qp_scr, qp_scr), ('k', 2, (lambda a, b: kmine_rows(l, a, b)), None, kc_scr)):
            base = 0 if part == 'q' else 1024
            for grp in range((ntile + 3) // 4):
                nt = min(4, ntile - grp * 4)
                wA, wAk = load_w(swa_qkv[:, :, base + grp * 512:base + grp * 512 + nt * 128], KD, nt * 128)
                wB, wBk = load_w(swa_qkp[:, :, base + grp * 512:base + grp * 512 + nt * 128], KD, nt * 128)
                for ci, (c0, n) in enumerate(CH):
                    cs, ck = load_cs(ci) if ci < 4 else (None, None)
                    for t in range(nt):
                        row0 = (grp * 4 + t) * 128
                        if ci < 4:
                            pa, pak, pb, pbk = proj_fm(hT, hkey, KD, wA, wAk, t * 128, 128, ci, wB, wBk, t * 128)
                            qk_tile(l, ci, 128, pa, pak, pb, pbk, cs, ck, dst_rot(row0, 128)[:, c0:c0 + n],
                                    plain_l[row0:row0 + 128, c0:c0 + n] if plain_l is not None else None,
                                    'qx_scr' if part == 'q' else 'kmine', 'qp_scr')
                        else:
                            pa, pak, _, _ = proj_fm(hT, hkey, KD, wA, wAk, t * 128, 128, ci)
                            dstc = qp_scr[row0:row0 + 128, c0:c0 + n] if part == 'q' else kc_scr[row0:row0 + 128, 0:n]
                            qk_tile(l, ci, 128, pa, pak, None, None, None, None, None, dstc, None,
                                    'qp_scr' if part == 'q' else 'kc_scr')
        w, wk = load_w(swa_qkv[:, :, 1280:1536], KD, 256)
        v_proj_tm(l, w, wk, 0, 256, 0)
        exchange(l)

        def mask(pt, ptk, qa, qb, mspec):
            jj, halo = mspec
            if halo is None:
                TT(pt[:, qa:qb], pt[:, qa:qb], swa_m[:, jj, qa:qb], ALU.mult, [ptk, 'swa_m'], [ptk])
            else:
                STT(pt[:, qa:qb], pt[:, qa:qb], halo_s[:, halo:halo + 1], swa_m[:, jj, qa:qb], ALU.mult, ALU.mult,
                    [ptk, 'swa_m', 'halo_s'], [ptk])

        for kvh in range(4):
            DMA('sp', kx[:64, 0:128], kall_rows(l, 0, kvh * 64, 64)[:, TL - 128:TL], 'kx', ['kall'], ['kx'])
            DMA('sp', kx[:64, 128:128 + TL], kmine_rows(l, kvh * 64, 64), 'kx', ['kmine'], ['kx'])
            DMA('sp', kx[:64, 128 + TL:256 + TL], kall_rows(l, 1, kvh * 64, 64)[:, 0:128], 'kx',
                ['kall'], ['kx'])
            DMA('sp', kcx[:64, :], kc_scr[kvh * 64:(kvh + 1) * 64, :], 'kcx', ['kc_scr'], ['kcx'])
            DMA('sp', vt[:, 0:1, :64], vall_rows(l, 0, TL - 128, 128)[:, kvh * 64:(kvh + 1) * 64].rearrange("(b p) c -> p b c", p=128),
                'vt', ['vall'], ['vt'])
            for j in range(2):
                DMA('sp', vt[:, 1 + 8 * j:9 + 8 * j, :64],
                    vmine_rows(l, j * 1024, 1024)[:, kvh * 64:(kvh + 1) * 64].rearrange("(b p) c -> p b c", p=128),
                    'vt', ['vmine'], ['vt'])
            DMA('sp', vt[:, 17:18, :64], vall_rows(l, 1, 0, 128)[:, kvh * 64:(kvh + 1) * 64].rearrange("(b p) c -> p b c", p=128),
                'vt', ['vall'], ['vt'])
            DMA('sp', vt[:, 32:34, :64], vc_scr[:, kvh * 64:(kvh + 1) * 64].rearrange("(b p) c -> p b c", p=128), 'vt',
                ['vc_scr'], ['vt'])
            for g in range(4):
                hq = kvh * 4 + g
                DMA('sp', q2[:64, :], qx_scr[hq * 64:(hq + 1) * 64, :], 'q2', ['qx_scr'], ['q2'])
                DMA('sp', q1[:64, :], qp_scr[hq * 64:(hq + 1) * 64, :], 'q1', ['qp_scr'], ['q1'])

                def xb(ci):
                    out = []
                    order = [None]
                    for jj in range(6):
                        qa = max(0, jj - 2) * 128
                        qb = min(4, jj + 1) * 128
                        blk = 4 * ci + jj
                        halo = None
                        if blk == 0:
                            halo = 0
                        elif blk == 17:
                            halo = 1
                        out.append((blk * 128, blk, (qa, qb), (jj, halo)))
                    return out
                attn_unit(scale, [(kx, q2, 64, 'kx', 'q2')], [(kcx, q1, 64, 'kcx', 'q1')], xb,
                          lambda vb: vt[:, vb, :64], 64, None, None, None,
                          plain_final(oh, 'oh', 64, den_add=small[:64, 24 + hq:25 + hq]), mask=mask)
                wo_apply(l, swa_wo, oh, 'oh', hq, 64)

    na_E = sb("na_E", [128, 22 * 64], BF16, at=R2 + 23552)
    na_rv_s = sb("na_rv_s", [128, 256], BF16, at=R2 + 26368)
    na_rv_f = tmp_lam

    def na_mixer(l):
        scale = 64 ** -0.5
        DMA('sp', na_rv_f[:, :], na_rv[:, :], 'misc', [], ['na_rv_f'])
        CP(na_rv_s[:, :], na_rv_f[:, :], ['na_rv_f'], ['na_rv'])
        for part, dstl, dstc in (('q', (lambda a, b: qp_scr[a:a + b, :]), qp_scr), ('k', (lambda a, b: kmine_rows(l, a, b)), kc_scr)):
            base = 0 if part == 'q' else 1024
            for grp in range(2):
                wA, wAk = load_w(na_qkv[:, :, base + grp * 512:base + (grp + 1) * 512], KD, 512)
                for ci, (c0, n) in enumerate(CH):
                    for t in range(4):
                        row0 = (grp * 4 + t) * 128
                        pa, pak, _, _ = proj_fm(hT, hkey, KD, wA, wAk, t * 128, 128, ci)
                        if part == 'q' or ci < 4:
                            qk_tile(l, ci, 128, pa, pak, None, None, None, None, None,
                                    dstl(row0, 128)[:, c0:c0 + n], None, 'qp_scr' if part == 'q' else 'kmine')
                        else:
                            qk_tile(l, ci, 128, pa, pak, None, None, None, None, None, kc_scr[row0:row0 + 128, 0:n],
                                    None, 'kc_scr')
        for half in range(2):
            w, wk = load_w(na_qkv[:, :, 2048 + half * 512:2048 + (half + 1) * 512], KD, 512)
            v_proj_tm(l, w, wk, 0, 512, half * 512)
        exchange(l)

        for h in range(16):
            for piece in range(3):
                a0 = piece * 512
                a1 = min(1408, a0 + 512)
                t, tk = tmp_p.get()
                DMA('sp', t[:, :a1 - a0], na_rpbE[h, :, a0:a1], 'tmpd%d' % tk[1], [], [tk])
                ACT(na_E[:, a0:a1], t[:, :a1 - a0], AF.Exp, [tk], ['na_E'])
            DMA('sp', kx[:64, 0:256], kall_rows(l, 0, h * 64, 64)[:, TL - 256:TL], 'kx', ['kall'], ['kx'])
            DMA('sp', kx[:64, 256:256 + TL], kmine_rows(l, h * 64, 64), 'kx', ['kmine'], ['kx'])
            DMA('sp', kx[:64, 256 + TL:512 + TL], kall_rows(l, 1, h * 64, 64)[:, 0:256], 'kx',
                ['kall'], ['kx'])
            DMA('sp', kcx[:64, :], kc_scr[h * 64:(h + 1) * 64, :], 'kcx', ['kc_scr'], ['kcx'])
            DMA('sp', vt[:, 0:2, :64], vall_rows(l, 0, TL - 256, 256)[:, h * 64:(h + 1) * 64].rearrange("(b p) c -> p b c", p=128),
                'vt', ['vall'], ['vt'])
            for j in range(2):
                DMA('sp', vt[:, 2 + 8 * j:10 + 8 * j, :64],
                    vmine_rows(l, j * 1024, 1024)[:, h * 64:(h + 1) * 64].rearrange("(b p) c -> p b c", p=128),
                    'vt', ['vmine'], ['vt'])
            DMA('sp', vt[:, 18:20, :64], vall_rows(l, 1, 0, 256)[:, h * 64:(h + 1) * 64].rearrange("(b p) c -> p b c", p=128),
                'vt', ['vall'], ['vt'])
            DMA('sp', vt[:, 32:34, :64], vc_scr[:, h * 64:(h + 1) * 64].rearrange("(b p) c -> p b c", p=128), 'vt',
                ['vc_scr'], ['vt'])
            DMA('sp', q1[:64, :], qp_scr[h * 64:(h + 1) * 64, :], 'q1', ['qp_scr'], ['q1'])

            def xb(ci):
                return [((4 * ci + jb) * 128, 4 * ci + jb, (0, 512), (ci, jb)) for jb in range(8)]

            def mask(pt, ptk, qa, qb, mspec):
                ci, jb = mspec
                j0 = 14 - 2 * jb
                TT(pt[:, 0:512], pt[:, 0:512], na_E[:, j0 * 64:(j0 + 8) * 64], ALU.mult, [ptk, 'na_E'], [ptk])
                i0 = (ci * 8 + jb) * 8
                TT(pt[:, 0:512].rearrange("p (a b) -> p a b", b=64), pt[:, 0:512].rearrange("p (a b) -> p a b", b=64),
                   na_rv_s[:, i0:i0 + 8].unsqueeze(2).broadcast_to([128, 8, 64]), ALU.mult, [ptk, 'na_rv'], [ptk])
            attn_unit(scale, [(kx, q1, 64, 'kx', 'q1')], [(kcx, q1, 64, 'kcx', 'q1')], xb,
                      lambda vb: vt[:, vb, :64], 64, None, None, None, plain_final(oh, 'oh', 64), mask=mask)
            wo_apply(l, na_wo, oh, 'oh', h, 64)

    for l in layers:
        mod_phase(l)
        norm_phase(0)
        ffn_phase(l, 0, 0)
        if do_mixer:
            norm_phase(1)
            if l == 0:
                fence(['dve'], R2_KEYS)
                mla_mixer(l)
                fence(['sp', 'act', 'dve'], R2_KEYS)
            elif l == 1:
                swa_mixer(l)
            elif l == 2:
                na_mixer(l)
            else:
                diff_mixer(l, 0.8 - 0.6 * float(np.exp(-0.3 * l)))
            fence(['act'], ATT_KEYS + HT_KEYS)
        norm_phase(2)
        ffn_phase(l, 1, 2)

    for ci, (c0, n) in enumerate(CH[:4]):
        ss, ssk = ps_p.get()
        for k in range(KD):
            sq, sqk = tmp_p.get()
            ACT(sq[:, :n], xT[:, k, c0:c0 + n], AF.Square, [('xT', ci, k)], [sqk])
            MM(ss[:, :n], ones_f[:, :], sq[:, :n], k == 0, k == KD - 1, [sqk, 'ones_f'], [ssk])
        rstd, rk = rs_p.get()
        ACT(rstd[:, :n], ss[:, :n], AF.Sqrt, [ssk], [rk], scale=1.0 / D, bias=eps_t[:, 0:1])
        RCP(rstd[:, :n], rstd[:, :n], [rk], [rk])
        for k in range(KD):
            t, tk = tmp_p.get()
            STT(t[:, :n], xT[:, k, c0:c0 + n], fng_s[:, k:k + 1], rstd[:, :n], ALU.mult, ALU.mult,
                [('xT', ci, k), rk, 'fng'], [tk])
            DMA('sp', outT[:, k, c0:c0 + n], t[:, :n], 'out', [tk], ['out'])
    if DEBUG and 3 in layers:
        dk1 = nc.dram_tensor("dbg_kall", [1024, TL], BF16, kind="ExternalOutput")
        dk2 = nc.dram_tensor("dbg_kmine", [512, TL], BF16, kind="ExternalOutput")
        DMA('sp', dk1[:, :], kparts[3][0][3][:, :], 'out', ['kall'], ['out'])
        DMA('sp', dk2[:, :], kparts[3][0][2][:, :], 'out', ['kmine'], ['out'])
    P.add('sp', None, ['out'], [])

    P.emit(nc, es)
    es.close()
    return nc


def _kmaj(w):
    K, N = w.shape
    return np.ascontiguousarray(w.reshape(K // 128, 128, N).transpose(1, 0, 2))


def _swap_halves(w, col0, ncols, hd=64):
    idx = np.arange(col0, col0 + ncols).reshape(-1, 2, hd // 2)[:, ::-1, :].reshape(-1)
    return w[:, idx]


def _rope_tables(pos0, n):
    t = np.arange(pos0, pos0 + n)
    row = (t // 64).astype(np.float32)
    col = (t % 64).astype(np.float32)
    nf = 16
    inv = np.exp(-np.log(np.float32(10000.0)) * np.arange(nf, dtype=np.float32) / nf).astype(np.float32)
    ang = np.concatenate([row[:, None] * inv, col[:, None] * inv], axis=-1).astype(np.float32)
    c = np.cos(ang).T.astype(np.float32)
    s = np.sin(ang).T.astype(np.float32)
    cos4 = np.concatenate([c, c, c, c], 0)
    sin4 = np.concatenate([-s, s, -s, s], 0)
    return np.ascontiguousarray(cos4), np.ascontiguousarray(sin4)


def _swa_masks():
    import ml_dtypes
    m = np.zeros((128, 6, 512), np.float32)
    kj = np.arange(128)[:, None]
    q = np.arange(512)[None, :]
    for jj in range(6):
        kpos = (jj - 1) * 128 + kj
        m[:, jj, :] = (np.abs(kpos - q) <= 128)
    return m.astype(ml_dtypes.bfloat16)


def _na_tables(rpb, half):
    H = rpb.shape[0]
    E = np.full((H, 128, 22, 64), NEGFILL, np.float32)
    kc = np.arange(64)[:, None]
    qc = np.arange(64)[None, :]
    cs = np.clip(qc - 8, 0, 48)
    colvalid = (kc >= cs) & (kc < cs + 16)
    coff = np.clip(kc - qc + 15, 0, 30)
    for hf in range(2):
        for jp in range(22):
            e = jp - 3 - hf
            if e < 0 or e > 14:
                continue
            g = rpb[:, 14 - e][:, coff]
            g = np.where(colvalid[None], g, np.float32(NEGFILL))
            E[:, hf * 64:(hf + 1) * 64, jp, :] = g
    rv = np.zeros((128, 4, 8, 8), np.float32)
    for ci in range(4):
        for jb in range(8):
            for hf in range(2):
                krl = 8 * ci - 4 + 2 * jb + hf
                kr = 32 * half + krl
                for qi in range(8):
                    qr = 32 * half + 8 * ci + qi
                    rs = min(max(qr - 4, 0), 56)
                    ok = (rs <= kr < rs + 8) and (0 <= kr < 64)
                    if krl < 0 and half == 0:
                        ok = False
                    if krl >= 32 and half == 1:
                        ok = False
                    rv[hf * 64:(hf + 1) * 64, ci, jb, qi] = 1.0 if ok else 0.0
    return E.reshape(H, 128, 22 * 64), rv.reshape(128, 256)


def prepare_inputs(inp, ncores=8):
    f = lambda a: np.ascontiguousarray(np.asarray(a, dtype=np.float32))
    x, c, ctx, c_ctx = f(inp['x']), f(inp['c']), f(inp['ctx']), f(inp['c_ctx'])
    shared = {}
    shared['modw'] = np.ascontiguousarray(f(inp['mod_w']).reshape(4, 8, 128, 9216).transpose(0, 2, 1, 3))
    shared['modb'] = np.ascontiguousarray(f(inp['mod_b']).reshape(4, 72, 128).transpose(2, 0, 1))
    shared['ng_in'] = np.ascontiguousarray(f(inp['norm_g']).reshape(4, 3, 8, 128).transpose(3, 0, 1, 2))
    shared['fng_in'] = np.ascontiguousarray(f(inp['final_norm_g']).reshape(8, 128).T)
    shared['win'] = np.ascontiguousarray(f(inp['ffn_w_in']).reshape(4, 2, 8, 128, 2 * DFF).transpose(0, 1, 3, 2, 4))
    shared['wout'] = np.ascontiguousarray(f(inp['ffn_w_out']).reshape(4, 2, 22, 128, D).transpose(0, 1, 3, 2, 4))
    wd = f(inp['mla_w_down'])[0]
    shared['mla_down'] = _kmaj(wd)
    shared['mla_downp'] = _kmaj(_swap_halves(wd, 768, 64))
    shared['mla_qn'] = np.ascontiguousarray(f(inp['mla_q_norm'])[0].reshape(4, 128).T)
    shared['mla_kvn'] = np.ascontiguousarray(f(inp['mla_kv_norm'])[0].reshape(2, 128).T)
    uq = f(inp['mla_w_uq'])[0]
    shared['mla_uq'] = _kmaj(uq)
    ropecols = np.concatenate([np.arange(h * 192 + 128, h * 192 + 192) for h in range(8)])
    uqr = uq[:, ropecols]
    shared['mla_uqp'] = _kmaj(_swap_halves(uqr, 0, 512))
    shared['mla_ukv'] = _kmaj(f(inp['mla_w_ukv'])[0])
    shared['mla_wo'] = _kmaj(f(inp['mla_w_o'])[0])
    sq = f(inp['swa_w_qkv'])[0]
    shared['swa_qkv'] = _kmaj(sq)
    shared['swa_qkp'] = _kmaj(_swap_halves(sq, 0, 1280))
    shared['swa_sink'] = np.ascontiguousarray(np.broadcast_to(f(inp['swa_sink'])[0][None, :], (128, 16)))
    swo = f(inp['swa_w_o'])[0]
    t = np.zeros((128, 16, D), np.float32)
    t[:64] = swo.reshape(16, 64, D).transpose(1, 0, 2)
    shared['swa_wo'] = t
    shared['swa_mask'] = _swa_masks()
    shared['na_qkv'] = _kmaj(f(inp['na_w_qkv'])[0])
    nwo = f(inp['na_w_o'])[0]
    t2 = np.zeros((128, 16, D), np.float32)
    t2[:64] = nwo.reshape(16, 64, D).transpose(1, 0, 2)
    shared['na_wo'] = t2
    dq = f(inp['diff_w_qkv'])[0]
    shared['df_qkv'] = _kmaj(dq)
    shared['df_qkp'] = _kmaj(_swap_halves(dq, 0, 2048))
    shared['df_lam'] = np.ascontiguousarray(np.broadcast_to(f(inp['diff_lambda'])[0].reshape(1, 256), (128, 256)))
    shared['df_ng'] = np.ascontiguousarray(f(inp['diff_norm_g'])[0].reshape(128, 1))
    shared['df_wo'] = _kmaj(f(inp['diff_w_o'])[0])
    rpb = f(inp['na_rpb'])[0]
    maps = []
    for core in range(ncores):
        b, h = core // 2, core % 2
        m = dict(shared)
        tok = np.concatenate([x[b, h * TL:(h + 1) * TL], ctx[b]], 0)
        m['xT_in'] = np.ascontiguousarray(tok.T.reshape(8, 128, T).transpose(1, 0, 2))
        cc = np.stack([c[b], c_ctx], -1)
        m['cc_in'] = np.ascontiguousarray(cc.reshape(8, 128, 2).transpose(1, 0, 2))
        m['cos4'], m['sin4'] = _rope_tables(h * TL, TL)
        hv = np.zeros((128, 2), np.float32)
        hv[:, 0] = 1.0 if h == 1 else 0.0
        hv[:, 1] = 1.0 if h == 0 else 0.0
        m['halo_v'] = hv
        E, rv = _na_tables(rpb, h)
        m['na_rpbE'] = E
        m['na_rv'] = rv
        maps.append(m)
    return maps


_NC_CACHE = {}


def kernel(**inputs):
    maps = prepare_inputs(inputs)
    if 'nc' not in _NC_CACHE:
        _NC_CACHE['nc'] = build()
    nc = _NC_CACHE['nc']
    names = set(_INPUT_NAMES)
    in_maps = [{k: v for k, v in m.items() if k in names} for m in maps]
    res = run_bass_kernel_spmd(nc, in_maps, core_ids=list(range(8)))
    out = np.zeros((4, 4096, D), np.float32)
    for core in range(8):
        b, h = core // 2, core % 2
        oT = np.asarray(res.results[core]["outT"]).reshape(128, 8, TL)
        out[b, h * TL:(h + 1) * TL] = oT.transpose(2, 1, 0).reshape(TL, D)
    return out


_INPUT_NAMES = ["xT_in", "cc_in", "modw", "modb", "ng_in", "fng_in", "win", "wout", "cos4", "sin4", "mla_down",
                "mla_downp", "mla_qn", "mla_kvn", "mla_uq", "mla_uqp", "mla_ukv", "mla_wo", "swa_qkv", "swa_qkp",
                "swa_sink", "swa_wo", "swa_mask", "halo_v", "na_qkv", "na_rpbE", "na_rv", "na_wo", "df_qkv", "df_qkp",
                "df_lam", "df_ng", "df_wo"]
```

```python
import numpy as np
from contextlib import ExitStack
import concourse.bass as bass
import concourse.mybir as mybir
from concourse.bass_utils import run_bass_kernel_spmd

F32 = mybir.dt.float32
BF16 = mybir.dt.bfloat16
AF = mybir.ActivationFunctionType
ALU = mybir.AluOpType

D = 1024
KD = 8
TL = 2048
TCX = 256
T = TL + TCX
DFF = 2816
CH = [(0, 512), (512, 512), (1024, 512), (1536, 512), (2048, 256)]
EPS = 1e-6
NEGFILL = -30000.0
ENGS = ['sp', 'act', 'pe', 'dve', 'pool']
DEBUG = False


class Op:
    __slots__ = ('eng', 'fn', 'r', 'w', 'dma', 'inc', 'waits', 'signal', 'val', 'fdeps')

    def __init__(self, eng, fn, r, w, dma, inc):
        self.eng, self.fn, self.r, self.w, self.dma, self.inc = eng, fn, r, w, dma, inc
        self.waits = []
        self.signal = False
        self.val = 0
        self.fdeps = ()


class Prog:
    def __init__(self):
        self.ops = []

    def add(self, eng, fn, r=(), w=(), dma=None, inc=16):
        self.ops.append(Op(eng, fn, tuple(r), tuple(w), dma, inc))

    def analyse(self):
        last_w = {}
        rd_eng = {}
        rd_dma = {}
        waited = {e: {} for e in ENGS}
        dma_cnt = {}
        ops = self.ops
        for j, op in enumerate(ops):
            deps = set()
            for b in op.r:
                i = last_w.get(b)
                if i is not None:
                    deps.add(i)
            for b in op.w:
                i = last_w.get(b)
                if i is not None:
                    deps.add(i)
                for i in rd_eng.get(b, {}).values():
                    deps.add(i)
                for i in rd_dma.get(b, ()):
                    deps.add(i)
            E = op.eng
            wt = waited[E]
            more = set()
            for i in deps:
                if ops[i].fn is None:
                    more.update(ops[i].fdeps)
            deps = set(i for i in deps if ops[i].fn is not None) | more
            if op.fn is None:
                op.fdeps = tuple(deps)
            for i in sorted(deps):
                p = ops[i]
                if p.dma is None:
                    if p.eng == E and E in ('pe', 'sp', 'pool'):
                        continue
                    if wt.get(p.eng, -1) >= i:
                        continue
                    wt[p.eng] = i
                    p.signal = True
                    op.waits.append(('c', i))
                else:
                    if wt.get(p.dma, 0) >= p.val:
                        continue
                    wt[p.dma] = p.val
                    op.waits.append(('d', p.dma, p.val))
            if op.dma is not None:
                dma_cnt[op.dma] = dma_cnt.get(op.dma, 0) + op.inc
                op.val = dma_cnt[op.dma]
            for b in op.r:
                if op.dma is not None:
                    rd_dma.setdefault(b, []).append(j)
                else:
                    rd_eng.setdefault(b, {})[E] = j
            for b in op.w:
                last_w[b] = j
                rd_eng[b] = {}
                rd_dma[b] = []
        cnt = {e: 0 for e in ENGS}
        for op in ops:
            if op.dma is None and op.signal:
                cnt[op.eng] += 1
                op.val = cnt[op.eng]
        self.dma_keys = sorted(dma_cnt.keys())
        return cnt

    def emit(self, nc, es):
        self.analyse()
        esem = {e: es.enter_context(nc.semaphore('s_' + e)) for e in ENGS}
        dsem = {k: es.enter_context(nc.semaphore('d_' + k)) for k in self.dma_keys}
        ops = self.ops
        per = {e: [op for op in ops if op.eng == e] for e in ENGS}

        def replay(ename, eng):
            for op in per[ename]:
                for w in op.waits:
                    if w[0] == 'c':
                        p = ops[w[1]]
                        eng.wait_ge(esem[p.eng], p.val)
                    else:
                        eng.wait_ge(dsem[w[1]], w[2])
                if op.fn is None:
                    continue
                ins = op.fn(eng)
                if op.dma is not None:
                    ins.then_inc(dsem[op.dma], op.inc)
                elif op.signal:
                    ins.then_inc(esem[op.eng], 1)

        with nc.Block() as block:
            @block.sync
            def _(e):
                replay('sp', e)

            @block.scalar
            def _(e):
                replay('act', e)

            @block.tensor
            def _(e):
                replay('pe', e)

            @block.vector
            def _(e):
                replay('dve', e)

            @block.gpsimd
            def _(e):
                replay('pool', e)


class Rot:
    def __init__(self, name, tiles):
        self.name, self.tiles, self.i = name, tiles, 0

    def get(self):
        i = self.i
        self.i = (i + 1) % len(self.tiles)
        return self.tiles[i], (self.name, i)


def build(layers=(0, 1, 2, 3), pair_groups=((0, 1), (2, 3), (4, 5), (6, 7)), do_mixer=True):
    nc = bass.Bass("TRN2", target_bir_lowering=False)
    P = Prog()
    es = ExitStack()

    def din(name, shape, dt=F32):
        return nc.dram_tensor(name, list(shape), dt, kind="ExternalInput")

    def dscr(name, shape, dt=BF16):
        return nc.dram_tensor(name, list(shape), dt)

    xT_in = din("xT_in", [128, KD, T])
    cc_in = din("cc_in", [128, KD, 2])
    modw = din("modw", [4, 128, KD, 9216])
    modb = din("modb", [128, 4, 72])
    ng_in = din("ng_in", [128, 4, 3, KD])
    fng_in = din("fng_in", [128, KD])
    win_d = din("win", [4, 2, 128, KD, 2 * DFF])
    wout_d = din("wout", [4, 2, 128, 22, D])
    cos_d = din("cos4", [128, TL])
    sin_d = din("sin4", [128, TL])
    mla_down = din("mla_down", [128, KD, 832])
    mla_downp = din("mla_downp", [128, KD, 64])
    mla_qn = din("mla_qn", [128, 4])
    mla_kvn = din("mla_kvn", [128, 2])
    mla_uq = din("mla_uq", [128, 4, 1536])
    mla_uqp = din("mla_uqp", [128, 4, 512])
    mla_ukv = din("mla_ukv", [128, 2, 2048])
    mla_wo = din("mla_wo", [128, 8, D])
    swa_qkv = din("swa_qkv", [128, KD, 1536])
    swa_qkp = din("swa_qkp", [128, KD, 1280])
    swa_sink = din("swa_sink", [128, 16])
    swa_wo = din("swa_wo", [128, 16, D])
    swa_mask = din("swa_mask", [128, 6, 512], BF16)
    halo_v = din("halo_v", [128, 2])
    na_qkv = din("na_qkv", [128, KD, 3072])
    na_rpbE = din("na_rpbE", [16, 128, 22 * 64])
    na_rv = din("na_rv", [128, 4 * 8 * 8])
    na_wo = din("na_wo", [128, 16, D])
    df_qkv = din("df_qkv", [128, KD, 3072])
    df_qkp = din("df_qkp", [128, KD, 2048])
    df_lam = din("df_lam", [128, 256])
    df_ng = din("df_ng", [128, 1])
    df_wo = din("df_wo", [128, 8, D])

    outT = nc.dram_tensor("outT", [128, KD, TL], F32, kind="ExternalOutput")

    qx_scr = dscr("qx_scr", [2048, TL])
    qp_scr = dscr("qp_scr", [2048, T])
    kc_scr = dscr("kc_scr", [1152, TCX])
    vc_scr = dscr("vc_scr", [TCX, 1024])
    KR = {0: 1088, 1: 256, 2: 1024, 3: 1024}
    VC = {0: 1024, 1: 256, 2: 1024, 3: 1024}
    bar_in = dscr("bar_in", [128, 128])
    bar_out = dscr("bar_out", [256, 128])
    kparts = {}
    vparts = {}
    for l in layers:
        kparts[l] = []
        r0 = 0
        while r0 < KR[l]:
            nr = min(512, KR[l] - r0)
            kparts[l].append((r0, nr, dscr("kmine%d_%d" % (l, r0), [nr, TL]), dscr("kall%d_%d" % (l, r0), [2 * nr, TL])))
            r0 += nr
        vparts[l] = [(dscr("vmine%d_%d" % (l, j), [1024, VC[l]]), dscr("vall%d_%d" % (l, j), [2048, VC[l]]))
                     for j in range(2)]

    def kmine_rows(l, row0, nrows):
        for (r0, nr, mine, all_) in kparts[l]:
            if r0 <= row0 and row0 + nrows <= r0 + nr:
                return mine[row0 - r0:row0 - r0 + nrows, :]
        raise AssertionError((l, row0, nrows))

    def kall_rows(l, r, row0, nrows):
        for (r0, nr, mine, all_) in kparts[l]:
            if r0 <= row0 and row0 + nrows <= r0 + nr:
                return all_[r * nr + row0 - r0:r * nr + row0 - r0 + nrows, :]
        raise AssertionError((l, row0, nrows))

    def vmine_rows(l, t0, nt):
        j = t0 // 1024
        assert t0 + nt <= (j + 1) * 1024
        return vparts[l][j][0][t0 - j * 1024:t0 - j * 1024 + nt, :]

    def vall_rows(l, r, t0, nt):
        j = t0 // 1024
        assert t0 + nt <= (j + 1) * 1024
        return vparts[l][j][1][r * 1024 + t0 - j * 1024:r * 1024 + t0 - j * 1024 + nt, :]

    ARENA = 212000
    arena = es.enter_context(nc.sbuf_tensor("arena", [128, ARENA // 4], F32))
    a0 = nc.lookup_mloc(arena).addr
    cur = [0]

    def sb(name, shape, dt=F32, at=None):
        nb = int(np.prod(shape[1:])) * (4 if dt == F32 else 2)
        nb = (nb + 31) // 32 * 32
        if at is None:
            off = cur[0]
            cur[0] += nb
            assert cur[0] <= ARENA, (name, cur[0])
        else:
            off = at
        return nc.alloc_sbuf_tensor_at(name, list(shape), dt, offset=a0 + off)

    def region(nbytes):
        off = cur[0]
        cur[0] += nbytes
        assert cur[0] <= ARENA, ('region', cur[0])
        return off

    xT = sb("xT", [128, KD, T])
    R1 = region(36864)
    R2 = region(27648)
    hT = sb("hT", [128, KD, T], BF16, at=R1)
    win_p = Rot("win", [sb("win%d" % i, [128, KD, 512], BF16) for i in range(3)])
    wout_p = Rot("wout", [sb("wout%d" % i, [128, 2, D], BF16) for i in range(2)])
    modw_p = Rot("modw", [sb("modw%d" % i, [128, KD, 128], F32, at=R2 + i * 4096) for i in range(2)])
    tmp_p = Rot("tmp", [sb("tmp%d" % i, [128, 512]) for i in range(4)])
    rs_p = Rot("rs", [sb("rs%d" % i, [128, 512]) for i in range(2)])
    cs_p = Rot("cs", [sb("cs%d" % i, [128, 2, 512]) for i in range(1)])
    act_p = Rot("act", [sb("act%d" % i, [128, 2, 512], BF16) for i in range(2)])
    pt_p = Rot("pt", [sb("pt%d" % i, [128, 512], BF16) for i in range(3)])
    stg_p = Rot("stg", [sb("stg%d" % i, [128, 512], BF16) for i in range(3)])
    condT = sb("condT", [128, KD, 2])
    modT = sb("modT", [128, 72, 2])
    modb_s = sb("modb_s", [128, 4, 72])
    ng_s = sb("ng_s", [128, 4, 3, KD])
    fng_s = sb("fng_s", [128, KD])
    gs_s = sb("gs_s", [128, 3, KD, 2])
    gt_s = sb("gt_s", [128, 3, KD, 2])
    ones_f = sb("ones_f", [128, 128])
    ones_b = sb("ones_b", [128, 128], BF16)
    small = sb("small", [128, 64])
    eps_t = sb("eps_t", [128, 8])
    q0 = sb("q0", [128, T], BF16, at=R1)
    q1 = sb("q1", [128, T], BF16, at=R1 + 4608)
    kx = sb("kx", [128, 2 * TL], BF16, at=R1 + 9216)
    krx = sb("krx", [128, 2 * TL], BF16, at=R1 + 17408)
    kcx = sb("kcx", [128, TCX], BF16, at=R1 + 25600)
    krc = sb("krc", [128, TCX], BF16, at=R1 + 26112)
    vt = sb("vt", [128, 34, 128], BF16, at=R1 + 26624)
    assert 26624 + 8704 <= 36864
    q2 = sb("q2", [128, TL], BF16)
    oh = sb("oh", [128, T], BF16)
    ATT_KEYS = ['q0', 'q1', 'kx', 'krx', 'kcx', 'krc', 'vt']
    HT_KEYS = [('hT', ci) for ci in range(5)]
    R2_KEYS = [('modw', 0), ('modw', 1), 'o0f', 'swa_m', 'na_E', 'na_rv'] + [('cq', ci) for ci in range(5)] + \
              [('ckv', ci) for ci in range(5)]

    def fence(engs, keys):
        for e_ in engs:
            P.add(e_, None, [], keys)

    ps_p = Rot("ps", [es.enter_context(nc.psum_tensor("ps%d" % i, [128, 512], F32)) for i in range(4)])
    acc_p = Rot("acc", [es.enter_context(nc.psum_tensor("acc%d" % i, [128, 512], F32)) for i in range(3)])
    pmod = es.enter_context(nc.psum_tensor("pmod", [128, 72, 2], F32))

    def ACT(out, in_, func, r, w, **kw):
        P.add('act', lambda e: e.activation(out=out, in_=in_, func=func, **kw), r, w)

    def MM(out, lhsT, rhs, start, stop, r, w):
        P.add('pe', lambda e: e.matmul(out, lhsT, rhs, start=start, stop=stop), r, w)

    def TT(out, in0, in1, op, r, w):
        P.add('dve', lambda e: e.tensor_tensor(out=out, in0=in0, in1=in1, op=op), r, w)

    def TS(out, in0, s1, s2, op0, op1, r, w):
        if op1 is None:
            P.add('dve', lambda e: e.tensor_scalar(out=out, in0=in0, scalar1=s1, scalar2=0.0, op0=op0, op1=ALU.add), r, w)
        else:
            P.add('dve', lambda e: e.tensor_scalar(out=out, in0=in0, scalar1=s1, scalar2=s2, op0=op0, op1=op1), r, w)

    def STT(out, in0, scalar, in1, op0, op1, r, w):
        P.add('dve', lambda e: e.scalar_tensor_tensor(out=out, in0=in0, scalar=scalar, in1=in1, op0=op0, op1=op1), r, w)

    def CP(out, in_, r, w):
        P.add('dve', lambda e: e.tensor_copy(out=out, in_=in_), r, w)

    def RCP(out, in_, r, w):
        P.add('dve', lambda e: e.reciprocal(out=out, in_=in_), r, w)

    def DMA(q, out, in_, key, r, w):
        P.add(q, lambda e: e.dma_start(out=out, in_=in_), r, w, dma=key)

    def MEMSET(ap, v, w):
        P.add('dve', lambda e: e.memset(ap, v), (), w)

    def xkeys(ci):
        return [('xT', ci, k) for k in range(KD)]

    MEMSET(ones_f[:, :], 1.0, ['ones_f'])
    MEMSET(ones_b[:, :], 1.0, ['ones_b'])
    MEMSET(eps_t[:, :], EPS, ['eps_t'])
    for ci, (c0, n) in enumerate(CH):
        DMA('sp', xT[:, :, c0:c0 + n], xT_in[:, :, c0:c0 + n], 'xin%d' % ci, [], xkeys(ci))
    DMA('sp', condT[:, :, :], cc_in[:, :, :], 'misc_c', [], ['condT'])
    DMA('sp', modb_s[:, :, :], modb[:, :, :], 'misc', [], ['modb'])
    DMA('sp', ng_s[:, :, :, :], ng_in[:, :, :, :], 'misc', [], ['ng'])
    DMA('sp', fng_s[:, :], fng_in[:, :], 'misc', [], ['fng'])
    ACT(condT[:, :, :], condT[:, :, :], AF.Silu, ['condT'], ['condT'])

    def mod_phase(l):
        for j in range(72):
            mw, mk = modw_p.get()
            DMA('sp', mw[:, :, :], modw[l, :, :, j * 128:(j + 1) * 128], 'modw%d' % mk[1], [], [mk])
            for k in range(KD):
                MM(pmod[:, j, :], mw[:, k, :], condT[:, k, :], k == 0, k == KD - 1, [mk, 'condT'], ['pmod'])
        for s in range(2):
            TT(modT[:, :, s], pmod[:, :, s], modb_s[:, l, :], ALU.add, ['pmod', 'modb'], ['modT'])
        for i in range(3):
            for s in range(2):
                STT(gs_s[:, i, :, s], modT[:, (3 * i + 1) * 8:(3 * i + 2) * 8, s], 1.0, ng_s[:, l, i, :],
                    ALU.add, ALU.mult, ['modT', 'ng'], ['gs'])
                TS(gt_s[:, i, :, s], modT[:, (3 * i + 2) * 8:(3 * i + 3) * 8, s], 0.5 if i != 1 else 1.0, None,
                   ALU.mult, None, ['modT'], ['gt'])

    def norm_phase(sub):
        for ci, (c0, n) in enumerate(CH):
            s = 1 if ci == 4 else 0
            ss, ssk = ps_p.get()
            for k in range(KD):
                sq, sqk = tmp_p.get()
                ACT(sq[:, :n], xT[:, k, c0:c0 + n], AF.Square, [('xT', ci, k)], [sqk])
                MM(ss[:, :n], ones_f[:, :], sq[:, :n], k == 0, k == KD - 1, [sqk, 'ones_f'], [ssk])
            rstd, rk = rs_p.get()
            ACT(rstd[:, :n], ss[:, :n], AF.Sqrt, [ssk], [rk], scale=1.0 / D, bias=eps_t[:, 0:1])
            RCP(rstd[:, :n], rstd[:, :n], [rk], [rk])
            for k in range(KD):
                t, tk = tmp_p.get()
                TT(t[:, :n], xT[:, k, c0:c0 + n], rstd[:, :n], ALU.mult, [('xT', ci, k), rk], [tk])
                ACT(hT[:, k, c0:c0 + n], t[:, :n], AF.Identity, [tk, 'gs', 'modT'], [('hT', ci)],
                    scale=gs_s[:, sub, k, s:s + 1], bias=modT[:, 3 * sub * 8 + k, s:s + 1])

    def ffn_phase(l, f, sub):
        for g in range(11):
            win, wk = win_p.get()
            wo_, wok = wout_p.get()
            DMA('pool', win[:, :, 0:256], win_d[l, f, :, :, g * 256:(g + 1) * 256], 'win%d' % wk[1], [], [wk])
            DMA('pool', win[:, :, 256:512], win_d[l, f, :, :, DFF + g * 256:DFF + (g + 1) * 256], 'win%d' % wk[1], [], [wk])
            DMA('pool', wo_[:, :, :], wout_d[l, f, :, 2 * g:2 * g + 2, :], 'wout%d' % wok[1], [], [wok])
            def stage_a(ci):
                c0, n = CH[ci]
                a, ak = act_p.get()
                for fb in range(2):
                    pg, pgk = ps_p.get()
                    pu, puk = ps_p.get()
                    for k in range(KD):
                        MM(pg[:, :n], win[:, k, fb * 128:(fb + 1) * 128], hT[:, k, c0:c0 + n], k == 0, k == KD - 1,
                           [wk, ('hT', ci)], [pgk])
                    for k in range(KD):
                        MM(pu[:, :n], win[:, k, 256 + fb * 128:256 + (fb + 1) * 128], hT[:, k, c0:c0 + n], k == 0,
                           k == KD - 1, [wk, ('hT', ci)], [puk])
                    sg, sgk = tmp_p.get()
                    ACT(sg[:, :n], pg[:, :n], AF.Silu, [pgk], [sgk])
                    TT(a[:, fb, :n], sg[:, :n], pu[:, :n], ALU.mult, [sgk, puk], [ak])
                return a, ak

            def stage_b(ci, a, ak):
                c0, n = CH[ci]
                s = 1 if ci == 4 else 0
                for dk in range(KD):
                    po, pok = acc_p.get()
                    for fb in range(2):
                        MM(po[:, :n], wo_[:, fb, dk * 128:(dk + 1) * 128], a[:, fb, :n], fb == 0, fb == 1, [wok, ak], [pok])
                    STT(xT[:, dk, c0:c0 + n], po[:, :n], gt_s[:, sub, dk, s:s + 1], xT[:, dk, c0:c0 + n],
                        ALU.mult, ALU.add, [pok, 'gt', ('xT', ci, dk)], [('xT', ci, dk)])

            prev = stage_a(0)
            for ci in range(1, len(CH)):
                cur_ = stage_a(ci)
                stage_b(ci - 1, *prev)
                prev = cur_
            stage_b(len(CH) - 1, *prev)

    def load_w(src_ap, K, ncols):
        w, wk = win_p.get()
        DMA('pool', w[:, 0:K, 0:ncols], src_ap, 'win%d' % wk[1], [], [wk])
        return w, wk

    def load_cs(ci):
        c0, n = CH[ci]
        cs, ck = cs_p.get()
        DMA('sp', cs[:, 0, :n], cos_d[:, c0:c0 + n], 'cs', [], [ck])
        DMA('sp', cs[:, 1, :n], sin_d[:, c0:c0 + n], 'cs', [], [ck])
        return cs, ck

    def proj_fm(src, srckey, K, wA, wAk, colA, M, ci, wB=None, wBk=None, colB=0):
        c0, n = CH[ci]
        pa, pak = ps_p.get()
        for k in range(K):
            MM(pa[:M, :n], wA[:, k, colA:colA + M], src[:, k, c0:c0 + n], k == 0, k == K - 1, [wAk, srckey(ci)], [pak])
        if wB is None:
            return pa, pak, None, None
        pb, pbk = ps_p.get()
        for k in range(K):
            MM(pb[:M, :n], wB[:, k, colB:colB + M], src[:, k, c0:c0 + n], k == 0, k == K - 1, [wBk, srckey(ci)], [pbk])
        return pa, pak, pb, pbk

    def rope_to(dst_ap, M, n, pa, pak, pb, pbk, cs, ck, dkey):
        t1, t1k = tmp_p.get()
        t2, t2k = tmp_p.get()
        TT(t1[:M, :n], pa[:M, :n], cs[:M, 0, :n], ALU.mult, [pak, ck], [t1k])
        TT(t2[:M, :n], pb[:M, :n], cs[:M, 1, :n], ALU.mult, [pbk, ck], [t2k])
        TT(dst_ap, t1[:M, :n], t2[:M, :n], ALU.add, [t1k, t2k], [dkey])

    def store_rows(dram_ap, sb_ap, sbkey, dkey):
        DMA('sp', dram_ap, sb_ap, dkey if isinstance(dkey, str) else dkey[0], [sbkey], [dkey])

    def hkey(ci):
        return ('hT', ci)

    def exchange(l):
        fence(['sp'], ATT_KEYS + HT_KEYS)
        grp = [list(g) for g in pair_groups]
        P.add('pool', lambda e: e.collective_compute(
            "AllGather", ALU.bypass, replica_groups=grp, ins=[bar_in.ap().opt()], outs=[bar_out.ap().opt()]),
            ['kmine', 'vmine'], ['kmine', 'vmine'], dma='ccp%d' % l, inc=1)
        for pi_, (r0, nr, mine, all_) in enumerate(kparts[l]):
            P.add('pool', lambda e, mine=mine, all_=all_: e.collective_compute(
                "AllGather", ALU.bypass, replica_groups=grp, ins=[mine.ap().opt()], outs=[all_.ap().opt()]),
                ['kmine'], [('kallp', pi_)], dma='cck%d_%d' % (l, pi_), inc=1)
        for pi_, (mine, all_) in enumerate(vparts[l]):
            P.add('pool', lambda e, mine=mine, all_=all_: e.collective_compute(
                "AllGather", ALU.bypass, replica_groups=grp, ins=[mine.ap().opt()], outs=[all_.ap().opt()]),
                ['vmine'], [('vallp', pi_)], dma='ccv%d_%d' % (l, pi_), inc=1)
        P.add('pool', lambda e: e.collective_compute(
            "AllGather", ALU.bypass, replica_groups=grp, ins=[bar_in.ap().opt()], outs=[bar_out.ap().opt()]),
            [('kallp', i) for i in range(len(kparts[l]))] + [('vallp', i) for i in range(2)], ['kall', 'vall'],
            dma='ccb%d' % l, inc=1)

    def v_proj_tm(l, w, wk, col0, ncols, vcol0):
        for tb in range(T // 128):
            ci = min(tb // 4, 4)
            pv, pvk = ps_p.get()
            for k in range(KD):
                MM(pv[:, :ncols], hT[:, k, tb * 128:(tb + 1) * 128], w[:, k, col0:col0 + ncols], k == 0, k == KD - 1,
                   [wk, ('hT', ci)], [pvk])
            st, sk = stg_p.get()
            CP(st[:, :ncols], pv[:, :ncols], [pvk], [sk])
            if tb < 16:
                store_rows(vmine_rows(l, tb * 128, 128)[:, vcol0:vcol0 + ncols], st[:, :ncols], sk, 'vmine')
            else:
                store_rows(vc_scr[(tb - 16) * 128:(tb - 15) * 128, vcol0:vcol0 + ncols], st[:, :ncols], sk, 'vc_scr')

    def qk_tile(l, ci, M, pa, pak, pb, pbk, cs, ck, dst_rot, dst_plain, rot_key, plain_key):
        c0, n = CH[ci]
        if dst_plain is not None:
            st, sk = stg_p.get()
            CP(st[:M, :n], pa[:M, :n], [pak], [sk])
            store_rows(dst_plain, st[:M, :n], sk, plain_key)
        if dst_rot is not None:
            st, sk = stg_p.get()
            rope_to(st[:M, :n], M, n, pa, pak, pb, pbk, cs, ck, sk)
            store_rows(dst_rot, st[:M, :n], sk, rot_key)

    def attn_unit(scale, passes_x, passes_c, xblocks, v_of, dv, qcols, o_dst, o_key, final, mask=None,
                  extra_r=()):
        for ci, (c0, n) in enumerate(CH):
            if ci == 4:
                blocks = []
            else:
                blocks = xblocks(ci)
            po, pok = acc_p.get()
            pd, pdk = acc_p.get()
            seq = [('c', 0)] + [('x', b) for b in blocks] + [('c', 1)]

            def stage_a(bi):
                kind, b = seq[bi]
                if kind == 'c':
                    kc0, vb, (qa, qb), mspec = b * 128, 32 + b, (0, n), None
                    passes = passes_c
                else:
                    kc0, vb, (qa, qb), mspec = b
                    passes = passes_x
                st_, stk = ps_p.get()
                for pi, (kt, qt, rows, kkey, qkey) in enumerate(passes):
                    MM(st_[:, qa:qb], kt[:rows, kc0:kc0 + 128], qt[:rows, c0 + qa:c0 + qb], pi == 0,
                       pi == len(passes) - 1, [kkey, qkey], [stk])
                pt, ptk = pt_p.get()
                ACT(pt[:, qa:qb], st_[:, qa:qb], AF.Exp, [stk], [ptk], scale=scale)
                if mspec is not None:
                    mask(pt, ptk, qa, qb, mspec)
                return (pt, ptk, vb, qa, qb)

            def stage_b(bi, st):
                pt, ptk, vb, qa, qb = st
                first = bi == 0
                last = bi == len(seq) - 1
                MM(po[:dv, qa:qb], v_of(vb), pt[:, qa:qb], first, last, ['vt', ptk], [pok])
                MM(pd[:dv, qa:qb], ones_b[:, :dv], pt[:, qa:qb], first, last, ['ones_b', ptk], [pdk])

            LA = 2
            pend = []
            for bi in range(len(seq)):
                pend.append((bi, stage_a(bi)))
                if len(pend) > LA:
                    b0, st0 = pend.pop(0)
                    stage_b(b0, st0)
            for b0, st0 in pend:
                stage_b(b0, st0)
            final(ci, c0, n, po, pok, pd, pdk)

    def plain_final(dst_tile, dst_key, dv, den_add=None):
        def fin(ci, c0, n, po, pok, pd, pdk):
            r, rk = tmp_p.get()
            if den_add is not None:
                TS(r[:dv, :n], pd[:dv, :n], den_add, None, ALU.add, None, [pdk, 'small'], [rk])
                RCP(r[:dv, :n], r[:dv, :n], [rk], [rk])
            else:
                RCP(r[:dv, :n], pd[:dv, :n], [pdk], [rk])
            TT(dst_tile[:dv, c0:c0 + n], po[:dv, :n], r[:dv, :n], ALU.mult, [pok, rk], [(dst_key, ci)])
        return fin

    def wo_apply(l, wo_d, src_tile, src_key, row_blk, rows):
        w, wk = wout_p.get()
        DMA('pool', w[:rows, 0, :], wo_d[0:rows, row_blk, :] if rows == 128 else wo_d[0:rows, row_blk, :],
            'wout%d' % wk[1], [], [wk])
        for ci, (c0, n) in enumerate(CH):
            s = 1 if ci == 4 else 0
            for dk in range(KD):
                po, pok = ps_p.get()
                MM(po[:, :n], w[:rows, 0, dk * 128:(dk + 1) * 128], src_tile[:rows, c0:c0 + n], True, True,
                   [wk, (src_key, ci)], [pok])
                STT(xT[:, dk, c0:c0 + n], po[:, :n], gt_s[:, 1, dk, s:s + 1], xT[:, dk, c0:c0 + n],
                    ALU.mult, ALU.add, [pok, 'gt', ('xT', ci, dk)], [('xT', ci, dk)])

    def load_v(l, col0, dv):
        for r in range(2):
            for j in range(2):
                DMA('sp', vt[:, 16 * r + 8 * j:16 * r + 8 * j + 8, :dv],
                    vall_rows(l, r, j * 1024, 1024)[:, col0:col0 + dv].rearrange("(b p) c -> p b c", p=128),
                    'vt', ['vall'], ['vt'])
        DMA('sp', vt[:, 32:34, :dv], vc_scr[:, col0:col0 + dv].rearrange("(b p) c -> p b c", p=128), 'vt',
            ['vc_scr'], ['vt'])

    def load_kx(l, tile, key, row0, rows):
        for r in range(2):
            DMA('sp', tile[:rows, r * TL:(r + 1) * TL], kall_rows(l, r, row0, rows), key, ['kall'], [key])

    def full_blocks(ci):
        n = CH[ci][1]
        return [(kb * 128, kb, (0, n), None) for kb in range(32)]

    def diff_mixer(l, lam_init):
        scale = 64 ** -0.5
        lam = small[:, 0:1]
        DMA('sp', tmp_lam[:, :], df_lam[:, :], 'misc', [], ['lamraw'])
        DMA('sp', small[:, 8:9], df_ng[:, :], 'misc', [], ['small'])
        TT(tmp_lam[:, 0:64], tmp_lam[:, 0:64], tmp_lam[:, 64:128], ALU.mult, ['lamraw'], ['lamraw'])
        TT(tmp_lam[:, 128:192], tmp_lam[:, 128:192], tmp_lam[:, 192:256], ALU.mult, ['lamraw'], ['lamraw'])
        P.add('dve', lambda e: e.reduce_sum(out=small[:, 1:2], in_=tmp_lam[:, 0:64], axis=mybir.AxisListType.X),
              ['lamraw'], ['small'])
        P.add('dve', lambda e: e.reduce_sum(out=small[:, 2:3], in_=tmp_lam[:, 128:192], axis=mybir.AxisListType.X),
              ['lamraw'], ['small'])
        ACT(small[:, 1:3], small[:, 1:3], AF.Exp, ['small'], ['small'])
        TT(small[:, 0:1], small[:, 1:2], small[:, 2:3], ALU.subtract, ['small'], ['small'])
        TS(small[:, 0:1], small[:, 0:1], lam_init, None, ALU.add, None, ['small'], ['small'])
        TS(small[:, 3:4], small[:, 0:1], -1.0, None, ALU.mult, None, ['small'], ['small'])
        TS(small[:, 9:10], small[:, 8:9], 1.0 - lam_init, None, ALU.mult, None, ['small'], ['small'])

        for part, dst_rot, dst_plain_l, dst_plain_c in (('q', (lambda a, b: qx_scr[a:a + b, :]), qp_scr, qp_scr), ('k', (lambda a, b: kmine_rows(l, a, b)), None, kc_scr)):
            base = 0 if part == 'q' else 1024
            for half in range(2):
                wA, wAk = load_w(df_qkv[:, :, base + half * 512:base + (half + 1) * 512], KD, 512)
                wB, wBk = load_w(df_qkp[:, :, base + half * 512:base + (half + 1) * 512], KD, 512)
                for ci, (c0, n) in enumerate(CH):
                    cs, ck = load_cs(ci) if ci < 4 else (None, None)
                    for t in range(4):
                        row0 = (half * 4 + t) * 128
                        if ci < 4:
                            pa, pak, pb, pbk = proj_fm(hT, hkey, KD, wA, wAk, t * 128, 128, ci, wB, wBk, t * 128)
                            qk_tile(l, ci, 128, pa, pak, pb, pbk, cs, ck,
                                    dst_rot(row0, 128)[:, c0:c0 + n],
                                    dst_plain_l[row0:row0 + 128, c0:c0 + n] if dst_plain_l is not None else None,
                                    'qx_scr' if part == 'q' else 'kmine', 'qp_scr')
                        else:
                            pa, pak, _, _ = proj_fm(hT, hkey, KD, wA, wAk, t * 128, 128, ci)
                            if part == 'q':
                                qk_tile(l, ci, 128, pa, pak, None, None, None, None, None,
                                        qp_scr[row0:row0 + 128, c0:c0 + n], None, 'qp_scr')
                            else:
                                qk_tile(l, ci, 128, pa, pak, None, None, None, None, None,
                                        kc_scr[row0:row0 + 128, 0:n], None, 'kc_scr')
        for half in range(2):
            w, wk = load_w(df_qkv[:, :, 2048 + half * 512:2048 + (half + 1) * 512], KD, 512)
            v_proj_tm(l, w, wk, 0, 512, half * 512)
        exchange(l)

        for h in range(8):
            load_v(l, h * 128, 128)
            for i in range(2):
                u = h * 2 + i
                qt_x, qt_p = (q0, q1)
                DMA('sp', q2[:64, :], qx_scr[u * 64:(u + 1) * 64, :], 'q2', ['qx_scr'], ['q2'])
                DMA('sp', q1[:64, :], qp_scr[u * 64:(u + 1) * 64, :], 'q1', ['qp_scr'], ['q1'])
                load_kx(l, kx, 'kx', u * 64, 64)
                DMA('sp', kcx[:64, :], kc_scr[u * 64:(u + 1) * 64, :], 'kcx', ['kc_scr'], ['kcx'])

                def fin(ci, c0, n, po, pok, pd, pdk, i=i):
                    r, rk = tmp_p.get()
                    RCP(r[:, :n], pd[:, :n], [pdk], [rk])
                    if i == 0:
                        TT(o0f[:, c0:c0 + n], po[:, :n], r[:, :n], ALU.mult, [pok, rk], [('o0f', ci)])
                    else:
                        t, tk = tmp_p.get()
                        TT(t[:, :n], po[:, :n], r[:, :n], ALU.mult, [pok, rk], [tk])
                        STT(o0f[:, c0:c0 + n], t[:, :n], small[:, 3:4], o0f[:, c0:c0 + n], ALU.mult, ALU.add,
                            [tk, 'small', ('o0f', ci)], [('o0f', ci)])
                attn_unit(scale, [(kx, q2, 64, 'kx', 'q2')], [(kcx, q1, 64, 'kcx', 'q1')], full_blocks,
                          lambda vb: vt[:, vb, :], 128, None, None, None, fin)
            for ci, (c0, n) in enumerate(CH):
                tk = ('o0f', ci)
                sq, sqk = tmp_p.get()
                ACT(sq[:, :n], o0f[:, c0:c0 + n], AF.Square, [tk], [sqk])
                ss, ssk = ps_p.get()
                MM(ss[:, :n], ones_f[:, :], sq[:, :n], True, True, [sqk, 'ones_f'], [ssk])
                rs, rsk = rs_p.get()
                ACT(rs[:, :n], ss[:, :n], AF.Sqrt, [ssk], [rsk], scale=1.0 / 128, bias=eps_t[:, 0:1])
                RCP(rs[:, :n], rs[:, :n], [rsk], [rsk])
                STT(oh[:, c0:c0 + n], o0f[:, c0:c0 + n], small[:, 9:10], rs[:, :n], ALU.mult, ALU.mult,
                    [tk, rsk, 'small'], [('oh', ci)])
            wo_apply(l, df_wo, oh, 'oh', h, 128)

    tmp_lam = sb("tmp_lam", [128, 256])
    o0f = sb("o0f", [128, T], F32, at=R2 + 8192)

    cqn = None

    def mla_mixer(l):
        scale = 192 ** -0.5
        DMA('sp', small[:, 16:20], mla_qn[:, :], 'misc', [], ['small'])
        DMA('sp', small[:, 20:22], mla_kvn[:, :], 'misc', [], ['small'])
        wq_, wqk = load_w(mla_down[:, :, 0:512], KD, 512)
        wk_, wkk = load_w(mla_down[:, :, 512:832], KD, 320)
        for ci, (c0, n) in enumerate(CH):
            for (wt_, wtk, nblk, nrm_col, dst, dkey, inv) in ((wq_, wqk, 4, 16, cq_t, 'cq', 1.0 / 512),
                                                           (wk_, wkk, 2, 20, ckv_t, 'ckv', 1.0 / 256)):
                pss = []
                ss, ssk = acc_p.get()
                for b in range(nblk):
                    pa, pak, _, _ = proj_fm(hT, hkey, KD, wt_, wtk, b * 128, 128, ci)
                    pss.append((pa, pak))
                    sq, sqk = tmp_p.get()
                    ACT(sq[:, :n], pa[:, :n], AF.Square, [pak], [sqk])
                    MM(ss[:, :n], ones_f[:, :], sq[:, :n], b == 0, b == nblk - 1, [sqk, 'ones_f'], [ssk])
                rs, rsk = rs_p.get()
                ACT(rs[:, :n], ss[:, :n], AF.Sqrt, [ssk], [rsk], scale=inv, bias=eps_t[:, 0:1])
                RCP(rs[:, :n], rs[:, :n], [rsk], [rsk])
                for b in range(nblk):
                    pa, pak = pss[b]
                    STT(dst[:, b, c0:c0 + n], pa[:, :n], small[:, nrm_col + b:nrm_col + b + 1], rs[:, :n],
                        ALU.mult, ALU.mult, [pak, rsk, 'small'], [(dkey, ci)])
        wp_, wpk = load_w(mla_downp[:, :, :], KD, 64)
        for ci, (c0, n) in enumerate(CH):
            if ci < 4:
                cs, ck = load_cs(ci)
                pa, pak, pb, pbk = proj_fm(hT, hkey, KD, wk_, wkk, 256, 64, ci, wp_, wpk, 0)
                qk_tile(l, ci, 64, pa, pak, pb, pbk, cs, ck, kmine_rows(l, 1024, 64)[:, c0:c0 + n], None, 'kmine', None)
            else:
                pa, pak, _, _ = proj_fm(hT, hkey, KD, wk_, wkk, 256, 64, ci)
                qk_tile(l, ci, 64, pa, pak, None, None, None, None, None, kc_scr[1024:1088, 0:n], None, 'kc_scr')
        cqkey = lambda ci: ('cq', ci)
        ckvkey = lambda ci: ('ckv', ci)
        for h in range(8):
            wA, wAk = load_w(mla_uq[:, :, h * 192:(h + 1) * 192], 4, 192)
            wB, wBk = load_w(mla_uqp[:, :, h * 64:(h + 1) * 64], 4, 64)
            wkv, wkvk = load_w(mla_ukv[:, :, h * 256:(h + 1) * 256], 2, 256)
            for ci, (c0, n) in enumerate(CH):
                pa, pak, _, _ = proj_fm(cq_t, cqkey, 4, wA, wAk, 0, 128, ci)
                qk_tile(l, ci, 128, pa, pak, None, None, None, None, None, qp_scr[h * 256:h * 256 + 128, c0:c0 + n],
                        None, 'qp_scr')
                if ci < 4:
                    cs, ck = load_cs(ci)
                    pa, pak, pb, pbk = proj_fm(cq_t, cqkey, 4, wA, wAk, 128, 64, ci, wB, wBk, 0)
                    qk_tile(l, ci, 64, pa, pak, pb, pbk, cs, ck, qx_scr[h * 64:h * 64 + 64, c0:c0 + n],
                            qp_scr[h * 256 + 128:h * 256 + 192, c0:c0 + n], 'qx_scr', 'qp_scr')
                else:
                    pa, pak, _, _ = proj_fm(cq_t, cqkey, 4, wA, wAk, 128, 64, ci)
                    qk_tile(l, ci, 64, pa, pak, None, None, None, None, None,
                            qp_scr[h * 256 + 128:h * 256 + 192, c0:c0 + n], None, 'qp_scr')
                pa, pak, _, _ = proj_fm(ckv_t, ckvkey, 2, wkv, wkvk, 0, 128, ci)
                if ci < 4:
                    qk_tile(l, ci, 128, pa, pak, None, None, None, None, None, kmine_rows(l, h * 128, 128)[:, c0:c0 + n],
                            None, 'kmine')
                else:
                    qk_tile(l, ci, 128, pa, pak, None, None, None, None, None, kc_scr[h * 128:(h + 1) * 128, 0:n],
                            None, 'kc_scr')
            for tb in range(T // 128):
                ci = min(tb // 4, 4)
                pv, pvk = ps_p.get()
                for k in range(2):
                    MM(pv[:, :128], ckv_t[:, k, tb * 128:(tb + 1) * 128], wkv[:, k, 128:256], k == 0, k == 1,
                       [wkvk, ('ckv', ci)], [pvk])
                st, sk = stg_p.get()
                CP(st[:, :128], pv[:, :128], [pvk], [sk])
                if tb < 16:
                    store_rows(vmine_rows(l, tb * 128, 128)[:, h * 128:(h + 1) * 128], st[:, :128], sk, 'vmine')
                else:
                    store_rows(vc_scr[(tb - 16) * 128:(tb - 15) * 128, h * 128:(h + 1) * 128], st[:, :128], sk, 'vc_scr')
        exchange(l)
        load_kx(l, krx, 'krx', 1024, 64)
        DMA('sp', krc[:64, :], kc_scr[1024:1088, :], 'krc', ['kc_scr'], ['krc'])
        for h in range(8):
            load_v(l, h * 128, 128)
            DMA('sp', q0[:, :], qp_scr[h * 256:h * 256 + 128, :], 'q0', ['qp_scr'], ['q0'])
            DMA('sp', q1[:64, :], qp_scr[h * 256 + 128:h * 256 + 192, :], 'q1', ['qp_scr'], ['q1'])
            DMA('sp', q2[:64, :], qx_scr[h * 64:(h + 1) * 64, :], 'q2', ['qx_scr'], ['q2'])
            load_kx(l, kx, 'kx', h * 128, 128)
            DMA('sp', kcx[:, :], kc_scr[h * 128:(h + 1) * 128, :], 'kcx', ['kc_scr'], ['kcx'])
            attn_unit(scale, [(kx, q0, 128, 'kx', 'q0'), (krx, q2, 64, 'krx', 'q2')],
                      [(kcx, q0, 128, 'kcx', 'q0'), (krc, q1, 64, 'krc', 'q1')], full_blocks,
                      lambda vb: vt[:, vb, :], 128, None, None, None, plain_final(oh, 'oh', 128))
            wo_apply(l, mla_wo, oh, 'oh', h, 128)

    cq_t = sb("cq_t", [128, 4, T], BF16, at=R2)
    ckv_t = sb("ckv_t", [128, 2, T], BF16, at=R2 + 18432)

    swa_m = sb("swa_m", [128, 6, 512], BF16, at=R2 + 17408)
    halo_s = sb("halo_s", [128, 2])

    def swa_mixer(l):
        scale = 64 ** -0.5
        DMA('sp', swa_m[:, :, :], swa_mask[:, :, :], 'misc', [], ['swa_m'])
        DMA('sp', halo_s[:, :], halo_v[:, :], 'misc', [], ['halo_s'])
        DMA('sp', small[:, 24:40], swa_sink[:, :], 'misc', [], ['small'])
        ACT(small[:, 24:40], small[:, 24:40], AF.Exp, ['small'], ['small'])
        for part, ntile, dst_rot, plain_l, plain_c in (('q', 8, (lambda a, b: qx_scr[a:a + b, :]), qp_scr, qp_scr), ('k', 2, (lambda a, b: kmine_rows(l, a, b)), None, kc_scr)):
            base = 0 if part == 'q' else 1024
            for grp in range((ntile + 3) // 4):
                nt = min(4, ntile - grp * 4)
                wA, wAk = load_w(swa_qkv[:, :, base + grp * 512:base + grp * 512 + nt * 128], KD, nt * 128)
                wB, wBk = load_w(swa_qkp[:, :, base + grp * 512:base + grp * 512 + nt * 128], KD, nt * 128)
                for ci, (c0, n) in enumerate(CH):
                    cs, ck = load_cs(ci) if ci < 4 else (None, None)
                    for t in range(nt):
                        row0 = (grp * 4 + t) * 128
                        if ci < 4:
                            pa, pak, pb, pbk = proj_fm(hT, hkey, KD, wA, wAk, t * 128, 128, ci, wB, wBk, t * 128)
                            qk_tile(l, ci, 128, pa, pak, pb, pbk, cs, ck, dst_rot(row0, 128)[:, c0:c0 + n],
                                    plain_l[row0:row0 + 128, c0:c0 + n] if plain_l is not None else None,
                                    'qx_scr' if part == 'q' else 'kmine', 'qp_scr')
                        else:
                            pa, pak, _, _ = proj_fm(hT, hkey, KD, wA, wAk, t * 128, 128, ci)
                            dstc = qp_scr[row0:row0 + 128, c0:c0 + n] if part == 'q' else kc_scr[row0:row0 + 128, 0:n]
                            qk_tile(l, ci, 128, pa, pak, None, None, None, None, None, dstc, None,
                                    'qp_scr' if part == 'q' else 'kc_scr')
        w, wk = load_w(swa_qkv[:, :, 1280:1536], KD, 256)
        v_proj_tm(l, w, wk, 0, 256, 0)
        exchange(l)

        def mask(pt, ptk, qa, qb, mspec):
            jj, halo = mspec
            if halo is None:
                TT(pt[:, qa:qb], pt[:, qa:qb], swa_m[:, jj, qa:qb], ALU.mult, [ptk, 'swa_m'], [ptk])
            else:
                STT(pt[:, qa:qb], pt[:, qa:qb], halo_s[:, halo:halo + 1], swa_m[:, jj, qa:qb], ALU.mult, ALU.mult,
                    [ptk, 'swa_m', 'halo_s'], [ptk])

        for kvh in range(4):
            DMA('sp', kx[:64, 0:128], kall_rows(l, 0, kvh * 64, 64)[:, TL - 128:TL], 'kx', ['kall'], ['kx'])
            DMA('sp', kx[:64, 128:128 + TL], kmine_rows(l, kvh * 64, 64), 'kx', ['kmine'], ['kx'])
            DMA('sp', kx[:64, 128 + TL:256 + TL], kall_rows(l, 1, kvh * 64, 64)[:, 0:128], 'kx',
                ['kall'], ['kx'])
            DMA('sp', kcx[:64, :], kc_scr[kvh * 64:(kvh + 1) * 64, :], 'kcx', ['kc_scr'], ['kcx'])
            DMA('sp', vt[:, 0:1, :64], vall_rows(l, 0, TL - 128, 128)[:, kvh * 64:(kvh + 1) * 64].rearrange("(b p) c -> p b c", p=128),
                'vt', ['vall'], ['vt'])
            for j in range(2):
                DMA('sp', vt[:, 1 + 8 * j:9 + 8 * j, :64],
                    vmine_rows(l, j * 1024, 1024)[:, kvh * 64:(kvh + 1) * 64].rearrange("(b p) c -> p b c", p=128),
                    'vt', ['vmine'], ['vt'])
            DMA('sp', vt[:, 17:18, :64], vall_rows(l, 1, 0, 128)[:, kvh * 64:(kvh + 1) * 64].rearrange("(b p) c -> p b c", p=128),
                'vt', ['vall'], ['vt'])
            DMA('sp', vt[:, 32:34, :64], vc_scr[:, kvh * 64:(kvh + 1) * 64].rearrange("(b p) c -> p b c", p=128), 'vt',
                ['vc_scr'], ['vt'])
            for g in range(4):
                hq = kvh * 4 + g
                DMA('sp', q2[:64, :], qx_scr[hq * 64:(hq + 1) * 64, :], 'q2', ['qx_scr'], ['q2'])
                DMA('sp', q1[:64, :], qp_scr[hq * 64:(hq + 1) * 64, :], 'q1', ['qp_scr'], ['q1'])

                def xb(ci):
                    out = []
                    order = [None]
                    for jj in range(6):
                        qa = max(0, jj - 2) * 128
                        qb = min(4, jj + 1) * 128
                        blk = 4 * ci + jj
                        halo = None
                        if blk == 0:
                            halo = 0
                        elif blk == 17:
                            halo = 1
                        out.append((blk * 128, blk, (qa, qb), (jj, halo)))
                    return out
                attn_unit(scale, [(kx, q2, 64, 'kx', 'q2')], [(kcx, q1, 64, 'kcx', 'q1')], xb,
                          lambda vb: vt[:, vb, :64], 64, None, None, None,
                          plain_final(oh, 'oh', 64, den_add=small[:64, 24 + hq:25 + hq]), mask=mask)
                wo_apply(l, swa_wo, oh, 'oh', hq, 64)

    na_E = sb("na_E", [128, 22 * 64], BF16, at=R2 + 23552)
    na_rv_s = sb("na_rv_s", [128, 256], BF16, at=R2 + 26368)
    na_rv_f = tmp_lam

    def na_mixer(l):
        scale = 64 ** -0.5
        DMA('sp', na_rv_f[:, :], na_rv[:, :], 'misc', [], ['na_rv_f'])
        CP(na_rv_s[:, :], na_rv_f[:, :], ['na_rv_f'], ['na_rv'])
        for part, dstl, dstc in (('q', (lambda a, b: qp_scr[a:a + b, :]), qp_scr), ('k', (lambda a, b: kmine_rows(l, a, b)), kc_scr)):
            base = 0 if part == 'q' else 1024
            for grp in range(2):
                wA, wAk = load_w(na_qkv[:, :, base + grp * 512:base + (grp + 1) * 512], KD, 512)
                for ci, (c0, n) in enumerate(CH):
                    for t in range(4):
                        row0 = (grp * 4 + t) * 128
                        pa, pak, _, _ = proj_fm(hT, hkey, KD, wA, wAk, t * 128, 128, ci)
                        if part == 'q' or ci < 4:
                            qk_tile(l, ci, 128, pa, pak, None, None, None, None, None,
                                    dstl(row0, 128)[:, c0:c0 + n], None, 'qp_scr' if part == 'q' else 'kmine')
                        else:
                            qk_tile(l, ci, 128, pa, pak, None, None, None, None, None, kc_scr[row0:row0 + 128, 0:n],
                                    None, 'kc_scr')
        for half in range(2):
            w, wk = load_w(na_qkv[:, :, 2048 + half * 512:2048 + (half + 1) * 512], KD, 512)
            v_proj_tm(l, w, wk, 0, 512, half * 512)
        exchange(l)

        for h in range(16):
            for piece in range(3):
                a0 = piece * 512
                a1 = min(1408, a0 + 512)
                t, tk = tmp_p.get()
                DMA('sp', t[:, :a1 - a0], na_rpbE[h, :, a0:a1], 'tmpd%d' % tk[1], [], [tk])
                ACT(na_E[:, a0:a1], t[:, :a1 - a0], AF.Exp, [tk], ['na_E'])
            DMA('sp', kx[:64, 0:256], kall_rows(l, 0, h * 64, 64)[:, TL - 256:TL], 'kx', ['kall'], ['kx'])
            DMA('sp', kx[:64, 256:256 + TL], kmine_rows(l, h * 64, 64), 'kx', ['kmine'], ['kx'])
            DMA('sp', kx[:64, 256 + TL:512 + TL], kall_rows(l, 1, h * 64, 64)[:, 0:256], 'kx',
                ['kall'], ['kx'])
            DMA('sp', kcx[:64, :], kc_scr[h * 64:(h + 1) * 64, :], 'kcx', ['kc_scr'], ['kcx'])
            DMA('sp', vt[:, 0:2, :64], vall_rows(l, 0, TL - 256, 256)[:, h * 64:(h + 1) * 64].rearrange("(b p) c -> p b c", p=128),
                'vt', ['vall'], ['vt'])
            for j in range(2):
                DMA('sp', vt[:, 2 + 8 * j:10 + 8 * j, :64],
                    vmine_rows(l, j * 1024, 1024)[:, h * 64:(h + 1) * 64].rearrange("(b p) c -> p b c", p=128),
                    'vt', ['vmine'], ['vt'])
            DMA('sp', vt[:, 18:20, :64], vall_rows(l, 1, 0, 256)[:, h * 64:(h + 1) * 64].rearrange("(b p) c -> p b c", p=128),
                'vt', ['vall'], ['vt'])
            DMA('sp', vt[:, 32:34, :64], vc_scr[:, h * 64:(h + 1) * 64].rearrange("(b p) c -> p b c", p=128), 'vt',
                ['vc_scr'], ['vt'])
            DMA('sp', q1[:64, :], qp_scr[h * 64:(h + 1) * 64, :], 'q1', ['qp_scr'], ['q1'])

            def xb(ci):
                return [((4 * ci + jb) * 128, 4 * ci + jb, (0, 512), (ci, jb)) for jb in range(8)]

            def mask(pt, ptk, qa, qb, mspec):
                ci, jb = mspec
                j0 = 14 - 2 * jb
                TT(pt[:, 0:512], pt[:, 0:512], na_E[:, j0 * 64:(j0 + 8) * 64], ALU.mult, [ptk, 'na_E'], [ptk])
                i0 = (ci * 8 + jb) * 8
                TT(pt[:, 0:512].rearrange("p (a b) -> p a b", b=64), pt[:, 0:512].rearrange("p (a b) -> p a b", b=64),
                   na_rv_s[:, i0:i0 + 8].unsqueeze(2).broadcast_to([128, 8, 64]), ALU.mult, [ptk, 'na_rv'], [ptk])
            attn_unit(scale, [(kx, q1, 64, 'kx', 'q1')], [(kcx, q1, 64, 'kcx', 'q1')], xb,
                      lambda vb: vt[:, vb, :64], 64, None, None, None, plain_final(oh, 'oh', 64), mask=mask)
            wo_apply(l, na_wo, oh, 'oh', h, 64)

    for l in layers:
        mod_phase(l)
        norm_phase(0)
        ffn_phase(l, 0, 0)
        if do_mixer:
            norm_phase(1)
            if l == 0:
                fence(['dve'], R2_KEYS)
                mla_mixer(l)
                fence(['sp', 'act', 'dve'], R2_KEYS)
            elif l == 1:
                swa_mixer(l)
            elif l == 2:
                na_mixer(l)
            else:
                diff_mixer(l, 0.8 - 0.6 * float(np.exp(-0.3 * l)))
            fence(['act'], ATT_KEYS + HT_KEYS)
        norm_phase(2)
        ffn_phase(l, 1, 2)

    for ci, (c0, n) in enumerate(CH[:4]):
        ss, ssk = ps_p.get()
        for k in range(KD):
            sq, sqk = tmp_p.get()
            ACT(sq[:, :n], xT[:, k, c0:c0 + n], AF.Square, [('xT', ci, k)], [sqk])
            MM(ss[:, :n], ones_f[:, :], sq[:, :n], k == 0, k == KD - 1, [sqk, 'ones_f'], [ssk])
        rstd, rk = rs_p.get()
        ACT(rstd[:, :n], ss[:, :n], AF.Sqrt, [ssk], [rk], scale=1.0 / D, bias=eps_t[:, 0:1])
        RCP(rstd[:, :n], rstd[:, :n], [rk], [rk])
        for k in range(KD):
            t, tk = tmp_p.get()
            STT(t[:, :n], xT[:, k, c0:c0 + n], fng_s[:, k:k + 1], rstd[:, :n], ALU.mult, ALU.mult,
                [('xT', ci, k), rk, 'fng'], [tk])
            DMA('sp', outT[:, k, c0:c0 + n], t[:, :n], 'out', [tk], ['out'])
    if DEBUG and 3 in layers:
        dk1 = nc.dram_tensor("dbg_kall", [1024, TL], BF16, kind="ExternalOutput")
        dk2 = nc.dram_tensor("dbg_kmine", [512, TL], BF16, kind="ExternalOutput")
        DMA('sp', dk1[:, :], kparts[3][0][3][:, :], 'out', ['kall'], ['out'])
        DMA('sp', dk2[:, :], kparts[3][0][2][:, :], 'out', ['kmine'], ['out'])
    P.add('sp', None, ['out'], [])

    P.emit(nc, es)
    es.close()
    return nc


def _kmaj(w):
    K, N = w.shape
    return np.ascontiguousarray(w.reshape(K // 128, 128, N).transpose(1, 0, 2))


def _swap_halves(w, col0, ncols, hd=64):
    idx = np.arange(col0, col0 + ncols).reshape(-1, 2, hd // 2)[:, ::-1, :].reshape(-1)
    return w[:, idx]


def _rope_tables(pos0, n):
    t = np.arange(pos0, pos0 + n)
    row = (t // 64).astype(np.float32)
    col = (t % 64).astype(np.float32)
    nf = 16
    inv = np.exp(-np.log(np.float32(10000.0)) * np.arange(nf, dtype=np.float32) / nf).astype(np.float32)
    ang = np.concatenate([row[:, None] * inv, col[:, None] * inv], axis=-1).astype(np.float32)
    c = np.cos(ang).T.astype(np.float32)
    s = np.sin(ang).T.astype(np.float32)
    cos4 = np.concatenate([c, c, c, c], 0)
    sin4 = np.concatenate([-s, s, -s, s], 0)
    return np.ascontiguousarray(cos4), np.ascontiguousarray(sin4)


def _swa_masks():
    import ml_dtypes
    m = np.zeros((128, 6, 512), np.float32)
    kj = np.arange(128)[:, None]
    q = np.arange(512)[None, :]
    for jj in range(6):
        kpos = (jj - 1) * 128 + kj
        m[:, jj, :] = (np.abs(kpos - q) <= 128)
    return m.astype(ml_dtypes.bfloat16)


def _na_tables(rpb, half):
    H = rpb.shape[0]
    E = np.full((H, 128, 22, 64), NEGFILL, np.float32)
    kc = np.arange(64)[:, None]
    qc = np.arange(64)[None, :]
    cs = np.clip(qc - 8, 0, 48)
    colvalid = (kc >= cs) & (kc < cs + 16)
    coff = np.clip(kc - qc + 15, 0, 30)
    for hf in range(2):
        for jp in range(22):
            e = jp - 3 - hf
            if e < 0 or e > 14:
                continue
            g = rpb[:, 14 - e][:, coff]
            g = np.where(colvalid[None], g, np.float32(NEGFILL))
            E[:, hf * 64:(hf + 1) * 64, jp, :] = g
    rv = np.zeros((128, 4, 8, 8), np.float32)
    for ci in range(4):
        for jb in range(8):
            for hf in range(2):
                krl = 8 * ci - 4 + 2 * jb + hf
                kr = 32 * half + krl
                for qi in range(8):
                    qr = 32 * half + 8 * ci + qi
                    rs = min(max(qr - 4, 0), 56)
                    ok = (rs <= kr < rs + 8) and (0 <= kr < 64)
                    if krl < 0 and half == 0:
                        ok = False
                    if krl >= 32 and half == 1:
                        ok = False
                    rv[hf * 64:(hf + 1) * 64, ci, jb, qi] = 1.0 if ok else 0.0
    return E.reshape(H, 128, 22 * 64), rv.reshape(128, 256)


def prepare_inputs(inp, ncores=8):
    f = lambda a: np.ascontiguousarray(np.asarray(a, dtype=np.float32))
    x, c, ctx, c_ctx = f(inp['x']), f(inp['c']), f(inp['ctx']), f(inp['c_ctx'])
    shared = {}
    shared['modw'] = np.ascontiguousarray(f(inp['mod_w']).reshape(4, 8, 128, 9216).transpose(0, 2, 1, 3))
    shared['modb'] = np.ascontiguousarray(f(inp['mod_b']).reshape(4, 72, 128).transpose(2, 0, 1))
    shared['ng_in'] = np.ascontiguousarray(f(inp['norm_g']).reshape(4, 3, 8, 128).transpose(3, 0, 1, 2))
    shared['fng_in'] = np.ascontiguousarray(f(inp['final_norm_g']).reshape(8, 128).T)
    shared['win'] = np.ascontiguousarray(f(inp['ffn_w_in']).reshape(4, 2, 8, 128, 2 * DFF).transpose(0, 1, 3, 2, 4))
    shared['wout'] = np.ascontiguousarray(f(inp['ffn_w_out']).reshape(4, 2, 22, 128, D).transpose(0, 1, 3, 2, 4))
    wd = f(inp['mla_w_down'])[0]
    shared['mla_down'] = _kmaj(wd)
    shared['mla_downp'] = _kmaj(_swap_halves(wd, 768, 64))
    shared['mla_qn'] = np.ascontiguousarray(f(inp['mla_q_norm'])[0].reshape(4, 128).T)
    shared['mla_kvn'] = np.ascontiguousarray(f(inp['mla_kv_norm'])[0].reshape(2, 128).T)
    uq = f(inp['mla_w_uq'])[0]
    shared['mla_uq'] = _kmaj(uq)
    ropecols = np.concatenate([np.arange(h * 192 + 128, h * 192 + 192) for h in range(8)])
    uqr = uq[:, ropecols]
    shared['mla_uqp'] = _kmaj(_swap_halves(uqr, 0, 512))
    shared['mla_ukv'] = _kmaj(f(inp['mla_w_ukv'])[0])
    shared['mla_wo'] = _kmaj(f(inp['mla_w_o'])[0])
    sq = f(inp['swa_w_qkv'])[0]
    shared['swa_qkv'] = _kmaj(sq)
    shared['swa_qkp'] = _kmaj(_swap_halves(sq, 0, 1280))
    shared['swa_sink'] = np.ascontiguousarray(np.broadcast_to(f(inp['swa_sink'])[0][None, :], (128, 16)))
    swo = f(inp['swa_w_o'])[0]
    t = np.zeros((128, 16, D), np.float32)
    t[:64] = swo.reshape(16, 64, D).transpose(1, 0, 2)
    shared['swa_wo'] = t
    shared['swa_mask'] = _swa_masks()
    shared['na_qkv'] = _kmaj(f(inp['na_w_qkv'])[0])
    nwo = f(inp['na_w_o'])[0]
    t2 = np.zeros((128, 16, D), np.float32)
    t2[:64] = nwo.reshape(16, 64, D).transpose(1, 0, 2)
    shared['na_wo'] = t2
    dq = f(inp['diff_w_qkv'])[0]
    shared['df_qkv'] = _kmaj(dq)
    shared['df_qkp'] = _kmaj(_swap_halves(dq, 0, 2048))
    shared['df_lam'] = np.ascontiguousarray(np.broadcast_to(f(inp['diff_lambda'])[0].reshape(1, 256), (128, 256)))
    shared['df_ng'] = np.ascontiguousarray(f(inp['diff_norm_g'])[0].reshape(128, 1))
    shared['df_wo'] = _kmaj(f(inp['diff_w_o'])[0])
    rpb = f(inp['na_rpb'])[0]
    maps = []
    for core in range(ncores):
        b, h = core // 2, core % 2
        m = dict(shared)
        tok = np.concatenate([x[b, h * TL:(h + 1) * TL], ctx[b]], 0)
        m['xT_in'] = np.ascontiguousarray(tok.T.reshape(8, 128, T).transpose(1, 0, 2))
        cc = np.stack([c[b], c_ctx], -1)
        m['cc_in'] = np.ascontiguousarray(cc.reshape(8, 128, 2).transpose(1, 0, 2))
        m['cos4'], m['sin4'] = _rope_tables(h * TL, TL)
        hv = np.zeros((128, 2), np.float32)
        hv[:, 0] = 1.0 if h == 1 else 0.0
        hv[:, 1] = 1.0 if h == 0 else 0.0
        m['halo_v'] = hv
        E, rv = _na_tables(rpb, h)
        m['na_rpbE'] = E
        m['na_rv'] = rv
        maps.append(m)
    return maps


_NC_CACHE = {}


def kernel(**inputs):
    maps = prepare_inputs(inputs)
    if 'nc' not in _NC_CACHE:
        _NC_CACHE['nc'] = build()
    nc = _NC_CACHE['nc']
    names = set(_INPUT_NAMES)
    in_maps = [{k: v for k, v in m.items() if k in names} for m in maps]
    res = run_bass_kernel_spmd(nc, in_maps, core_ids=list(range(8)))
    out = np.zeros((4, 4096, D), np.float32)
    for core in range(8):
        b, h = core // 2, core % 2
        oT = np.asarray(res.results[core]["outT"]).reshape(128, 8, TL)
        out[b, h * TL:(h + 1) * TL] = oT.transpose(2, 1, 0).reshape(TL, D)
    return out


_INPUT_NAMES = ["xT_in", "cc_in", "modw", "modb", "ng_in", "fng_in", "win", "wout", "cos4", "sin4", "mla_down",
                "mla_downp", "mla_qn", "mla_kvn", "mla_uq", "mla_uqp", "mla_ukv", "mla_wo", "swa_qkv", "swa_qkp",
                "swa_sink", "swa_wo", "swa_mask", "halo_v", "na_qkv", "na_rpbE", "na_rv", "na_wo", "df_qkv", "df_qkp",
                "df_lam", "df_ng", "df_wo"]
```

```python
import numpy as np
from contextlib import ExitStack
import concourse.bass as bass
import concourse.mybir as mybir
from concourse.bass_utils import run_bass_kernel_spmd

F32 = mybir.dt.float32
BF16 = mybir.dt.bfloat16
AF = mybir.ActivationFunctionType
ALU = mybir.AluOpType

D = 1024
KD = 8
TL = 2048
TCX = 256
T = TL + TCX
DFF = 2816
CH = [(0, 512), (512, 512), (1024, 512), (1536, 512), (2048, 256)]
EPS = 1e-6
NEGFILL = -30000.0
ENGS = ['sp', 'act', 'pe', 'dve', 'pool']
DEBUG = False


class Op:
    __slots__ = ('eng', 'fn', 'r', 'w', 'dma', 'inc', 'waits', 'signal', 'val', 'fdeps')

    def __init__(self, eng, fn, r, w, dma, inc):
        self.eng, self.fn, self.r, self.w, self.dma, self.inc = eng, fn, r, w, dma, inc
        self.waits = []
        self.signal = False
        self.val = 0
        self.fdeps = ()


class Prog:
    def __init__(self):
        self.ops = []

    def add(self, eng, fn, r=(), w=(), dma=None, inc=16):
        self.ops.append(Op(eng, fn, tuple(r), tuple(w), dma, inc))

    def analyse(self):
        last_w = {}
        rd_eng = {}
        rd_dma = {}
        waited = {e: {} for e in ENGS}
        dma_cnt = {}
        ops = self.ops
        for j, op in enumerate(ops):
            deps = set()
            for b in op.r:
                i = last_w.get(b)
                if i is not None:
                    deps.add(i)
            for b in op.w:
                i = last_w.get(b)
                if i is not None:
                    deps.add(i)
                for i in rd_eng.get(b, {}).values():
                    deps.add(i)
                for i in rd_dma.get(b, ()):
                    deps.add(i)
            E = op.eng
            wt = waited[E]
            more = set()
            for i in deps:
                if ops[i].fn is None:
                    more.update(ops[i].fdeps)
            deps = set(i for i in deps if ops[i].fn is not None) | more
            if op.fn is None:
                op.fdeps = tuple(deps)
            for i in sorted(deps):
                p = ops[i]
                if p.dma is None:
                    if p.eng == E and E in ('pe', 'sp', 'pool'):
                        continue
                    if wt.get(p.eng, -1) >= i:
                        continue
                    wt[p.eng] = i
                    p.signal = True
                    op.waits.append(('c', i))
                else:
                    if wt.get(p.dma, 0) >= p.val:
                        continue
                    wt[p.dma] = p.val
                    op.waits.append(('d', p.dma, p.val))
            if op.dma is not None:
                dma_cnt[op.dma] = dma_cnt.get(op.dma, 0) + op.inc
                op.val = dma_cnt[op.dma]
            for b in op.r:
                if op.dma is not None:
                    rd_dma.setdefault(b, []).append(j)
                else:
                    rd_eng.setdefault(b, {})[E] = j
            for b in op.w:
                last_w[b] = j
                rd_eng[b] = {}
                rd_dma[b] = []
        cnt = {e: 0 for e in ENGS}
        for op in ops:
            if op.dma is None and op.signal:
                cnt[op.eng] += 1
                op.val = cnt[op.eng]
        self.dma_keys = sorted(dma_cnt.keys())
        return cnt

    def emit(self, nc, es):
        self.analyse()
        esem = {e: es.enter_context(nc.semaphore('s_' + e)) for e in ENGS}
        dsem = {k: es.enter_context(nc.semaphore('d_' + k)) for k in self.dma_keys}
        ops = self.ops
        per = {e: [op for op in ops if op.eng == e] for e in ENGS}

        def replay(ename, eng):
            for op in per[ename]:
                for w in op.waits:
                    if w[0] == 'c':
                        p = ops[w[1]]
                        eng.wait_ge(esem[p.eng], p.val)
                    else:
                        eng.wait_ge(dsem[w[1]], w[2])
                if op.fn is None:
                    continue
                ins = op.fn(eng)
                if op.dma is not None:
                    ins.then_inc(dsem[op.dma], op.inc)
                elif op.signal:
                    ins.then_inc(esem[op.eng], 1)

        with nc.Block() as block:
            @block.sync
            def _(e):
                replay('sp', e)

            @block.scalar
            def _(e):
                replay('act', e)

            @block.tensor
            def _(e):
                replay('pe', e)

            @block.vector
            def _(e):
                replay('dve', e)

            @block.gpsimd
            def _(e):
                replay('pool', e)


class Rot:
    def __init__(self, name, tiles):
        self.name, self.tiles, self.i = name, tiles, 0

    def get(self):
        i = self.i
        self.i = (i + 1) % len(self.tiles)
        return self.tiles[i], (self.name, i)


def build(layers=(0, 1, 2, 3), pair_groups=((0, 1), (2, 3), (4, 5), (6, 7)), do_mixer=True):
    nc = bass.Bass("TRN2", target_bir_lowering=False)
    P = Prog()
    es = ExitStack()

    def din(name, shape, dt=F32):
        return nc.dram_tensor(name, list(shape), dt, kind="ExternalInput")

    def dscr(name, shape, dt=BF16):
        return nc.dram_tensor(name, list(shape), dt)

    xT_in = din("xT_in", [128, KD, T])
    cc_in = din("cc_in", [128, KD, 2])
    modw = din("modw", [4, 128, KD, 9216])
    modb = din("modb", [128, 4, 72])
    ng_in = din("ng_in", [128, 4, 3, KD])
    fng_in = din("fng_in", [128, KD])
    win_d = din("win", [4, 2, 128, KD, 2 * DFF])
    wout_d = din("wout", [4, 2, 128, 22, D])
    cos_d = din("cos4", [128, TL])
    sin_d = din("sin4", [128, TL])
    mla_down = din("mla_down", [128, KD, 832])
    mla_downp = din("mla_downp", [128, KD, 64])
    mla_qn = din("mla_qn", [128, 4])
    mla_kvn = din("mla_kvn", [128, 2])
    mla_uq = din("mla_uq", [128, 4, 1536])
    mla_uqp = din("mla_uqp", [128, 4, 512])
    mla_ukv = din("mla_ukv", [128, 2, 2048])
    mla_wo = din("mla_wo", [128, 8, D])
    swa_qkv = din("swa_qkv", [128, KD, 1536])
    swa_qkp = din("swa_qkp", [128, KD, 1280])
    swa_sink = din("swa_sink", [128, 16])
    swa_wo = din("swa_wo", [128, 16, D])
    swa_mask = din("swa_mask", [128, 6, 512], BF16)
    halo_v = din("halo_v", [128, 2])
    na_qkv = din("na_qkv", [128, KD, 3072])
    na_rpbE = din("na_rpbE", [16, 128, 22 * 64])
    na_rv = din("na_rv", [128, 4 * 8 * 8])
    na_wo = din("na_wo", [128, 16, D])
    df_qkv = din("df_qkv", [128, KD, 3072])
    df_qkp = din("df_qkp", [128, KD, 2048])
    df_lam = din("df_lam", [128, 256])
    df_ng = din("df_ng", [128, 1])
    df_wo = din("df_wo", [128, 8, D])

    outT = nc.dram_tensor("outT", [128, KD, TL], F32, kind="ExternalOutput")

    qx_scr = dscr("qx_scr", [2048, TL])
    qp_scr = dscr("qp_scr", [2048, T])
    kc_scr = dscr("kc_scr", [1152, TCX])
    vc_scr = dscr("vc_scr", [TCX, 1024])
    KR = {0: 1088, 1: 256, 2: 1024, 3: 1024}
    VC = {0: 1024, 1: 256, 2: 1024, 3: 1024}
    bar_in = dscr("bar_in", [128, 128])
    bar_out = dscr("bar_out", [256, 128])
    kparts = {}
    vparts = {}
    for l in layers:
        kparts[l] = []
        r0 = 0
        while r0 < KR[l]:
            nr = min(512, KR[l] - r0)
            kparts[l].append((r0, nr, dscr("kmine%d_%d" % (l, r0), [nr, TL]), dscr("kall%d_%d" % (l, r0), [2 * nr, TL])))
            r0 += nr
        vparts[l] = [(dscr("vmine%d_%d" % (l, j), [1024, VC[l]]), dscr("vall%d_%d" % (l, j), [2048, VC[l]]))
                     for j in range(2)]

    def kmine_rows(l, row0, nrows):
        for (r0, nr, mine, all_) in kparts[l]:
            if r0 <= row0 and row0 + nrows <= r0 + nr:
                return mine[row0 - r0:row0 - r0 + nrows, :]
        raise AssertionError((l, row0, nrows))

    def kall_rows(l, r, row0, nrows):
        for (r0, nr, mine, all_) in kparts[l]:
            if r0 <= row0 and row0 + nrows <= r0 + nr:
                return all_[r * nr + row0 - r0:r * nr + row0 - r0 + nrows, :]
        raise AssertionError((l, row0, nrows))

    def vmine_rows(l, t0, nt):
        j = t0 // 1024
        assert t0 + nt <= (j + 1) * 1024
        return vparts[l][j][0][t0 - j * 1024:t0 - j * 1024 + nt, :]

    def vall_rows(l, r, t0, nt):
        j = t0 // 1024
        assert t0 + nt <= (j + 1) * 1024
        return vparts[l][j][1][r * 1024 + t0 - j * 1024:r * 1024 + t0 - j * 1024 + nt, :]

    ARENA = 212000
    arena = es.enter_context(nc.sbuf_tensor("arena", [128, ARENA // 4], F32))
    a0 = nc.lookup_mloc(arena).addr
    cur = [0]

    def sb(name, shape, dt=F32, at=None):
        nb = int(np.prod(shape[1:])) * (4 if dt == F32 else 2)
        nb = (nb + 31) // 32 * 32
        if at is None:
            off = cur[0]
            cur[0] += nb
            assert cur[0] <= ARENA, (name, cur[0])
        else:
            off = at
        return nc.alloc_sbuf_tensor_at(name, list(shape), dt, offset=a0 + off)

    def region(nbytes):
        off = cur[0]
        cur[0] += nbytes
        assert cur[0] <= ARENA, ('region', cur[0])
        return off

    xT = sb("xT", [128, KD, T])
    R1 = region(36864)
    R2 = region(27648)
    hT = sb("hT", [128, KD, T], BF16, at=R1)
    win_p = Rot("win", [sb("win%d" % i, [128, KD, 512], BF16) for i in range(3)])
    wout_p = Rot("wout", [sb("wout%d" % i, [128, 2, D], BF16) for i in range(2)])
    modw_p = Rot("modw", [sb("modw%d" % i, [128, KD, 128], F32, at=R2 + i * 4096) for i in range(2)])
    tmp_p = Rot("tmp", [sb("tmp%d" % i, [128, 512]) for i in range(4)])
    rs_p = Rot("rs", [sb("rs%d" % i, [128, 512]) for i in range(2)])
    cs_p = Rot("cs", [sb("cs%d" % i, [128, 2, 512]) for i in range(1)])
    act_p = Rot("act", [sb("act%d" % i, [128, 2, 512], BF16) for i in range(2)])
    pt_p = Rot("pt", [sb("pt%d" % i, [128, 512], BF16) for i in range(3)])
    stg_p = Rot("stg", [sb("stg%d" % i, [128, 512], BF16) for i in range(3)])
    condT = sb("condT", [128, KD, 2])
    modT = sb("modT", [128, 72, 2])
    modb_s = sb("modb_s", [128, 4, 72])
    ng_s = sb("ng_s", [128, 4, 3, KD])
    fng_s = sb("fng_s", [128, KD])
    gs_s = sb("gs_s", [128, 3, KD, 2])
    gt_s = sb("gt_s", [128, 3, KD, 2])
    ones_f = sb("ones_f", [128, 128])
    ones_b = sb("ones_b", [128, 128], BF16)
    small = sb("small", [128, 64])
    eps_t = sb("eps_t", [128, 8])
    q0 = sb("q0", [128, T], BF16, at=R1)
    q1 = sb("q1", [128, T], BF16, at=R1 + 4608)
    kx = sb("kx", [128, 2 * TL], BF16, at=R1 + 9216)
    krx = sb("krx", [128, 2 * TL], BF16, at=R1 + 17408)
    kcx = sb("kcx", [128, TCX], BF16, at=R1 + 25600)
    krc = sb("krc", [128, TCX], BF16, at=R1 + 26112)
    vt = sb("vt", [128, 34, 128], BF16, at=R1 + 26624)
    assert 26624 + 8704 <= 36864
    q2 = sb("q2", [128, TL], BF16)
    oh = sb("oh", [128, T], BF16)
    ATT_KEYS = ['q0', 'q1', 'kx', 'krx', 'kcx', 'krc', 'vt']
    HT_KEYS = [('hT', ci) for ci in range(5)]
    R2_KEYS = [('modw', 0), ('modw', 1), 'o0f', 'swa_m', 'na_E', 'na_rv'] + [('cq', ci) for ci in range(5)] + \
              [('ckv', ci) for ci in range(5)]

    def fence(engs, keys):
        for e_ in engs:
            P.add(e_, None, [], keys)

    ps_p = Rot("ps", [es.enter_context(nc.psum_tensor("ps%d" % i, [128, 512], F32)) for i in range(4)])
    acc_p = Rot("acc", [es.enter_context(nc.psum_tensor("acc%d" % i, [128, 512], F32)) for i in range(3)])
    pmod = es.enter_context(nc.psum_tensor("pmod", [128, 72, 2], F32))

    def ACT(out, in_, func, r, w, **kw):
        P.add('act', lambda e: e.activation(out=out, in_=in_, func=func, **kw), r, w)

    def MM(out, lhsT, rhs, start, stop, r, w):
        P.add('pe', lambda e: e.matmul(out, lhsT, rhs, start=start, stop=stop), r, w)

    def TT(out, in0, in1, op, r, w):
        P.add('dve', lambda e: e.tensor_tensor(out=out, in0=in0, in1=in1, op=op), r, w)

    def TS(out, in0, s1, s2, op0, op1, r, w):
        if op1 is None:
            P.add('dve', lambda e: e.tensor_scalar(out=out, in0=in0, scalar1=s1, scalar2=0.0, op0=op0, op1=ALU.add), r, w)
        else:
            P.add('dve', lambda e: e.tensor_scalar(out=out, in0=in0, scalar1=s1, scalar2=s2, op0=op0, op1=op1), r, w)

    def STT(out, in0, scalar, in1, op0, op1, r, w):
        P.add('dve', lambda e: e.scalar_tensor_tensor(out=out, in0=in0, scalar=scalar, in1=in1, op0=op0, op1=op1), r, w)

    def CP(out, in_, r, w):
        P.add('dve', lambda e: e.tensor_copy(out=out, in_=in_), r, w)

    def RCP(out, in_, r, w):
        P.add('dve', lambda e: e.reciprocal(out=out, in_=in_), r, w)

    def DMA(q, out, in_, key, r, w):
        P.add(q, lambda e: e.dma_start(out=out, in_=in_), r, w, dma=key)

    def MEMSET(ap, v, w):
        P.add('dve', lambda e: e.memset(ap, v), (), w)

    NCH = [len(CH)]

    def xkeys(ci):
        return [('xT', ci, k) for k in range(KD)]

    MEMSET(ones_f[:, :], 1.0, ['ones_f'])
    MEMSET(ones_b[:, :], 1.0, ['ones_b'])
    MEMSET(eps_t[:, :], EPS, ['eps_t'])
    for ci, (c0, n) in enumerate(CH):
        DMA('sp', xT[:, :, c0:c0 + n], xT_in[:, :, c0:c0 + n], 'xin%d' % ci, [], xkeys(ci))
    DMA('sp', condT[:, :, :], cc_in[:, :, :], 'misc_c', [], ['condT'])
    DMA('sp', modb_s[:, :, :], modb[:, :, :], 'misc', [], ['modb'])
    DMA('sp', ng_s[:, :, :, :], ng_in[:, :, :, :], 'misc', [], ['ng'])
    DMA('sp', fng_s[:, :], fng_in[:, :], 'misc', [], ['fng'])
    ACT(condT[:, :, :], condT[:, :, :], AF.Silu, ['condT'], ['condT'])

    def mod_phase(l):
        for j in range(72):
            mw, mk = modw_p.get()
            DMA('sp', mw[:, :, :], modw[l, :, :, j * 128:(j + 1) * 128], 'modw%d' % mk[1], [], [mk])
            for k in range(KD):
                MM(pmod[:, j, :], mw[:, k, :], condT[:, k, :], k == 0, k == KD - 1, [mk, 'condT'], ['pmod'])
        for s in range(2):
            TT(modT[:, :, s], pmod[:, :, s], modb_s[:, l, :], ALU.add, ['pmod', 'modb'], ['modT'])
        for i in range(3):
            for s in range(2):
                STT(gs_s[:, i, :, s], modT[:, (3 * i + 1) * 8:(3 * i + 2) * 8, s], 1.0, ng_s[:, l, i, :],
                    ALU.add, ALU.mult, ['modT', 'ng'], ['gs'])
                TS(gt_s[:, i, :, s], modT[:, (3 * i + 2) * 8:(3 * i + 3) * 8, s], 0.5 if i != 1 else 1.0, None,
                   ALU.mult, None, ['modT'], ['gt'])

    def norm_phase(sub):
        for ci, (c0, n) in enumerate(CH[:NCH[0]]):
            s = 1 if ci == 4 else 0
            ss, ssk = ps_p.get()
            for k in range(KD):
                sq, sqk = tmp_p.get()
                ACT(sq[:, :n], xT[:, k, c0:c0 + n], AF.Square, [('xT', ci, k)], [sqk])
                MM(ss[:, :n], ones_f[:, :], sq[:, :n], k == 0, k == KD - 1, [sqk, 'ones_f'], [ssk])
            rstd, rk = rs_p.get()
            ACT(rstd[:, :n], ss[:, :n], AF.Sqrt, [ssk], [rk], scale=1.0 / D, bias=eps_t[:, 0:1])
            RCP(rstd[:, :n], rstd[:, :n], [rk], [rk])
            for k in range(KD):
                t, tk = tmp_p.get()
                TT(t[:, :n], xT[:, k, c0:c0 + n], rstd[:, :n], ALU.mult, [('xT', ci, k), rk], [tk])
                ACT(hT[:, k, c0:c0 + n], t[:, :n], AF.Identity, [tk, 'gs', 'modT'], [('hT', ci)],
                    scale=gs_s[:, sub, k, s:s + 1], bias=modT[:, 3 * sub * 8 + k, s:s + 1])

    def ffn_phase(l, f, sub):
        for g in range(11):
            win, wk = win_p.get()
            wo_, wok = wout_p.get()
            DMA('pool', win[:, :, 0:256], win_d[l, f, :, :, g * 256:(g + 1) * 256], 'win%d' % wk[1], [], [wk])
            DMA('pool', win[:, :, 256:512], win_d[l, f, :, :, DFF + g * 256:DFF + (g + 1) * 256], 'win%d' % wk[1], [], [wk])
            DMA('pool', wo_[:, :, :], wout_d[l, f, :, 2 * g:2 * g + 2, :], 'wout%d' % wok[1], [], [wok])
            def stage_a(ci):
                c0, n = CH[ci]
                a, ak = act_p.get()
                for fb in range(2):
                    pg, pgk = ps_p.get()
                    pu, puk = ps_p.get()
                    for k in range(KD):
                        MM(pg[:, :n], win[:, k, fb * 128:(fb + 1) * 128], hT[:, k, c0:c0 + n], k == 0, k == KD - 1,
                           [wk, ('hT', ci)], [pgk])
                    for k in range(KD):
                        MM(pu[:, :n], win[:, k, 256 + fb * 128:256 + (fb + 1) * 128], hT[:, k, c0:c0 + n], k == 0,
                           k == KD - 1, [wk, ('hT', ci)], [puk])
                    sg, sgk = tmp_p.get()
                    ACT(sg[:, :n], pg[:, :n], AF.Silu, [pgk], [sgk])
                    TT(a[:, fb, :n], sg[:, :n], pu[:, :n], ALU.mult, [sgk, puk], [ak])
                return a, ak

            def stage_b(ci, a, ak):
                c0, n = CH[ci]
                s = 1 if ci == 4 else 0
                for dk in range(KD):
                    po, pok = acc_p.get()
                    for fb in range(2):
                        MM(po[:, :n], wo_[:, fb, dk * 128:(dk + 1) * 128], a[:, fb, :n], fb == 0, fb == 1, [wok, ak], [pok])
                    STT(xT[:, dk, c0:c0 + n], po[:, :n], gt_s[:, sub, dk, s:s + 1], xT[:, dk, c0:c0 + n],
                        ALU.mult, ALU.add, [pok, 'gt', ('xT', ci, dk)], [('xT', ci, dk)])

            prev = stage_a(0)
            for ci in range(1, NCH[0]):
                cur_ = stage_a(ci)
                stage_b(ci - 1, *prev)
                prev = cur_
            stage_b(NCH[0] - 1, *prev)

    def load_w(src_ap, K, ncols):
        w, wk = win_p.get()
        DMA('pool', w[:, 0:K, 0:ncols], src_ap, 'win%d' % wk[1], [], [wk])
        return w, wk

    def load_cs(ci):
        c0, n = CH[ci]
        cs, ck = cs_p.get()
        DMA('sp', cs[:, 0, :n], cos_d[:, c0:c0 + n], 'cs', [], [ck])
        DMA('sp', cs[:, 1, :n], sin_d[:, c0:c0 + n], 'cs', [], [ck])
        return cs, ck

    def proj_fm(src, srckey, K, wA, wAk, colA, M, ci, wB=None, wBk=None, colB=0):
        c0, n = CH[ci]
        pa, pak = ps_p.get()
        for k in range(K):
            MM(pa[:M, :n], wA[:, k, colA:colA + M], src[:, k, c0:c0 + n], k == 0, k == K - 1, [wAk, srckey(ci)], [pak])
        if wB is None:
            return pa, pak, None, None
        pb, pbk = ps_p.get()
        for k in range(K):
            MM(pb[:M, :n], wB[:, k, colB:colB + M], src[:, k, c0:c0 + n], k == 0, k == K - 1, [wBk, srckey(ci)], [pbk])
        return pa, pak, pb, pbk

    def rope_to(dst_ap, M, n, pa, pak, pb, pbk, cs, ck, dkey):
        t1, t1k = tmp_p.get()
        t2, t2k = tmp_p.get()
        TT(t1[:M, :n], pa[:M, :n], cs[:M, 0, :n], ALU.mult, [pak, ck], [t1k])
        TT(t2[:M, :n], pb[:M, :n], cs[:M, 1, :n], ALU.mult, [pbk, ck], [t2k])
        TT(dst_ap, t1[:M, :n], t2[:M, :n], ALU.add, [t1k, t2k], [dkey])

    def store_rows(dram_ap, sb_ap, sbkey, dkey):
        DMA('sp', dram_ap, sb_ap, dkey if isinstance(dkey, str) else dkey[0], [sbkey], [dkey])

    def hkey(ci):
        return ('hT', ci)

    def exchange(l):
        fence(['sp'], ATT_KEYS + HT_KEYS)
        grp = [list(g) for g in pair_groups]
        P.add('pool', lambda e: e.collective_compute(
            "AllGather", ALU.bypass, replica_groups=grp, ins=[bar_in.ap().opt()], outs=[bar_out.ap().opt()]),
            ['kmine', 'vmine'], ['kmine', 'vmine'], dma='ccp%d' % l, inc=1)
        for pi_, (r0, nr, mine, all_) in enumerate(kparts[l]):
            P.add('pool', lambda e, mine=mine, all_=all_: e.collective_compute(
                "AllGather", ALU.bypass, replica_groups=grp, ins=[mine.ap().opt()], outs=[all_.ap().opt()]),
                ['kmine'], [('kallp', pi_)], dma='cck%d_%d' % (l, pi_), inc=1)
        for pi_, (mine, all_) in enumerate(vparts[l]):
            P.add('pool', lambda e, mine=mine, all_=all_: e.collective_compute(
                "AllGather", ALU.bypass, replica_groups=grp, ins=[mine.ap().opt()], outs=[all_.ap().opt()]),
                ['vmine'], [('vallp', pi_)], dma='ccv%d_%d' % (l, pi_), inc=1)
        P.add('pool', lambda e: e.collective_compute(
            "AllGather", ALU.bypass, replica_groups=grp, ins=[bar_in.ap().opt()], outs=[bar_out.ap().opt()]),
            [('kallp', i) for i in range(len(kparts[l]))] + [('vallp', i) for i in range(2)], ['kall', 'vall'],
            dma='ccb%d' % l, inc=1)

    def v_proj_tm(l, w, wk, col0, ncols, vcol0):
        for tb in range(T // 128):
            ci = min(tb // 4, 4)
            pv, pvk = ps_p.get()
            for k in range(KD):
                MM(pv[:, :ncols], hT[:, k, tb * 128:(tb + 1) * 128], w[:, k, col0:col0 + ncols], k == 0, k == KD - 1,
                   [wk, ('hT', ci)], [pvk])
            st, sk = stg_p.get()
            CP(st[:, :ncols], pv[:, :ncols], [pvk], [sk])
            if tb < 16:
                store_rows(vmine_rows(l, tb * 128, 128)[:, vcol0:vcol0 + ncols], st[:, :ncols], sk, 'vmine')
            else:
                store_rows(vc_scr[(tb - 16) * 128:(tb - 15) * 128, vcol0:vcol0 + ncols], st[:, :ncols], sk, 'vc_scr')

    def qk_tile(l, ci, M, pa, pak, pb, pbk, cs, ck, dst_rot, dst_plain, rot_key, plain_key):
        c0, n = CH[ci]
        if dst_plain is not None:
            st, sk = stg_p.get()
            CP(st[:M, :n], pa[:M, :n], [pak], [sk])
            store_rows(dst_plain, st[:M, :n], sk, plain_key)
        if dst_rot is not None:
            st, sk = stg_p.get()
            rope_to(st[:M, :n], M, n, pa, pak, pb, pbk, cs, ck, sk)
            store_rows(dst_rot, st[:M, :n], sk, rot_key)

    def attn_unit(scale, passes_x, passes_c, xblocks, v_of, dv, qcols, o_dst, o_key, final, mask=None,
                  extra_r=()):
        for ci, (c0, n) in enumerate(CH[:NCH[0]]):
            if ci == 4:
                blocks = []
            else:
                blocks = xblocks(ci)
            po, pok = acc_p.get()
            pd, pdk = acc_p.get()
            seq = [('c', 0)] + [('x', b) for b in blocks] + [('c', 1)]

            def stage_a(bi):
                kind, b = seq[bi]
                if kind == 'c':
                    kc0, vb, (qa, qb), mspec = b * 128, 32 + b, (0, n), None
                    passes = passes_c
                else:
                    kc0, vb, (qa, qb), mspec = b
                    passes = passes_x
                st_, stk = ps_p.get()
                for pi, (kt, qt, rows, kkey, qkey) in enumerate(passes):
                    MM(st_[:, qa:qb], kt[:rows, kc0:kc0 + 128], qt[:rows, c0 + qa:c0 + qb], pi == 0,
                       pi == len(passes) - 1, [kkey, qkey], [stk])
                pt, ptk = pt_p.get()
                ACT(pt[:, qa:qb], st_[:, qa:qb], AF.Exp, [stk], [ptk], scale=scale)
                if mspec is not None:
                    mask(pt, ptk, qa, qb, mspec)
                return (pt, ptk, vb, qa, qb)

            def stage_b(bi, st):
                pt, ptk, vb, qa, qb = st
                first = bi == 0
                last = bi == len(seq) - 1
                MM(po[:dv, qa:qb], v_of(vb), pt[:, qa:qb], first, last, ['vt', ptk], [pok])
                MM(pd[:dv, qa:qb], ones_b[:, :dv], pt[:, qa:qb], first, last, ['ones_b', ptk], [pdk])

            LA = 2
            pend = []
            for bi in range(len(seq)):
                pend.append((bi, stage_a(bi)))
                if len(pend) > LA:
                    b0, st0 = pend.pop(0)
                    stage_b(b0, st0)
            for b0, st0 in pend:
                stage_b(b0, st0)
            final(ci, c0, n, po, pok, pd, pdk)

    def plain_final(dst_tile, dst_key, dv, den_add=None):
        def fin(ci, c0, n, po, pok, pd, pdk):
            r, rk = tmp_p.get()
            if den_add is not None:
                TS(r[:dv, :n], pd[:dv, :n], den_add, None, ALU.add, None, [pdk, 'small'], [rk])
                RCP(r[:dv, :n], r[:dv, :n], [rk], [rk])
            else:
                RCP(r[:dv, :n], pd[:dv, :n], [pdk], [rk])
            TT(dst_tile[:dv, c0:c0 + n], po[:dv, :n], r[:dv, :n], ALU.mult, [pok, rk], [(dst_key, ci)])
        return fin

    def wo_apply(l, wo_d, src_tile, src_key, row_blk, rows):
        w, wk = wout_p.get()
        DMA('pool', w[:rows, 0, :], wo_d[0:rows, row_blk, :] if rows == 128 else wo_d[0:rows, row_blk, :],
            'wout%d' % wk[1], [], [wk])
        for ci, (c0, n) in enumerate(CH[:NCH[0]]):
            s = 1 if ci == 4 else 0
            for dk in range(KD):
                po, pok = ps_p.get()
                MM(po[:, :n], w[:rows, 0, dk * 128:(dk + 1) * 128], src_tile[:rows, c0:c0 + n], True, True,
                   [wk, (src_key, ci)], [pok])
                STT(xT[:, dk, c0:c0 + n], po[:, :n], gt_s[:, 1, dk, s:s + 1], xT[:, dk, c0:c0 + n],
                    ALU.mult, ALU.add, [pok, 'gt', ('xT', ci, dk)], [('xT', ci, dk)])

    def load_v(l, col0, dv):
        for r in range(2):
            for j in range(2):
                DMA('sp', vt[:, 16 * r + 8 * j:16 * r + 8 * j + 8, :dv],
                    vall_rows(l, r, j * 1024, 1024)[:, col0:col0 + dv].rearrange("(b p) c -> p b c", p=128),
                    'vt', ['vall'], ['vt'])
        DMA('sp', vt[:, 32:34, :dv], vc_scr[:, col0:col0 + dv].rearrange("(b p) c -> p b c", p=128), 'vt',
            ['vc_scr'], ['vt'])

    def load_kx(l, tile, key, row0, rows):
        for r in range(2):
            DMA('sp', tile[:rows, r * TL:(r + 1) * TL], kall_rows(l, r, row0, rows), key, ['kall'], [key])

    def full_blocks(ci):
        n = CH[ci][1]
        return [(kb * 128, kb, (0, n), None) for kb in range(32)]

    def diff_mixer(l, lam_init, last=False):
        scale = 64 ** -0.5
        lam = small[:, 0:1]
        DMA('sp', tmp_lam[:, :], df_lam[:, :], 'misc', [], ['lamraw'])
        DMA('sp', small[:, 8:9], df_ng[:, :], 'misc', [], ['small'])
        TT(tmp_lam[:, 0:64], tmp_lam[:, 0:64], tmp_lam[:, 64:128], ALU.mult, ['lamraw'], ['lamraw'])
        TT(tmp_lam[:, 128:192], tmp_lam[:, 128:192], tmp_lam[:, 192:256], ALU.mult, ['lamraw'], ['lamraw'])
        P.add('dve', lambda e: e.reduce_sum(out=small[:, 1:2], in_=tmp_lam[:, 0:64], axis=mybir.AxisListType.X),
              ['lamraw'], ['small'])
        P.add('dve', lambda e: e.reduce_sum(out=small[:, 2:3], in_=tmp_lam[:, 128:192], axis=mybir.AxisListType.X),
              ['lamraw'], ['small'])
        ACT(small[:, 1:3], small[:, 1:3], AF.Exp, ['small'], ['small'])
        TT(small[:, 0:1], small[:, 1:2], small[:, 2:3], ALU.subtract, ['small'], ['small'])
        TS(small[:, 0:1], small[:, 0:1], lam_init, None, ALU.add, None, ['small'], ['small'])
        TS(small[:, 3:4], small[:, 0:1], -1.0, None, ALU.mult, None, ['small'], ['small'])
        TS(small[:, 9:10], small[:, 8:9], 1.0 - lam_init, None, ALU.mult, None, ['small'], ['small'])

        for part, dst_rot, dst_plain_l, dst_plain_c in (('q', (lambda a, b: qx_scr[a:a + b, :]), qp_scr, qp_scr), ('k', (lambda a, b: kmine_rows(l, a, b)), None, kc_scr)):
            base = 0 if part == 'q' else 1024
            for half in range(2):
                wA, wAk = load_w(df_qkv[:, :, base + half * 512:base + (half + 1) * 512], KD, 512)
                wB, wBk = load_w(df_qkp[:, :, base + half * 512:base + (half + 1) * 512], KD, 512)
                for ci, (c0, n) in enumerate(CH):
                    cs, ck = load_cs(ci) if ci < 4 else (None, None)
                    for t in range(4):
                        row0 = (half * 4 + t) * 128
                        if ci < 4:
                            pa, pak, pb, pbk = proj_fm(hT, hkey, KD, wA, wAk, t * 128, 128, ci, wB, wBk, t * 128)
                            qk_tile(l, ci, 128, pa, pak, pb, pbk, cs, ck,
                                    dst_rot(row0, 128)[:, c0:c0 + n],
                                    dst_plain_l[row0:row0 + 128, c0:c0 + n] if dst_plain_l is not None else None,
                                    'qx_scr' if part == 'q' else 'kmine', 'qp_scr')
                        else:
                            pa, pak, _, _ = proj_fm(hT, hkey, KD, wA, wAk, t * 128, 128, ci)
                            if part == 'q':
                                qk_tile(l, ci, 128, pa, pak, None, None, None, None, None,
                                        qp_scr[row0:row0 + 128, c0:c0 + n], None, 'qp_scr')
                            else:
                                qk_tile(l, ci, 128, pa, pak, None, None, None, None, None,
                                        kc_scr[row0:row0 + 128, 0:n], None, 'kc_scr')
        for half in range(2):
            w, wk = load_w(df_qkv[:, :, 2048 + half * 512:2048 + (half + 1) * 512], KD, 512)
            v_proj_tm(l, w, wk, 0, 512, half * 512)
        exchange(l)

        if last:
            NCH[0] = 4
        for h in range(8):
            load_v(l, h * 128, 128)
            for i in range(2):
                u = h * 2 + i
                qt_x, qt_p = (q0, q1)
                DMA('sp', q2[:64, :], qx_scr[u * 64:(u + 1) * 64, :], 'q2', ['qx_scr'], ['q2'])
                DMA('sp', q1[:64, :], qp_scr[u * 64:(u + 1) * 64, :], 'q1', ['qp_scr'], ['q1'])
                load_kx(l, kx, 'kx', u * 64, 64)
                DMA('sp', kcx[:64, :], kc_scr[u * 64:(u + 1) * 64, :], 'kcx', ['kc_scr'], ['kcx'])

                def fin(ci, c0, n, po, pok, pd, pdk, i=i):
                    r, rk = tmp_p.get()
                    RCP(r[:, :n], pd[:, :n], [pdk], [rk])
                    if i == 0:
                        TT(o0f[:, c0:c0 + n], po[:, :n], r[:, :n], ALU.mult, [pok, rk], [('o0f', ci)])
                    else:
                        t, tk = tmp_p.get()
                        TT(t[:, :n], po[:, :n], r[:, :n], ALU.mult, [pok, rk], [tk])
                        STT(o0f[:, c0:c0 + n], t[:, :n], small[:, 3:4], o0f[:, c0:c0 + n], ALU.mult, ALU.add,
                            [tk, 'small', ('o0f', ci)], [('o0f', ci)])
                attn_unit(scale, [(kx, q2, 64, 'kx', 'q2')], [(kcx, q1, 64, 'kcx', 'q1')], full_blocks,
                          lambda vb: vt[:, vb, :], 128, None, None, None, fin)
            for ci, (c0, n) in enumerate(CH[:NCH[0]]):
                tk = ('o0f', ci)
                sq, sqk = tmp_p.get()
                ACT(sq[:, :n], o0f[:, c0:c0 + n], AF.Square, [tk], [sqk])
                ss, ssk = ps_p.get()
                MM(ss[:, :n], ones_f[:, :], sq[:, :n], True, True, [sqk, 'ones_f'], [ssk])
                rs, rsk = rs_p.get()
                ACT(rs[:, :n], ss[:, :n], AF.Sqrt, [ssk], [rsk], scale=1.0 / 128, bias=eps_t[:, 0:1])
                RCP(rs[:, :n], rs[:, :n], [rsk], [rsk])
                STT(oh[:, c0:c0 + n], o0f[:, c0:c0 + n], small[:, 9:10], rs[:, :n], ALU.mult, ALU.mult,
                    [tk, rsk, 'small'], [('oh', ci)])
            wo_apply(l, df_wo, oh, 'oh', h, 128)

    tmp_lam = sb("tmp_lam", [128, 256])
    o0f = sb("o0f", [128, T], F32, at=R2 + 8192)

    cqn = None

    def mla_mixer(l):
        scale = 192 ** -0.5
        DMA('sp', small[:, 16:20], mla_qn[:, :], 'misc', [], ['small'])
        DMA('sp', small[:, 20:22], mla_kvn[:, :], 'misc', [], ['small'])
        wq_, wqk = load_w(mla_down[:, :, 0:512], KD, 512)
        wk_, wkk = load_w(mla_down[:, :, 512:832], KD, 320)
        for ci, (c0, n) in enumerate(CH):
            for (wt_, wtk, nblk, nrm_col, dst, dkey, inv) in ((wq_, wqk, 4, 16, cq_t, 'cq', 1.0 / 512),
                                                           (wk_, wkk, 2, 20, ckv_t, 'ckv', 1.0 / 256)):
                pss = []
                ss, ssk = acc_p.get()
                for b in range(nblk):
                    pa, pak, _, _ = proj_fm(hT, hkey, KD, wt_, wtk, b * 128, 128, ci)
                    pss.append((pa, pak))
                    sq, sqk = tmp_p.get()
                    ACT(sq[:, :n], pa[:, :n], AF.Square, [pak], [sqk])
                    MM(ss[:, :n], ones_f[:, :], sq[:, :n], b == 0, b == nblk - 1, [sqk, 'ones_f'], [ssk])
                rs, rsk = rs_p.get()
                ACT(rs[:, :n], ss[:, :n], AF.Sqrt, [ssk], [rsk], scale=inv, bias=eps_t[:, 0:1])
                RCP(rs[:, :n], rs[:, :n], [rsk], [rsk])
                for b in range(nblk):
                    pa, pak = pss[b]
                    STT(dst[:, b, c0:c0 + n], pa[:, :n], small[:, nrm_col + b:nrm_col + b + 1], rs[:, :n],
                        ALU.mult, ALU.mult, [pak, rsk, 'small'], [(dkey, ci)])
        wp_, wpk = load_w(mla_downp[:, :, :], KD, 64)
        for ci, (c0, n) in enumerate(CH):
            if ci < 4:
                cs, ck = load_cs(ci)
                pa, pak, pb, pbk = proj_fm(hT, hkey, KD, wk_, wkk, 256, 64, ci, wp_, wpk, 0)
                qk_tile(l, ci, 64, pa, pak, pb, pbk, cs, ck, kmine_rows(l, 1024, 64)[:, c0:c0 + n], None, 'kmine', None)
            else:
                pa, pak, _, _ = proj_fm(hT, hkey, KD, wk_, wkk, 256, 64, ci)
                qk_tile(l, ci, 64, pa, pak, None, None, None, None, None, kc_scr[1024:1088, 0:n], None, 'kc_scr')
        cqkey = lambda ci: ('cq', ci)
        ckvkey = lambda ci: ('ckv', ci)
        for h in range(8):
            wA, wAk = load_w(mla_uq[:, :, h * 192:(h + 1) * 192], 4, 192)
            wB, wBk = load_w(mla_uqp[:, :, h * 64:(h + 1) * 64], 4, 64)
            wkv, wkvk = load_w(mla_ukv[:, :, h * 256:(h + 1) * 256], 2, 256)
            for ci, (c0, n) in enumerate(CH):
                pa, pak, _, _ = proj_fm(cq_t, cqkey, 4, wA, wAk, 0, 128, ci)
                qk_tile(l, ci, 128, pa, pak, None, None, None, None, None, qp_scr[h * 256:h * 256 + 128, c0:c0 + n],
                        None, 'qp_scr')
                if ci < 4:
                    cs, ck = load_cs(ci)
                    pa, pak, pb, pbk = proj_fm(cq_t, cqkey, 4, wA, wAk, 128, 64, ci, wB, wBk, 0)
                    qk_tile(l, ci, 64, pa, pak, pb, pbk, cs, ck, qx_scr[h * 64:h * 64 + 64, c0:c0 + n],
                            qp_scr[h * 256 + 128:h * 256 + 192, c0:c0 + n], 'qx_scr', 'qp_scr')
                else:
                    pa, pak, _, _ = proj_fm(cq_t, cqkey, 4, wA, wAk, 128, 64, ci)
                    qk_tile(l, ci, 64, pa, pak, None, None, None, None, None,
                            qp_scr[h * 256 + 128:h * 256 + 192, c0:c0 + n], None, 'qp_scr')
                pa, pak, _, _ = proj_fm(ckv_t, ckvkey, 2, wkv, wkvk, 0, 128, ci)
                if ci < 4:
                    qk_tile(l, ci, 128, pa, pak, None, None, None, None, None, kmine_rows(l, h * 128, 128)[:, c0:c0 + n],
                            None, 'kmine')
                else:
                    qk_tile(l, ci, 128, pa, pak, None, None, None, None, None, kc_scr[h * 128:(h + 1) * 128, 0:n],
                            None, 'kc_scr')
            for tb in range(T // 128):
                ci = min(tb // 4, 4)
                pv, pvk = ps_p.get()
                for k in range(2):
                    MM(pv[:, :128], ckv_t[:, k, tb * 128:(tb + 1) * 128], wkv[:, k, 128:256], k == 0, k == 1,
                       [wkvk, ('ckv', ci)], [pvk])
                st, sk = stg_p.get()
                CP(st[:, :128], pv[:, :128], [pvk], [sk])
                if tb < 16:
                    store_rows(vmine_rows(l, tb * 128, 128)[:, h * 128:(h + 1) * 128], st[:, :128], sk, 'vmine')
                else:
                    store_rows(vc_scr[(tb - 16) * 128:(tb - 15) * 128, h * 128:(h + 1) * 128], st[:, :128], sk, 'vc_scr')
        exchange(l)
        load_kx(l, krx, 'krx', 1024, 64)
        DMA('sp', krc[:64, :], kc_scr[1024:1088, :], 'krc', ['kc_scr'], ['krc'])
        for h in range(8):
            load_v(l, h * 128, 128)
            DMA('sp', q0[:, :], qp_scr[h * 256:h * 256 + 128, :], 'q0', ['qp_scr'], ['q0'])
            DMA('sp', q1[:64, :], qp_scr[h * 256 + 128:h * 256 + 192, :], 'q1', ['qp_scr'], ['q1'])
            DMA('sp', q2[:64, :], qx_scr[h * 64:(h + 1) * 64, :], 'q2', ['qx_scr'], ['q2'])
            load_kx(l, kx, 'kx', h * 128, 128)
            DMA('sp', kcx[:, :], kc_scr[h * 128:(h + 1) * 128, :], 'kcx', ['kc_scr'], ['kcx'])
            attn_unit(scale, [(kx, q0, 128, 'kx', 'q0'), (krx, q2, 64, 'krx', 'q2')],
                      [(kcx, q0, 128, 'kcx', 'q0'), (krc, q1, 64, 'krc', 'q1')], full_blocks,
                      lambda vb: vt[:, vb, :], 128, None, None, None, plain_final(oh, 'oh', 128))
            wo_apply(l, mla_wo, oh, 'oh', h, 128)

    cq_t = sb("cq_t", [128, 4, T], BF16, at=R2)
    ckv_t = sb("ckv_t", [128, 2, T], BF16, at=R2 + 18432)

    swa_m = sb("swa_m", [128, 6, 512], BF16, at=R2 + 17408)
    halo_s = sb("halo_s", [128, 2])

    def swa_mixer(l):
        scale = 64 ** -0.5
        DMA('sp', swa_m[:, :, :], swa_mask[:, :, :], 'misc', [], ['swa_m'])
        DMA('sp', halo_s[:, :], halo_v[:, :], 'misc', [], ['halo_s'])
        DMA('sp', small[:, 24:40], swa_sink[:, :], 'misc', [], ['small'])
        ACT(small[:, 24:40], small[:, 24:40], AF.Exp, ['small'], ['small'])
        for part, ntile, dst_rot, plain_l, plain_c in (('q', 8, (lambda a, b: qx_scr[a:a + b, :]), qp_scr, qp_scr), ('k', 2, (lambda a, b: kmine_rows(l, a, b)), None, kc_scr)):
            base = 0 if part == 'q' else 1024
            for grp in range((ntile + 3) // 4):
                nt = min(4, ntile - grp * 4)
                wA, wAk = load_w(swa_qkv[:, :, base + grp * 512:base + grp * 512 + nt * 128], KD, nt * 128)
                wB, wBk = load_w(swa_qkp[:, :, base + grp * 512:base + grp * 512 + nt * 128], KD, nt * 128)
                for ci, (c0, n) in enumerate(CH):
                    cs, ck = load_cs(ci) if ci < 4 else (None, None)
                    for t in range(nt):
                        row0 = (grp * 4 + t) * 128
                        if ci < 4:
                            pa, pak, pb, pbk = proj_fm(hT, hkey, KD, wA, wAk, t * 128, 128, ci, wB, wBk, t * 128)
                            qk_tile(l, ci, 128, pa, pak, pb, pbk, cs, ck, dst_rot(row0, 128)[:, c0:c0 + n],
                                    plain_l[row0:row0 + 128, c0:c0 + n] if plain_l is not None else None,
                                    'qx_scr' if part == 'q' else 'kmine', 'qp_scr')
                        else:
                            pa, pak, _, _ = proj_fm(hT, hkey, KD, wA, wAk, t * 128, 128, ci)
                            dstc = qp_scr[row0:row0 + 128, c0:c0 + n] if part == 'q' else kc_scr[row0:row0 + 128, 0:n]
                            qk_tile(l, ci, 128, pa, pak, None, None, None, None, None, dstc, None,
                                    'qp_scr' if part == 'q' else 'kc_scr')
        w, wk = load_w(swa_qkv[:, :, 1280:1536], KD, 256)
        v_proj_tm(l, w, wk, 0, 256, 0)
        exchange(l)

        def mask(pt, ptk, qa, qb, mspec):
            jj, halo = mspec
            if halo is None:
                TT(pt[:, qa:qb], pt[:, qa:qb], swa_m[:, jj, qa:qb], ALU.mult, [ptk, 'swa_m'], [ptk])
            else:
                STT(pt[:, qa:qb], pt[:, qa:qb], halo_s[:, halo:halo + 1], swa_m[:, jj, qa:qb], ALU.mult, ALU.mult,
                    [ptk, 'swa_m', 'halo_s'], [ptk])

        for kvh in range(4):
            DMA('sp', kx[:64, 0:128], kall_rows(l, 0, kvh * 64, 64)[:, TL - 128:TL], 'kx', ['kall'], ['kx'])
            DMA('sp', kx[:64, 128:128 + TL], kmine_rows(l, kvh * 64, 64), 'kx', ['kmine'], ['kx'])
            DMA('sp', kx[:64, 128 + TL:256 + TL], kall_rows(l, 1, kvh * 64, 64)[:, 0:128], 'kx',
                ['kall'], ['kx'])
            DMA('sp', kcx[:64, :], kc_scr[kvh * 64:(kvh + 1) * 64, :], 'kcx', ['kc_scr'], ['kcx'])
            DMA('sp', vt[:, 0:1, :64], vall_rows(l, 0, TL - 128, 128)[:, kvh * 64:(kvh + 1) * 64].rearrange("(b p) c -> p b c", p=128),
                'vt', ['vall'], ['vt'])
            for j in range(2):
                DMA('sp', vt[:, 1 + 8 * j:9 + 8 * j, :64],
                    vmine_rows(l, j * 1024, 1024)[:, kvh * 64:(kvh + 1) * 64].rearrange("(b p) c -> p b c", p=128),
                    'vt', ['vmine'], ['vt'])
            DMA('sp', vt[:, 17:18, :64], vall_rows(l, 1, 0, 128)[:, kvh * 64:(kvh + 1) * 64].rearrange("(b p) c -> p b c", p=128),
                'vt', ['vall'], ['vt'])
            DMA('sp', vt[:, 32:34, :64], vc_scr[:, kvh * 64:(kvh + 1) * 64].rearrange("(b p) c -> p b c", p=128), 'vt',
                ['vc_scr'], ['vt'])
            for g in range(4):
                hq = kvh * 4 + g
                DMA('sp', q2[:64, :], qx_scr[hq * 64:(hq + 1) * 64, :], 'q2', ['qx_scr'], ['q2'])
                DMA('sp', q1[:64, :], qp_scr[hq * 64:(hq + 1) * 64, :], 'q1', ['qp_scr'], ['q1'])

                def xb(ci):
                    out = []
                    order = [None]
                    for jj in range(6):
                        qa = max(0, jj - 2) * 128
                        qb = min(4, jj + 1) * 128
                        blk = 4 * ci + jj
                        halo = None
                        if blk == 0:
                            halo = 0
                        elif blk == 17:
                            halo = 1
                        out.append((blk * 128, blk, (qa, qb), (jj, halo)))
                    return out
                attn_unit(scale, [(kx, q2, 64, 'kx', 'q2')], [(kcx, q1, 64, 'kcx', 'q1')], xb,
                          lambda vb: vt[:, vb, :64], 64, None, None, None,
                          plain_final(oh, 'oh', 64, den_add=small[:64, 24 + hq:25 + hq]), mask=mask)
                wo_apply(l, swa_wo, oh, 'oh', hq, 64)

    na_E = sb("na_E", [128, 22 * 64], BF16, at=R2 + 23552)
    na_rv_s = sb("na_rv_s", [128, 256], BF16, at=R2 + 26368)
    na_rv_f = tmp_lam

    def na_mixer(l):
        scale = 64 ** -0.5
        DMA('sp', na_rv_f[:, :], na_rv[:, :], 'misc', [], ['na_rv_f'])
        CP(na_rv_s[:, :], na_rv_f[:, :], ['na_rv_f'], ['na_rv'])
        for part, dstl, dstc in (('q', (lambda a, b: qp_scr[a:a + b, :]), qp_scr), ('k', (lambda a, b: kmine_rows(l, a, b)), kc_scr)):
            base = 0 if part == 'q' else 1024
            for grp in range(2):
                wA, wAk = load_w(na_qkv[:, :, base + grp * 512:base + (grp + 1) * 512], KD, 512)
                for ci, (c0, n) in enumerate(CH):
                    for t in range(4):
                        row0 = (grp * 4 + t) * 128
                        pa, pak, _, _ = proj_fm(hT, hkey, KD, wA, wAk, t * 128, 128, ci)
                        if part == 'q' or ci < 4:
                            qk_tile(l, ci, 128, pa, pak, None, None, None, None, None,
                                    dstl(row0, 128)[:, c0:c0 + n], None, 'qp_scr' if part == 'q' else 'kmine')
                        else:
                            qk_tile(l, ci, 128, pa, pak, None, None, None, None, None, kc_scr[row0:row0 + 128, 0:n],
                                    None, 'kc_scr')
        for half in range(2):
            w, wk = load_w(na_qkv[:, :, 2048 + half * 512:2048 + (half + 1) * 512], KD, 512)
            v_proj_tm(l, w, wk, 0, 512, half * 512)
        exchange(l)

        for h in range(16):
            for piece in range(3):
                a0 = piece * 512
                a1 = min(1408, a0 + 512)
                t, tk = tmp_p.get()
                DMA('sp', t[:, :a1 - a0], na_rpbE[h, :, a0:a1], 'tmpd%d' % tk[1], [], [tk])
                ACT(na_E[:, a0:a1], t[:, :a1 - a0], AF.Exp, [tk], ['na_E'])
            DMA('sp', kx[:64, 0:256], kall_rows(l, 0, h * 64, 64)[:, TL - 256:TL], 'kx', ['kall'], ['kx'])
            DMA('sp', kx[:64, 256:256 + TL], kmine_rows(l, h * 64, 64), 'kx', ['kmine'], ['kx'])
            DMA('sp', kx[:64, 256 + TL:512 + TL], kall_rows(l, 1, h * 64, 64)[:, 0:256], 'kx',
                ['kall'], ['kx'])
            DMA('sp', kcx[:64, :], kc_scr[h * 64:(h + 1) * 64, :], 'kcx', ['kc_scr'], ['kcx'])
            DMA('sp', vt[:, 0:2, :64], vall_rows(l, 0, TL - 256, 256)[:, h * 64:(h + 1) * 64].rearrange("(b p) c -> p b c", p=128),
                'vt', ['vall'], ['vt'])
            for j in range(2):
                DMA('sp', vt[:, 2 + 8 * j:10 + 8 * j, :64],
                    vmine_rows(l, j * 1024, 1024)[:, h * 64:(h + 1) * 64].rearrange("(b p) c -> p b c", p=128),
                    'vt', ['vmine'], ['vt'])
            DMA('sp', vt[:, 18:20, :64], vall_rows(l, 1, 0, 256)[:, h * 64:(h + 1) * 64].rearrange("(b p) c -> p b c", p=128),
                'vt', ['vall'], ['vt'])
            DMA('sp', vt[:, 32:34, :64], vc_scr[:, h * 64:(h + 1) * 64].rearrange("(b p) c -> p b c", p=128), 'vt',
                ['vc_scr'], ['vt'])
            DMA('sp', q1[:64, :], qp_scr[h * 64:(h + 1) * 64, :], 'q1', ['qp_scr'], ['q1'])

            def xb(ci):
                return [((4 * ci + jb) * 128, 4 * ci + jb, (0, 512), (ci, jb)) for jb in range(8)]

            def mask(pt, ptk, qa, qb, mspec):
                ci, jb = mspec
                j0 = 14 - 2 * jb
                TT(pt[:, 0:512], pt[:, 0:512], na_E[:, j0 * 64:(j0 + 8) * 64], ALU.mult, [ptk, 'na_E'], [ptk])
                i0 = (ci * 8 + jb) * 8
                TT(pt[:, 0:512].rearrange("p (a b) -> p a b", b=64), pt[:, 0:512].rearrange("p (a b) -> p a b", b=64),
                   na_rv_s[:, i0:i0 + 8].unsqueeze(2).broadcast_to([128, 8, 64]), ALU.mult, [ptk, 'na_rv'], [ptk])
            attn_unit(scale, [(kx, q1, 64, 'kx', 'q1')], [(kcx, q1, 64, 'kcx', 'q1')], xb,
                      lambda vb: vt[:, vb, :64], 64, None, None, None, plain_final(oh, 'oh', 64), mask=mask)
            wo_apply(l, na_wo, oh, 'oh', h, 64)

    for l in layers:
        mod_phase(l)
        norm_phase(0)
        ffn_phase(l, 0, 0)
        if do_mixer:
            norm_phase(1)
            if l == 0:
                fence(['dve'], R2_KEYS)
                mla_mixer(l)
                fence(['sp', 'act', 'dve'], R2_KEYS)
            elif l == 1:
                swa_mixer(l)
            elif l == 2:
                na_mixer(l)
            else:
                diff_mixer(l, 0.8 - 0.6 * float(np.exp(-0.3 * l)), last=(l == layers[-1]))
            fence(['act'], ATT_KEYS + HT_KEYS)
        norm_phase(2)
        ffn_phase(l, 1, 2)

    for ci, (c0, n) in enumerate(CH[:4]):
        ss, ssk = ps_p.get()
        for k in range(KD):
            sq, sqk = tmp_p.get()
            ACT(sq[:, :n], xT[:, k, c0:c0 + n], AF.Square, [('xT', ci, k)], [sqk])
            MM(ss[:, :n], ones_f[:, :], sq[:, :n], k == 0, k == KD - 1, [sqk, 'ones_f'], [ssk])
        rstd, rk = rs_p.get()
        ACT(rstd[:, :n], ss[:, :n], AF.Sqrt, [ssk], [rk], scale=1.0 / D, bias=eps_t[:, 0:1])
        RCP(rstd[:, :n], rstd[:, :n], [rk], [rk])
        for k in range(KD):
            t, tk = tmp_p.get()
            STT(t[:, :n], xT[:, k, c0:c0 + n], fng_s[:, k:k + 1], rstd[:, :n], ALU.mult, ALU.mult,
                [('xT', ci, k), rk, 'fng'], [tk])
            DMA('sp', outT[:, k, c0:c0 + n], t[:, :n], 'out', [tk], ['out'])
    if DEBUG and 3 in layers:
        dk1 = nc.dram_tensor("dbg_kall", [1024, TL], BF16, kind="ExternalOutput")
        dk2 = nc.dram_tensor("dbg_kmine", [512, TL], BF16, kind="ExternalOutput")
        DMA('sp', dk1[:, :], kparts[3][0][3][:, :], 'out', ['kall'], ['out'])
        DMA('sp', dk2[:, :], kparts[3][0][2][:, :], 'out', ['kmine'], ['out'])
    P.add('sp', None, ['out'], [])

    P.emit(nc, es)
    es.close()
    return nc


def _kmaj(w):
    K, N = w.shape
    return np.ascontiguousarray(w.reshape(K // 128, 128, N).transpose(1, 0, 2))


def _swap_halves(w, col0, ncols, hd=64):
    idx = np.arange(col0, col0 + ncols).reshape(-1, 2, hd // 2)[:, ::-1, :].reshape(-1)
    return w[:, idx]


def _rope_tables(pos0, n):
    t = np.arange(pos0, pos0 + n)
    row = (t // 64).astype(np.float32)
    col = (t % 64).astype(np.float32)
    nf = 16
    inv = np.exp(-np.log(np.float32(10000.0)) * np.arange(nf, dtype=np.float32) / nf).astype(np.float32)
    ang = np.concatenate([row[:, None] * inv, col[:, None] * inv], axis=-1).astype(np.float32)
    c = np.cos(ang).T.astype(np.float32)
    s = np.sin(ang).T.astype(np.float32)
    cos4 = np.concatenate([c, c, c, c], 0)
    sin4 = np.concatenate([-s, s, -s, s], 0)
    return np.ascontiguousarray(cos4), np.ascontiguousarray(sin4)


def _swa_masks():
    import ml_dtypes
    m = np.zeros((128, 6, 512), np.float32)
    kj = np.arange(128)[:, None]
    q = np.arange(512)[None, :]
    for jj in range(6):
        kpos = (jj - 1) * 128 + kj
        m[:, jj, :] = (np.abs(kpos - q) <= 128)
    return m.astype(ml_dtypes.bfloat16)


def _na_tables(rpb, half):
    H = rpb.shape[0]
    E = np.full((H, 128, 22, 64), NEGFILL, np.float32)
    kc = np.arange(64)[:, None]
    qc = np.arange(64)[None, :]
    cs = np.clip(qc - 8, 0, 48)
    colvalid = (kc >= cs) & (kc < cs + 16)
    coff = np.clip(kc - qc + 15, 0, 30)
    for hf in range(2):
        for jp in range(22):
            e = jp - 3 - hf
            if e < 0 or e > 14:
                continue
            g = rpb[:, 14 - e][:, coff]
            g = np.where(colvalid[None], g, np.float32(NEGFILL))
            E[:, hf * 64:(hf + 1) * 64, jp, :] = g
    rv = np.zeros((128, 4, 8, 8), np.float32)
    for ci in range(4):
        for jb in range(8):
            for hf in range(2):
                krl = 8 * ci - 4 + 2 * jb + hf
                kr = 32 * half + krl
                for qi in range(8):
                    qr = 32 * half + 8 * ci + qi
                    rs = min(max(qr - 4, 0), 56)
                    ok = (rs <= kr < rs + 8) and (0 <= kr < 64)
                    if krl < 0 and half == 0:
                        ok = False
                    if krl >= 32 and half == 1:
                        ok = False
                    rv[hf * 64:(hf + 1) * 64, ci, jb, qi] = 1.0 if ok else 0.0
    return E.reshape(H, 128, 22 * 64), rv.reshape(128, 256)


def prepare_inputs(inp, ncores=8):
    f = lambda a: np.ascontiguousarray(np.asarray(a, dtype=np.float32))
    x, c, ctx, c_ctx = f(inp['x']), f(inp['c']), f(inp['ctx']), f(inp['c_ctx'])
    shared = {}
    shared['modw'] = np.ascontiguousarray(f(inp['mod_w']).reshape(4, 8, 128, 9216).transpose(0, 2, 1, 3))
    shared['modb'] = np.ascontiguousarray(f(inp['mod_b']).reshape(4, 72, 128).transpose(2, 0, 1))
    shared['ng_in'] = np.ascontiguousarray(f(inp['norm_g']).reshape(4, 3, 8, 128).transpose(3, 0, 1, 2))
    shared['fng_in'] = np.ascontiguousarray(f(inp['final_norm_g']).reshape(8, 128).T)
    shared['win'] = np.ascontiguousarray(f(inp['ffn_w_in']).reshape(4, 2, 8, 128, 2 * DFF).transpose(0, 1, 3, 2, 4))
    shared['wout'] = np.ascontiguousarray(f(inp['ffn_w_out']).reshape(4, 2, 22, 128, D).transpose(0, 1, 3, 2, 4))
    wd = f(inp['mla_w_down'])[0]
    shared['mla_down'] = _kmaj(wd)
    shared['mla_downp'] = _kmaj(_swap_halves(wd, 768, 64))
    shared['mla_qn'] = np.ascontiguousarray(f(inp['mla_q_norm'])[0].reshape(4, 128).T)
    shared['mla_kvn'] = np.ascontiguousarray(f(inp['mla_kv_norm'])[0].reshape(2, 128).T)
    uq = f(inp['mla_w_uq'])[0]
    shared['mla_uq'] = _kmaj(uq)
    ropecols = np.concatenate([np.arange(h * 192 + 128, h * 192 + 192) for h in range(8)])
    uqr = uq[:, ropecols]
    shared['mla_uqp'] = _kmaj(_swap_halves(uqr, 0, 512))
    shared['mla_ukv'] = _kmaj(f(inp['mla_w_ukv'])[0])
    shared['mla_wo'] = _kmaj(f(inp['mla_w_o'])[0])
    sq = f(inp['swa_w_qkv'])[0]
    shared['swa_qkv'] = _kmaj(sq)
    shared['swa_qkp'] = _kmaj(_swap_halves(sq, 0, 1280))
    shared['swa_sink'] = np.ascontiguousarray(np.broadcast_to(f(inp['swa_sink'])[0][None, :], (128, 16)))
    swo = f(inp['swa_w_o'])[0]
    t = np.zeros((128, 16, D), np.float32)
    t[:64] = swo.reshape(16, 64, D).transpose(1, 0, 2)
    shared['swa_wo'] = t
    shared['swa_mask'] = _swa_masks()
    shared['na_qkv'] = _kmaj(f(inp['na_w_qkv'])[0])
    nwo = f(inp['na_w_o'])[0]
    t2 = np.zeros((128, 16, D), np.float32)
    t2[:64] = nwo.reshape(16, 64, D).transpose(1, 0, 2)
    shared['na_wo'] = t2
    dq = f(inp['diff_w_qkv'])[0]
    shared['df_qkv'] = _kmaj(dq)
    shared['df_qkp'] = _kmaj(_swap_halves(dq, 0, 2048))
    shared['df_lam'] = np.ascontiguousarray(np.broadcast_to(f(inp['diff_lambda'])[0].reshape(1, 256), (128, 256)))
    shared['df_ng'] = np.ascontiguousarray(f(inp['diff_norm_g'])[0].reshape(128, 1))
    shared['df_wo'] = _kmaj(f(inp['diff_w_o'])[0])
    rpb = f(inp['na_rpb'])[0]
    maps = []
    for core in range(ncores):
        b, h = core // 2, core % 2
        m = dict(shared)
        tok = np.concatenate([x[b, h * TL:(h + 1) * TL], ctx[b]], 0)
        m['xT_in'] = np.ascontiguousarray(tok.T.reshape(8, 128, T).transpose(1, 0, 2))
        cc = np.stack([c[b], c_ctx], -1)
        m['cc_in'] = np.ascontiguousarray(cc.reshape(8, 128, 2).transpose(1, 0, 2))
        m['cos4'], m['sin4'] = _rope_tables(h * TL, TL)
        hv = np.zeros((128, 2), np.float32)
        hv[:, 0] = 1.0 if h == 1 else 0.0
        hv[:, 1] = 1.0 if h == 0 else 0.0
        m['halo_v'] = hv
        E, rv = _na_tables(rpb, h)
        m['na_rpbE'] = E
        m['na_rv'] = rv
        maps.append(m)
    return maps


_NC_CACHE = {}


def kernel(**inputs):
    maps = prepare_inputs(inputs)
    if 'nc' not in _NC_CACHE:
        _NC_CACHE['nc'] = build()
    nc = _NC_CACHE['nc']
    names = set(_INPUT_NAMES)
    in_maps = [{k: v for k, v in m.items() if k in names} for m in maps]
    res = run_bass_kernel_spmd(nc, in_maps, core_ids=list(range(8)))
    out = np.zeros((4, 4096, D), np.float32)
    for core in range(8):
        b, h = core // 2, core % 2
        oT = np.asarray(res.results[core]["outT"]).reshape(128, 8, TL)
        out[b, h * TL:(h + 1) * TL] = oT.transpose(2, 1, 0).reshape(TL, D)
    return out


_INPUT_NAMES = ["xT_in", "cc_in", "modw", "modb", "ng_in", "fng_in", "win", "wout", "cos4", "sin4", "mla_down",
                "mla_downp", "mla_qn", "mla_kvn", "mla_uq", "mla_uqp", "mla_ukv", "mla_wo", "swa_qkv", "swa_qkp",
                "swa_sink", "swa_wo", "swa_mask", "halo_v", "na_qkv", "na_rpbE", "na_rv", "na_wo", "df_qkv", "df_qkp",
                "df_lam", "df_ng", "df_wo"]
```
